# Optimizing a Trainium2 kernel written in Bass

```python
import math
import jax, jax.numpy as jnp
from jax import lax
import numpy as np

D_MODEL = 1024
BATCH = 8
SEQ = 4096
DEPTH = 1

D_MIX = D_MODEL
D_CONV = D_MIX // 2
D_ATTN = D_MIX - D_CONV
HEAD_DIM = 64
N_HEADS = D_ATTN // HEAD_DIM
N_CONV_GROUPS = D_CONV // HEAD_DIM
CONV_WIDTH = 31
BLOCK = 256
TOP_K = 3
Q_CHUNK = 32
ROPE_DIM = HEAD_DIM // 4
ROPE_THETA = 500000.0
EPS = 1e-6
NEG = -1e30
SPLITS = (D_CONV, D_CONV, D_CONV, D_ATTN, D_ATTN, D_ATTN, D_ATTN)
D_IN = sum(SPLITS)

kernel_name = "hybrid_conformer_conv_moba_parallel_heads"


def rmsnorm(x, g):
    xf = x.astype(jnp.float32)
    y = xf * lax.rsqrt(jnp.mean(xf * xf, axis=-1, keepdims=True) + EPS)
    return (y * g.astype(jnp.float32)).astype(x.dtype)


def layernorm(x, g, b):
    xf = x.astype(jnp.float32)
    mu = jnp.mean(xf, axis=-1, keepdims=True)
    var = jnp.mean(jnp.square(xf - mu), axis=-1, keepdims=True)
    y = (xf - mu) * lax.rsqrt(var + EPS)
    return (y * g.astype(jnp.float32) + b.astype(jnp.float32)).astype(x.dtype)


def partial_rope(t, positions):
    half = ROPE_DIM // 2
    inv = ROPE_THETA ** (-(jnp.arange(half, dtype=jnp.float32) * 2.0) / ROPE_DIM)
    ang = positions.astype(jnp.float32)[..., None] * inv
    cos = jnp.cos(ang)[:, :, None, :]
    sin = jnp.sin(ang)[:, :, None, :]
    t1 = t[..., :half].astype(jnp.float32)
    t2 = t[..., half:ROPE_DIM].astype(jnp.float32)
    r1 = (t1 * cos - t2 * sin).astype(t.dtype)
    r2 = (t2 * cos + t1 * sin).astype(t.dtype)
    return jnp.concatenate([r1, r2, t[..., ROPE_DIM:]], axis=-1)


def conformer_conv(val, glu_gate, w_dw, b_dw, g_ln, b_ln, w_pw, b_pw):
    u = val * jax.nn.sigmoid(glu_gate)
    u = lax.conv_general_dilated(
        u, w_dw.reshape(CONV_WIDTH, 1, D_CONV).astype(u.dtype),
        window_strides=(1,), padding=[(CONV_WIDTH - 1, 0)],
        dimension_numbers=("NWC", "WIO", "NWC"),
        feature_group_count=D_CONV) + b_dw
    u = jax.nn.silu(layernorm(u, g_ln, b_ln))
    return u @ w_pw + b_pw


def moba_attention(q, k, v):
    B, S, H, Dh = q.shape
    nb = -(-S // BLOCK)
    pad = nb * BLOCK - S
    q = q.transpose(0, 2, 1, 3)
    kp = jnp.pad(k.transpose(0, 2, 1, 3), ((0, 0), (0, 0), (0, pad), (0, 0)))
    vp = jnp.pad(v.transpose(0, 2, 1, 3), ((0, 0), (0, 0), (0, pad), (0, 0)))
    kb = kp.reshape(B, H, nb, BLOCK, Dh)
    vb = vp.reshape(B, H, nb, BLOCK, Dh)
    scale = Dh ** -0.5
    q_block = jnp.arange(S) // BLOCK
    n_sel = min(TOP_K, nb - 1)
    if n_sel > 0:
        kmean = jnp.mean(kb.astype(jnp.float32), axis=3)
        gate = jnp.einsum("bhsd,bhnd->bhsn", q.astype(jnp.float32), kmean)
        past = jnp.arange(nb)[None, :] < q_block[:, None]
        gate = jnp.where(past[None, None], gate, NEG)
        _, sel = lax.top_k(gate, n_sel)
        valid = jnp.arange(n_sel)[None, :] < q_block[:, None]
    bidx = jnp.arange(B)[:, None, None]
    hidx = jnp.arange(H)[None, :, None]

    def chunk(i):
        start = i * Q_CHUNK
        qc = lax.dynamic_slice_in_dim(q, start, Q_CHUNK, axis=2)
        tpos = start + jnp.arange(Q_CHUNK)
        own = start // BLOCK
        ko = lax.dynamic_slice_in_dim(kp, own * BLOCK, BLOCK, axis=2)
        vo = lax.dynamic_slice_in_dim(vp, own * BLOCK, BLOCK, axis=2)
        kpos = own * BLOCK + jnp.arange(BLOCK)
        lo = jnp.einsum("bhqd,bhkd->bhqk", qc, ko).astype(jnp.float32) * scale
        lo = jnp.where((kpos[None, :] <= tpos[:, None])[None, None], lo, NEG)
        logits = [lo]
        if n_sel > 0:
            selc = lax.dynamic_slice_in_dim(sel, start, Q_CHUNK, axis=2)
            validc = lax.dynamic_slice_in_dim(valid, start, Q_CHUNK, axis=0)
            for r in range(n_sel):
                kg = kb[bidx, hidx, selc[..., r]]
                lr = jnp.einsum("bhqd,bhqkd->bhqk", qc, kg).astype(jnp.float32) * scale
                logits.append(jnp.where(validc[None, None, :, r:r + 1], lr, NEG))
        probs = jax.nn.softmax(jnp.concatenate(logits, axis=-1), axis=-1).astype(vp.dtype)
        out = jnp.einsum("bhqk,bhkd->bhqd", probs[..., :BLOCK], vo)
        if n_sel > 0:
            for r in range(n_sel):
                vg = vb[bidx, hidx, selc[..., r]]
                pr = probs[..., (r + 1) * BLOCK:(r + 2) * BLOCK]
                out = out + jnp.einsum("bhqk,bhqkd->bhqd", pr, vg)
        return out

    outs = lax.map(chunk, jnp.arange(S // Q_CHUNK))
    return outs.transpose(1, 0, 3, 2, 4).reshape(B, S, H * Dh)


def setup_inputs(seed: int = 0) -> dict:
    key = jax.random.key(seed)
    ks = jax.random.split(key, 16)
    f32 = jnp.float32
    x = jax.random.normal(ks[0], (BATCH, SEQ, D_MODEL), f32)
    c = jax.random.normal(ks[1], (BATCH, D_MODEL), f32)
    positions = jnp.broadcast_to(jnp.arange(SEQ, dtype=jnp.int32), (BATCH, SEQ))
    w_ada = jax.random.normal(ks[2], (DEPTH, D_MODEL, 3 * D_MODEL), f32) * (0.5 * D_MODEL ** -0.5)
    b_ada = jax.random.normal(ks[3], (DEPTH, 3 * D_MODEL), f32) * 0.02
    g_norm = 1.0 + 0.02 * jax.random.normal(ks[4], (DEPTH, D_MODEL), f32)
    w_in = jax.random.normal(ks[5], (DEPTH, D_MODEL, D_IN), f32) * D_MODEL ** -0.5
    w_dw = jax.random.normal(ks[6], (DEPTH, CONV_WIDTH, D_CONV), f32) * CONV_WIDTH ** -0.5
    b_dw = jax.random.normal(ks[7], (DEPTH, D_CONV), f32) * 0.02
    g_ln_conv = 1.0 + 0.02 * jax.random.normal(ks[8], (DEPTH, D_CONV), f32)
    b_ln_conv = jax.random.normal(ks[9], (DEPTH, D_CONV), f32) * 0.02
    w_pw = jax.random.normal(ks[10], (DEPTH, D_CONV, D_CONV), f32) * D_CONV ** -0.5
    b_pw = jax.random.normal(ks[11], (DEPTH, D_CONV), f32) * 0.02
    w_out = jax.random.normal(ks[12], (DEPTH, D_MIX, D_MODEL), f32) * D_MIX ** -0.5
    g_final = 1.0 + 0.02 * jax.random.normal(ks[13], (D_MODEL,), f32)
    return {"x": x, "c": c, "positions": positions, "w_ada": w_ada, "b_ada": b_ada,
            "g_norm": g_norm, "w_in": w_in, "w_dw": w_dw, "b_dw": b_dw,
            "g_ln_conv": g_ln_conv, "b_ln_conv": b_ln_conv, "w_pw": w_pw, "b_pw": b_pw,
            "w_out": w_out, "g_final": g_final}


def reference(x, c, positions, w_ada, b_ada, g_norm, w_in, w_dw, b_dw, g_ln_conv,
              b_ln_conv, w_pw, b_pw, w_out, g_final):
    B, S, _ = x.shape
    cut = np.cumsum(SPLITS)[:-1].tolist()
    for l in range(DEPTH):
        mod = jax.nn.silu(c) @ w_ada[l] + b_ada[l]
        shift, scale, gate = jnp.split(mod, 3, axis=-1)
        h = rmsnorm(x, g_norm[l]) * (1.0 + scale[:, None, :]) + shift[:, None, :]
        proj = h @ w_in[l]
        c_val, c_glu, c_gate, q, k, v, a_gate = jnp.split(proj, cut, axis=-1)
        y_conv = conformer_conv(c_val, c_glu, w_dw[l], b_dw[l], g_ln_conv[l],
                                b_ln_conv[l], w_pw[l], b_pw[l]) * jax.nn.silu(c_gate)
        q = partial_rope(q.reshape(B, S, N_HEADS, HEAD_DIM), positions)
        k = partial_rope(k.reshape(B, S, N_HEADS, HEAD_DIM), positions)
        v = v.reshape(B, S, N_HEADS, HEAD_DIM)
        y_attn = moba_attention(q, k, v) * jax.nn.silu(a_gate)
        y = jnp.concatenate([y_conv, y_attn], axis=-1) @ w_out[l]
        x = x + gate[:, None, :] * y
    return rmsnorm(x, g_final)
```

```python
import numpy as np
import ml_dtypes
from contextlib import ExitStack
import concourse.bass as bass
import concourse.mybir as mybir
from concourse.bass_utils import run_bass_kernel_spmd

F32, BF16, I32 = mybir.dt.float32, mybir.dt.bfloat16, mybir.dt.int32
AF = mybir.ActivationFunctionType
ALU = mybir.AluOpType
AX = mybir.AxisListType

D = 1024
EPS = 1e-6
NEGB = -30000.0
ENGS = ["sp", "pe", "act", "dve", "pool"]
LAST_PROG = None


class Prog:
    def __init__(self):
        self.ops = {e: [] for e in ENGS}
        self.lastw = {}
        self.readers = {}
        self.dmacnt = {}
        self.bar = None

    def barrier(self, fn):
        deps = set()
        for e in ENGS:
            if self.ops[e]:
                o = self.ops[e][-1]
                deps.add(o["me"])
            for o in self.ops[e]:
                if o["dma"] is not None:
                    deps.add(o["me"])
        me = self.op("pool", fn)
        self.ops["pool"][-1]["deps"] |= deps
        self.bar = me
        return me

    def op(self, eng, fn, r=(), w=(), dma=None):
        idx = len(self.ops[eng])
        w = list(w) + [x for x in r if isinstance(x, str) and x.startswith("ps") and x not in w]
        deps = set()
        if self.bar is not None:
            deps.add(self.bar)
        for x in r:
            if x in self.lastw:
                deps.add(self.lastw[x])
        for x in w:
            if x in self.lastw:
                deps.add(self.lastw[x])
            for rd in self.readers.get(x, ()):
                deps.add(rd)
        if dma is not None:
            n = self.dmacnt.get(dma, 0) + 1
            self.dmacnt[dma] = n
            me = ("dma", dma, n)
        else:
            me = (eng, idx)
        deps.discard(me)
        self.ops[eng].append(dict(fn=fn, deps=deps, dma=dma, sig=False, me=me))
        for x in r:
            self.readers.setdefault(x, []).append(me)
        for x in w:
            self.lastw[x] = me
            self.readers[x] = []
        return me

    def emit(self, nc, final_waits):
        for e in ENGS:
            for o in self.ops[e]:
                for d in o["deps"]:
                    if d[0] != "dma" and not (d[0] == "pe" and e == "pe"):
                        self.ops[d[0]][d[1]]["sig"] = True
        cnt = {}
        for e in ENGS:
            c = 0
            arr = []
            for o in self.ops[e]:
                if o["sig"]:
                    c += 1
                arr.append(c)
            cnt[e] = arr
        with ExitStack() as st:
            esem = {e: st.enter_context(nc.semaphore("es_" + e)) for e in ENGS}
            dsem = {k: st.enter_context(nc.semaphore("ds_%d" % i))
                    for i, k in enumerate(self.dmacnt)}
            block = st.enter_context(nc.Block())

            def run(h, e):
                waited = {}
                for o in self.ops[e]:
                    need = {}
                    for d in o["deps"]:
                        if d[0] == "dma":
                            k = ("dma", d[1])
                            v = 16 * d[2]
                        else:
                            if d[0] == "pe" and e == "pe":
                                continue
                            k = d[0]
                            v = cnt[d[0]][d[1]]
                        need[k] = max(need.get(k, 0), v)
                    for k, v in need.items():
                        if waited.get(k, 0) >= v:
                            continue
                        waited[k] = v
                        h.wait_ge(dsem[k[1]] if isinstance(k, tuple) else esem[k], v)
                    ins = o["fn"](h)
                    if o["dma"] is not None:
                        ins.then_inc(dsem[o["dma"]], 16)
                    elif o["sig"]:
                        ins.then_inc(esem[e], 1)
                if e == "sp":
                    for k, n in final_waits.items():
                        h.wait_ge(dsem[k], 16 * n)

            @block.sync
            def _(h):
                run(h, "sp")

            @block.tensor
            def _(h):
                run(h, "pe")

            @block.scalar
            def _(h):
                run(h, "act")

            @block.vector
            def _(h):
                run(h, "dve")

            @block.gpsimd
            def _(h):
                run(h, "pool")


class _Cut(Exception):
    pass


def build(S, debug=False, upto=3, cut=0):
    try:
        return _build(S, debug, upto, cut)
    except _Cut as c:
        return c.args[0]


def _build(S, debug=False, upto=3, cut=0):
    NT, NS, NB = S // 512, S // 128, S // 256
    nc = bass.Bass("TRN2", target_bir_lowering=False)
    P = Prog()
    global LAST_PROG
    LAST_PROG = P

    def din(name, shape, dt=F32):
        return nc.dram_tensor(name, list(shape), dt, kind="ExternalInput").ap()

    x_d = din("x", [S, D])
    vecs_d = din("vecs", [56, 128])
    bada_d = din("b_ada", [3 * D])
    pos_d = din("pos", [128, NS], I32)
    wada_d = din("w_ada", [D, 3 * D])
    wc_d = din("w_c", [D, 1536])
    wpair_d = din("w_pair", [4, D, 512])
    wdw_d = din("w_dw", [31, 512])
    wpw_d = din("w_pw", [512, 512])
    wout_d = din("w_out", [D, D])
    gfin_d = din("g_final", [D])
    ident_d = din("ident", [128, 128])
    invf_d = din("invf", [128, 32])
    trib_d = din("trib", [128, 128])
    oneh_d = din("onehot", [16, S])
    out_d = nc.dram_tensor("out", [S, D], F32, kind="ExternalOutput").ap()
    skind = "ExternalOutput" if debug else "Internal"
    ycT_d = nc.dram_tensor("ycT", [512, S], BF16, kind=skind).ap()
    yT_d = nc.dram_tensor("yT", [512, S], BF16, kind=skind).ap()
    if debug:
        hT_dbg = nc.dram_tensor("hT_dbg", [128, 8 * S], BF16, kind="ExternalOutput").ap()

    def sb(name, shape, dt=F32):
        return nc.alloc_sbuf_tensor("s_" + name, list(shape), dt)

    ident = sb("ident", [128, 128])
    identb = sb("identb", [128, 128], BF16)
    onesb = sb("onesb", [128, 128], BF16)
    vT = sb("vT", [128, 56])
    sc = sb("sc", [128, 8])
    modc = sb("modc", [128, 16])
    Acol = sb("Acol", [128, 8])
    shc = sb("shc", [128, 8])
    gate_bc = sb("gate_bc", [128, D])
    gfin_bc = sb("gfin_bc", [128, D])
    cos4 = sb("cos4", [128, NS, 32])
    sin4 = sb("sin4", [128, NS, 32])
    tribb = sb("tribb", [128, 128], BF16)
    hT = sb("hT", [128, 8, S], BF16)
    ssq = sb("ssq", [128, NS])
    rstd = sb("rstd", [128, NS])
    ssq2 = sb("ssq2", [128, NS])
    rstd2 = sb("rstd2", [128, NS])
    ARW = 31600
    AR = sb("arena", [128, ARW])
    ps = [nc.alloc_psum_tensor("ps%d" % i, [128, 512], F32) for i in range(8)]

    class Arena:
        def __init__(self):
            self.off = 0

        def get(self, shape, dt=F32):
            n = int(np.prod(shape[1:]))
            words = (n * (4 if dt in (F32, I32) else 2) + 3) // 4
            words = (words + 7) // 8 * 8
            a = AR[:, self.off:self.off + words]
            self.off += words
            assert self.off <= ARW, (self.off, ARW)
            if dt != F32:
                a = a.bitcast(dt)
            a = a[:, 0:n]
            if len(shape) == 3:
                a = a.rearrange("p (a b) -> p a b", b=shape[2])
            elif len(shape) == 4:
                a = a.rearrange("p (a b c) -> p a b c", b=shape[2], c=shape[3])
            return a

    phase_no = [0]

    def AT():
        return ("arena",)

    def new_phase():
        P.barrier(lambda e: e.memset(dummy[:, 0:1], 0.0))
        return Arena()

    dummy = sb("dummy", [128, 8])
    AR_R = [AT()]

    A0 = Arena()
    vraw = A0.get([128, 128])
    onesf = A0.get([128, 128])
    wa = [A0.get([128, 3 * D]) for _ in range(2)]
    gbias = A0.get([128, D])
    posi = A0.get([128, NS], I32)
    posf = A0.get([128, NS])
    invf = A0.get([128, 32])
    ang = A0.get([128, NS, 32])
    ki = A0.get([128, NS, 32], I32)
    kf = A0.get([128, NS, 32])
    tribf = A0.get([128, 128])
    sgn = A0.get([128, 32])
    NXT = 6
    xt = [A0.get([128, D]) for _ in range(NXT)]
    NXN = 8
    xn = [A0.get([128, D]) for _ in range(NXN)]
    sq_junk = A0.get([128, D])
    sdv = A0.get([128, NS])
    scb2 = A0.get([128, 8, 128])
    wg = [A0.get([128, D]) for _ in range(4)]

    P.op("sp", lambda e: e.dma_start(out=ident[:, :], in_=ident_d[:, :]), w=["ident"], dma="c0")
    P.op("sp", lambda e: e.dma_start(out=vraw[0:56, :], in_=vecs_d[:, :]), r=AR_R, w=["vraw"], dma="c1")
    P.op("sp", lambda e: e.dma_start(out=posi, in_=pos_d[:, :]), r=AR_R, w=["posi"], dma="c2")
    P.op("sp", lambda e: e.dma_start(out=invf, in_=invf_d[:, :]), r=AR_R, w=["invf"], dma="c3")
    P.op("sp", lambda e: e.dma_start(out=tribf, in_=trib_d[:, :]), r=AR_R, w=["tribf"], dma="c4")
    P.op("sp", lambda e: e.dma_start(out=gfin_bc[:, :], in_=gfin_d.partition_broadcast(128)), w=["gfin"], dma="c5")
    P.op("sp", lambda e: e.dma_start(out=gbias, in_=bada_d[2 * D:3 * D].partition_broadcast(128)), r=AR_R, w=["gbias"], dma="c6")

    P.op("dve", lambda e: e.tensor_copy(out=identb[:, :], in_=ident[:, :]), r=["ident"], w=["identb"])
    P.op("dve", lambda e: e.tensor_copy(out=tribb[:, :], in_=tribf), r=["tribf"], w=["tribb"])
    P.op("pool", lambda e: e.memset(onesb[:, :], 1.0 / 512.0), w=["onesb"])
    P.op("pool", lambda e: e.memset(onesf, 1.0), r=AR_R, w=["onesf"])
    P.op("pool", lambda e: e.memset(ssq[:, :], 0.0), w=["ssq"])
    P.op("pool", lambda e: e.memset(ssq2[:, :], 0.0), w=["ssq2"])

    P.op("pe", lambda e: e.transpose(out=ps[0][:, 0:56], in_=vraw[0:56, :], identity=ident[0:56, 0:56]),
         r=["vraw", "ident"], w=["ps0"])
    P.op("dve", lambda e: e.tensor_copy(out=vT[:, :], in_=ps[0][:, 0:56]), r=["ps0"], w=["vT"])
    P.op("act", lambda e: e.activation(out=sc[:, :], in_=vT[:, 32:40], func=AF.Silu), r=["vT"], w=["sc"])
    for kc in range(8):
        P.op("dve", lambda e, kc=kc: e.tensor_scalar(out=scb2[:, kc, :], in0=onesf, scalar1=sc[:, kc:kc + 1],
                                                     scalar2=None, op0=ALU.mult),
             r=["onesf", "sc"], w=[("scb", kc)])

    TWO_PI = 2.0 * np.pi
    HI = 6.28125
    LO = TWO_PI - HI
    P.op("dve", lambda e: e.tensor_copy(out=posf, in_=posi), r=["posi"], w=["posf"])
    for st_ in range(NS):
        P.op("dve", lambda e, s=st_: e.tensor_scalar(out=ang[:, s, :], in0=invf, scalar1=posf[:, s:s + 1],
                                                     scalar2=None, op0=ALU.mult),
             r=["posf", "invf"], w=[("ang", st_)])
    ANG_ALL = [("ang", s) for s in range(NS)]

    def trig(dst, shift, tag):
        P.op("dve", lambda e: e.tensor_scalar(out=kf, in0=ang, scalar1=shift, scalar2=1.0 / TWO_PI,
                                              op0=ALU.add, op1=ALU.mult), r=ANG_ALL + ["cs_prev"], w=["kf"])
        P.op("dve", lambda e: e.tensor_copy(out=ki, in_=kf), r=["kf"], w=["ki"])
        P.op("dve", lambda e: e.tensor_copy(out=kf, in_=ki), r=["ki"], w=["kf"])
        P.op("dve", lambda e: e.scalar_tensor_tensor(out=dst[:, :, :], in0=kf, scalar=-HI, in1=ang,
                                                     op0=ALU.mult, op1=ALU.add), r=["kf"] + ANG_ALL, w=[tag])
        P.op("dve", lambda e: e.scalar_tensor_tensor(out=dst[:, :, :], in0=kf, scalar=-LO, in1=dst[:, :, :],
                                                     op0=ALU.mult, op1=ALU.add), r=["kf", tag], w=[tag])
        if shift != 0.0:
            P.op("dve", lambda e: e.tensor_scalar(out=dst[:, :, :], in0=dst[:, :, :], scalar1=shift, scalar2=None,
                                                  op0=ALU.add), r=[tag], w=[tag])
        P.op("dve", lambda e: e.tensor_scalar(out=dst[:, :, :], in0=dst[:, :, :], scalar1=3.14159, scalar2=-3.14159,
                                              op0=ALU.min, op1=ALU.max), r=[tag], w=[tag])
        P.op("act", lambda e: e.activation(out=dst[:, :, :], in_=dst[:, :, :], func=AF.Sin), r=[tag], w=[tag, "cs_prev"])

    trig(sin4, 0.0, "sin4")
    trig(cos4, np.pi / 2.0, "cos4")

    for kc in range(8):
        b = kc % 2
        P.op("act", lambda e, kc=kc, b=b: e.dma_start(out=wa[b][:, 0:2 * D], in_=wada_d[kc * 128:(kc + 1) * 128, 0:2 * D]),
             r=AR_R, w=[("wa", b)], dma=("wa", b))

        def f_cols(e, kc=kc, b=b):
            ins = None
            for m in range(16):
                ins = e.matmul(ps[0][:, kc * 16 + m:kc * 16 + m + 1], lhsT=wa[b][:, m * 128:(m + 1) * 128],
                               rhs=sc[:, kc:kc + 1], start=True, stop=True)
            return ins
        P.op("pe", f_cols, r=[("wa", b), "sc"], w=["ps0"])

    def gate_dma(kc):
        P.op("act", lambda e: e.dma_start(out=wg[kc % 4], in_=wada_d[kc * 128:(kc + 1) * 128, 2 * D:3 * D]),
             r=AR_R, w=[("wg", kc % 4)], dma=("wg", kc % 4))

    def gate_mm(kc):
        for half in range(2):
            P.op("pe", lambda e, half=half: e.matmul(
                ps[1 + half][:, :], lhsT=scb2[:, kc, :], rhs=wg[kc % 4][:, half * 512:(half + 1) * 512],
                start=(kc == 0), stop=(kc == 7)), r=[("wg", kc % 4), ("scb", kc)], w=["ps%d" % (1 + half)])

    gate_dma(0)
    gate_dma(1)
    P.op("dve", lambda e: e.tensor_reduce(out=modc[:, :], in_=ps[0][:, 0:128].rearrange("p (k m) -> p m k", m=16),
                                          axis=AX.X, op=ALU.add), r=["ps0"], w=["modc"])
    P.op("dve", lambda e: e.tensor_tensor(out=modc[:, :], in0=modc[:, :], in1=vT[:, 0:16], op=ALU.add),
         r=["modc", "vT"], w=["modc"])
    P.op("dve", lambda e: e.tensor_copy(out=shc[:, :], in_=modc[:, 0:8]), r=["modc"], w=["shc"])
    P.op("dve", lambda e: e.scalar_tensor_tensor(out=Acol[:, :], in0=modc[:, 8:16], scalar=1.0, in1=vT[:, 24:32],
                                                 op0=ALU.add, op1=ALU.mult), r=["modc", "vT"], w=["Acol"])

    _pre1 = Arena()
    wc = _pre1.get([128, 8, 1536], BF16)
    DEAD0 = ["vraw", "onesf", ("wa", 0), ("wa", 1)]
    for kc in range(8):
        for part in range(3):
            P.op("pool", lambda e, kc=kc, part=part: e.dma_start(
                out=wc[:, kc, part * 512:(part + 1) * 512], in_=wc_d[kc * 128:(kc + 1) * 128, part * 512:(part + 1) * 512]),
                r=AR_R,
                w=[("wc", kc, part)] + (DEAD0 if (kc, part) == (0, 0) else []), dma=("wc", (kc * 3 + part) % 4))

    def xstage1(tt):
        for s in range(4):
            st_ = tt * 4 + s
            xb_, nb_ = st_ % NXT, st_ % NXN
            P.op("sp", lambda e, st_=st_, xb_=xb_: e.dma_start(out=xt[xb_], in_=x_d[st_ * 128:(st_ + 1) * 128, :]),
                 r=AR_R, w=[("xt", xb_)], dma=("xt", xb_))
            P.op("act", lambda e, st_=st_, xb_=xb_: e.activation(out=sq_junk, in_=xt[xb_], func=AF.Square,
                                                                accum_out=ssq[:, st_:st_ + 1]),
                 r=[("xt", xb_), "ssq"], w=["sqj", ("ssq", st_)])
            P.op("act", lambda e, st_=st_: e.activation(out=sdv[:, st_:st_ + 1], in_=ssq[:, st_:st_ + 1], func=AF.Sqrt,
                                                       bias=EPS, scale=1.0 / D), r=[("ssq", st_)], w=[("sdv", st_)])
            P.op("dve", lambda e, st_=st_: e.reciprocal(out=rstd[:, st_:st_ + 1], in_=sdv[:, st_:st_ + 1]),
                 r=[("sdv", st_)], w=[("rstd", st_)])
            P.op("dve", lambda e, st_=st_, xb_=xb_, nb_=nb_: e.tensor_scalar(
                out=xn[nb_], in0=xt[xb_], scalar1=rstd[:, st_:st_ + 1], scalar2=None, op0=ALU.mult),
                r=[("xt", xb_), ("rstd", st_)], w=[("xn", nb_)])

    def xstage2(tt):
        for kc in range(8):
            pb = 3 + kc % 4

            def f_tr(e, kc=kc, pb=pb):
                ins = None
                for s in range(4):
                    nb_ = (tt * 4 + s) % NXN
                    ins = e.transpose(out=ps[pb][:, s * 128:(s + 1) * 128], in_=xn[nb_][:, kc * 128:(kc + 1) * 128],
                                      identity=ident[:, :])
                return ins
            P.op("pe", f_tr, r=[("xn", (tt * 4 + s) % NXN) for s in range(4)] + ["ident"], w=["ps%d" % pb])
            P.op("act", lambda e, kc=kc, pb=pb: e.activation(
                out=hT[:, kc, tt * 512:(tt + 1) * 512], in_=ps[pb][:, :], func=AF.Identity,
                bias=shc[:, kc:kc + 1], scale=Acol[:, kc:kc + 1]),
                r=["ps%d" % pb, "shc", "Acol"], w=[("hT", tt)])

    xstage1(0)
    GPT = 8 // NT if NT <= 8 else 1
    for tt in range(NT):
        if tt + 1 < NT:
            xstage1(tt + 1)
        xstage2(tt)
        for kc in range(tt * GPT, (tt + 1) * GPT):
            if kc + 2 < 8:
                gate_dma(kc + 2)
            gate_mm(kc)
    for half in range(2):
        P.op("dve", lambda e, half=half: e.tensor_tensor(
            out=gate_bc[:, half * 512:(half + 1) * 512], in0=ps[1 + half][:, :],
            in1=gbias[:, half * 512:(half + 1) * 512], op=ALU.add),
            r=["ps%d" % (1 + half), "gbias"], w=[("gate_bc", half)])
    if debug:
        P.op("sp", lambda e: e.dma_start(out=hT_dbg[:, :], in_=hT[:, :, :].rearrange("p a b -> p (a b)")),
             r=[("hT", t) for t in range(NT)], dma="dbg_hT")

    HT_R = lambda tt: [("hT", tt)]

    def ck(n):
        if cut == n:
            P.barrier(lambda e: e.memset(dummy[:, 0:1], 0.0))
            P.emit(nc, {})
            raise _Cut(nc)
    if upto == 0:
        P.emit(nc, {"dbg_hT": 1})
        return nc

    A1 = new_phase()
    wc_ = A1.get([128, 8, 1536], BF16)
    WC_R = [("wc", kc, part) for kc in range(8) for part in range(3)]
    wpw = A1.get([128, 4, 512], BF16)
    wdraw = A1.get([128, 512])
    wdT = A1.get([128, 4, 32])
    Dg = A1.get([128, 4, 31, 128], BF16)
    U = [A1.get([128, 4, 542], BF16) for _ in range(2)]
    sg = [A1.get([128, 512]) for _ in range(2)]
    gc = [A1.get([128, 4, 512], BF16) for _ in range(2)]
    v32 = A1.get([128, 4, 512])
    vb = A1.get([128, 4, 512], BF16)
    sqb = A1.get([128, 4, 512], BF16)
    m2 = A1.get([128, 512])
    var = A1.get([128, 512])
    rs = A1.get([128, 512])
    mean_sb = A1.get([128, 512])
    tz = [A1.get([128, 512]) for _ in range(2)]
    zb = A1.get([128, 4, 512], BF16)
    yc = [A1.get([128, 4, 512], BF16) for _ in range(2)]

    for kc in range(4):
        P.op("pool", lambda e, kc=kc: e.dma_start(out=wpw[:, kc, :], in_=wpw_d[kc * 128:(kc + 1) * 128, :]),
             r=AR_R, w=[("wpw", kc)], dma=("wpw", kc % 2))
    P.op("sp", lambda e: e.dma_start(out=wdraw[0:31, :], in_=wdw_d[:, :]), r=AR_R, w=["wdraw"], dma="wdraw")
    for c in range(4):
        P.op("pe", lambda e, c=c: e.transpose(out=ps[0][:, c * 32:c * 32 + 31], in_=wdraw[0:31, c * 128:(c + 1) * 128],
                                              identity=ident[0:31, 0:31]), r=["wdraw", "ident"], w=["ps0"])
    P.op("dve", lambda e: e.tensor_copy(out=wdT[:, :, 0:31], in_=ps[0][:, 0:128].rearrange("p (c t) -> p c t", t=32)[:, :, 0:31]),
         r=["ps0"], w=["wdT"])
    for c in range(4):
        for tap in range(31):
            if tap % 2 == 0:
                P.op("dve", lambda e, c=c, tap=tap: e.tensor_scalar(out=Dg[:, c, tap, :], in0=ident[:, :],
                                                                    scalar1=wdT[:, c, tap:tap + 1], scalar2=None,
                                                                    op0=ALU.mult),
                     r=["wdT", "ident"] + AR_R, w=[("Dg", c, 0)])
            else:
                P.op("act", lambda e, c=c, tap=tap: e.activation(out=Dg[:, c, tap, :], in_=ident[:, :], func=AF.Identity,
                                                                 scale=wdT[:, c, tap:tap + 1]),
                     r=["wdT", "ident"] + AR_R, w=[("Dg", c, 1)])
    P.op("pool", lambda e: e.memset(U[0][:, :, 0:30], 0.0), r=AR_R, w=[("Uh", 0)])
    ck(1)

    def cA(tt, cs):
        ub = tt % 2
        tsl = slice(tt * 512, (tt + 1) * 512)
        gcb = gc[tt % 2]
        for c in cs:
            pbase = 0 if c % 2 == 0 else 3
            for gi, goff in enumerate((512, 0, 1024)):
                def f_in(e, c=c, goff=goff, pb=pbase + gi):
                    ins = None
                    for kc in range(8):
                        ins = e.matmul(ps[pb][:, :], lhsT=wc[:, kc, goff + c * 128:goff + (c + 1) * 128],
                                       rhs=hT[:, kc, tsl], start=(kc == 0), stop=(kc == 7))
                    return ins
                P.op("pe", f_in, r=WC_R + HT_R(tt), w=["ps%d" % (pbase + gi)])
            sgb = c % 2
            P.op("act", lambda e, pb=pbase, sgb=sgb: e.activation(out=sg[sgb], in_=ps[pb][:, :], func=AF.Sigmoid),
                 r=["ps%d" % pbase] + AR_R, w=[("sg", sgb)])
            P.op("dve", lambda e, pb=pbase + 1, sgb=sgb, c=c: e.tensor_tensor(
                out=U[ub][:, c, 30:542], in0=ps[pb][:, :], in1=sg[sgb], op=ALU.mult),
                r=["ps%d" % (pbase + 1), ("sg", sgb)], w=[("U", ub, c)])
            P.op("act", lambda e, pb=pbase + 2, c=c: e.activation(out=gcb[:, c, :], in_=ps[pb][:, :], func=AF.Sigmoid),
                 r=["ps%d" % (pbase + 2)], w=[("gc", tt % 2, c)])
            P.op("dve", lambda e, pb=pbase + 2, c=c: e.tensor_tensor(
                out=gcb[:, c, :], in0=ps[pb][:, :], in1=gcb[:, c, :], op=ALU.mult),
                r=["ps%d" % (pbase + 2), ("gc", tt % 2, c)], w=[("gc", tt % 2, c)])
        if tt == NT - 1 and 3 in cs:
            _pre2 = Arena()
            wp_pre = [_pre2.get([128, 8, 512], BF16) for _ in range(2)]
            for p_ in range(2):
                for kc in range(8):
                    P.op("pool", lambda e, p_=p_, kc=kc: e.dma_start(out=wp_pre[p_][:, kc, :],
                                                                  in_=wpair_d[p_, kc * 128:(kc + 1) * 128, :]),
                         r=AR_R,
                         w=[("wp", p_, kc)] + (WC_R if (p_, kc) == (0, 0) else []), dma=("wp", p_, kc % 2))

    def cB(tt):
        ub = tt % 2
        P.op("pool", lambda e: e.tensor_copy(out=U[1 - ub][:, :, 0:30], in_=U[ub][:, :, 512:542]),
             r=[("U", ub, c) for c in range(4)] + AR_R, w=[("Uh", 1 - ub)])

    def cC(tt):
        ub = tt % 2
        for c in range(4):
            pb = 6 + c % 2

            def f_cv(e, c=c, pb=pb):
                ins = None
                for tap in range(31):
                    ins = e.matmul(ps[pb][:, :], lhsT=Dg[:, c, tap, :], rhs=U[ub][:, c, tap:tap + 512],
                                   start=(tap == 0), stop=(tap == 30))
                return ins
            P.op("pe", f_cv, r=[("Dg", c, 0), ("Dg", c, 1), ("U", ub, c), ("Uh", ub)], w=["ps%d" % pb])
            P.op("dve", lambda e, c=c, pb=pb: e.tensor_scalar(out=v32[:, c, :], in0=ps[pb][:, :],
                                                             scalar1=vT[:, 40 + c:41 + c], scalar2=None, op0=ALU.add),
                 r=["ps%d" % pb, "vT"], w=[("v32", c)])
            P.op("act", lambda e, c=c, pb=pb: e.activation(out=vb[:, c, :], in_=ps[pb][:, :], func=AF.Identity,
                                                          bias=vT[:, 40 + c:41 + c], scale=1.0),
                 r=["ps%d" % pb, "vT"], w=[("vb", c)])
            P.op("act", lambda e, c=c, pb=pb: e.activation(out=sqb[:, c, :], in_=ps[pb][:, :], func=AF.Square,
                                                          bias=vT[:, 40 + c:41 + c], scale=1.0),
                 r=["ps%d" % pb, "vT"], w=[("sqb", c)])

    def cD(tt):
        def f_st(e, src, pb):
            ins = None
            for c in range(4):
                ins = e.matmul(ps[pb][:, :], lhsT=onesb[:, :], rhs=src[:, c, :], start=(c == 0), stop=(c == 3))
            return ins
        P.op("pe", lambda e: f_st(e, vb, 0), r=[("vb", c) for c in range(4)] + ["onesb"], w=["ps0"])
        P.op("pe", lambda e: f_st(e, sqb, 1), r=[("sqb", c) for c in range(4)] + ["onesb"], w=["ps1"])
        P.op("act", lambda e: e.activation(out=m2, in_=ps[0][:, :], func=AF.Square), r=["ps0"] + AR_R, w=["m2"])
        P.op("act", lambda e: e.activation(out=mean_sb, in_=ps[0][:, :], func=AF.Identity), r=["ps0"], w=["mean_sb"])
        P.op("dve", lambda e: e.tensor_tensor(out=var, in0=ps[1][:, :], in1=m2, op=ALU.subtract),
             r=["ps1", "m2"], w=["var"])
        P.op("act", lambda e: e.activation(out=var, in_=var, func=AF.Sqrt, bias=EPS, scale=1.0), r=["var"], w=["var"])
        P.op("dve", lambda e: e.reciprocal(out=rs, in_=var), r=["var"], w=["rs"])

    def cE(tt):
        for c in range(4):
            tb = c % 2
            P.op("pool", lambda e, c=c, tb=tb: e.tensor_tensor(out=tz[tb], in0=v32[:, c, :], in1=mean_sb, op=ALU.subtract),
                 r=[("v32", c), "mean_sb"] + AR_R, w=[("tz", tb)])
            P.op("dve", lambda e, tb=tb: e.tensor_tensor(out=tz[tb], in0=tz[tb], in1=rs, op=ALU.mult),
                 r=[("tz", tb), "rs"], w=[("tz", tb)])
            P.op("act", lambda e, c=c, tb=tb: e.activation(out=zb[:, c, :], in_=tz[tb], func=AF.Silu,
                                                          bias=vT[:, 48 + c:49 + c], scale=vT[:, 44 + c:45 + c]),
                 r=[("tz", tb), "vT"], w=[("zb", c)])

    def cF(tt):
        yb = tt % 2
        tsl = slice(tt * 512, (tt + 1) * 512)
        gcb = gc[tt % 2]
        for oc in range(4):
            pb = 3 + oc % 3

            def f_pw(e, oc=oc, pb=pb):
                ins = None
                for k4 in range(4):
                    ins = e.matmul(ps[pb][:, :], lhsT=wpw[:, k4, oc * 128:(oc + 1) * 128], rhs=zb[:, k4, :],
                                   start=(k4 == 0), stop=(k4 == 3))
                return ins
            P.op("pe", f_pw, r=[("wpw", k4) for k4 in range(4)] + [("zb", c) for c in range(4)], w=["ps%d" % pb])
            P.op("dve", lambda e, oc=oc, pb=pb: e.scalar_tensor_tensor(
                out=yc[yb][:, oc, :], in0=ps[pb][:, :], scalar=vT[:, 52 + oc:53 + oc], in1=gcb[:, oc, :],
                op0=ALU.add, op1=ALU.mult), r=["ps%d" % pb, "vT", ("gc", tt % 2, oc)], w=[("yc", yb)])
        P.op("sp", lambda e: e.dma_start(out=ycT_d.rearrange("(c p) t -> p c t", p=128)[:, :, tsl], in_=yc[yb]),
             r=[("yc", yb)] + AR_R, w=[("ycT", tt)], dma=("yc", yb))

    cA(0, [0, 1, 2, 3])
    cB(0)
    cC(0)
    cD(0)
    for tt in range(NT):
        nxt = tt + 1 < NT
        if nxt:
            cA(tt + 1, [0, 1])
        cE(tt)
        if nxt:
            cA(tt + 1, [2, 3])
            cB(tt + 1)
        cF(tt)
        if nxt:
            cC(tt + 1)
            cD(tt + 1)

    if upto == 1:
        P.emit(nc, {("yc", 0): P.dmacnt[("yc", 0)], ("yc", 1): P.dmacnt[("yc", 1)]})
        return nc
    A2 = new_phase()
    wp = [A2.get([128, 8, 512], BF16) for _ in range(2)]
    KAs = [[A2.get([128, S], BF16) for _ in range(2)] for _ in range(2)]
    VAs = [A2.get([128, NS, 2, 128], BF16) for _ in range(2)]
    NQ = 5
    QAs = [[A2.get([128, 512], BF16) for _ in range(2)] for _ in range(NQ)]
    GAs = [A2.get([128, 512], BF16) for _ in range(NQ)]
    GAtmp = A2.get([128, 512])
    QK4 = [A2.get([128, 4, 256])]
    rtmp = [A2.get([128, 4, 4, 8]) for _ in range(4)]
    GSall = A2.get([128, NT, 8, 16])
    Lt = [A2.get([128, 512])]
    kms = [A2.get([128, 16]) for _ in range(2)]
    kmh = [A2.get([128, 16], BF16) for _ in range(2)]
    kml = [A2.get([128, 16], BF16) for _ in range(2)]
    m8 = A2.get([128, 8, 8])
    tmpm = A2.get([128, 8, 16])
    MB = A2.get([128, 4, 128], BF16)
    NPT = 5
    PT = [A2.get([128, 512], BF16) for _ in range(NPT)]
    Rt = [A2.get([128, 512])]
    Tt = A2.get([128, 512])
    Yb = [A2.get([128, 512], BF16) for _ in range(2)]

    for ks in range(2):
        for hh in range(2):
            P.op("dve", lambda e, ks=ks, hh=hh: e.memset(KAs[ks][hh][:, :], 0.0), r=AR_R, w=[("KAinit", ks, hh)])
            P.op("pool", lambda e, ks=ks, hh=hh: e.dma_start(out=KAs[ks][hh][64:80, :], in_=oneh_d[:, :]),
                 r=[("KAinit", ks, hh)], w=[("KAm", ks, hh)], dma=("oneh", ks, hh))
        P.op("dve", lambda e, ks=ks: e.memset(VAs[ks][:, :, :, 64:128], 1.0), r=AR_R, w=[("VAones", ks)])
    for hh in range(2):
        for qi in range(NQ):
            P.op("dve", lambda e, hh=hh, qi=qi: e.memset(QAs[qi][hh], 0.0), r=AR_R,
                 w=[("QAq", qi, hh), ("QAm", qi, hh)])
        P.op("pool", lambda e, hh=hh: e.memset(kmh[hh], 0.0), r=AR_R, w=[("kmh", hh)])
        P.op("pool", lambda e, hh=hh: e.memset(kml[hh], 0.0), r=AR_R, w=[("kml", hh)])
        P.op("pool", lambda e, hh=hh: e.memset(kms[hh], 0.0), r=AR_R, w=[("kms", hh)])
    P.op("pool", lambda e: e.memset(MB, 0.0), r=AR_R, w=["MB"])
    P.op("pool", lambda e: e.memset(GSall, -1e30), r=AR_R, w=["GSall"])
    for tt in range(NT):
        for half_, b_ in ((0, 2 * tt), (1, 2 * tt + 1)):
            P.op("pool", lambda e, tt=tt, half_=half_, b_=b_: e.memset(GSall[:, tt, half_ * 4:half_ * 4 + 4, b_:b_ + 1], 1e30),
                 r=AR_R, w=["GSall"])

    pt_ctr = [0]
    WP_R = [[("wp", wb_, kc) for kc in range(8)] for wb_ in range(2)]

    def prep(p, tt):
        wb = p % 2
        ks = p % 2
        KA = KAs[ks]
        VA = VAs[ks]
        par = (p * NT + tt) % NQ
        QA = QAs
        GA = GAs
        tsl = slice(tt * 512, (tt + 1) * 512)
        qk4 = QK4[0]

        for s in range(4):
            st_ = tt * 4 + s

            tb = 1 - s % 2

            def f_tok(e, st_=st_, tb=tb):
                ins = None
                for kc in range(8):
                    ins = e.matmul(ps[tb][:, 0:384], lhsT=hT[:, kc, st_ * 128:(st_ + 1) * 128],
                                   rhs=wp[wb][:, kc, 0:384], start=(kc == 0), stop=(kc == 7))
                return ins
            P.op("pe", f_tok, r=WP_R[wb] + HT_R(tt), w=["ps%d" % tb])
            P.op("dve", lambda e, s=s, tb=tb: e.tensor_copy(out=qk4[:, s, :], in_=ps[tb][:, 0:256]),
                 r=["ps%d" % tb] + AR_R, w=[("qk4", s)])
            P.op("dve", lambda e, st_=st_, tb=tb: e.tensor_copy(
                out=VA[:, st_, :, 0:64], in_=ps[tb][:, 256:384].rearrange("p (h d) -> p h d", d=64)),
                r=["ps%d" % tb] + AR_R, w=[("VA", ks, st_)])
            yield
        def f_ga(e):
            ins = None
            for kc in range(8):
                ins = e.matmul(ps[1][:, :], lhsT=wp[wb][:, kc, 384:512], rhs=hT[:, kc, tsl],
                               start=(kc == 0), stop=(kc == 7))
            return ins
        P.op("pe", f_ga, r=WP_R[wb] + HT_R(tt), w=["ps1"])
        P.op("act", lambda e: e.activation(out=GAtmp, in_=ps[1][:, :], func=AF.Exp, scale=-1.0),
             r=["ps1"] + AR_R, w=["GAtmp"])
        P.op("act", lambda e: e.activation(out=GAtmp, in_=GAtmp, func=AF.Ln, bias=1.0, scale=1.0),
             r=["GAtmp"], w=["GAtmp"])
        P.op("act", lambda e: e.activation(out=GAtmp, in_=GAtmp, func=AF.Exp, scale=-1.0),
             r=["GAtmp"], w=["GAtmp"])
        P.op("dve", lambda e: e.tensor_tensor(out=GA[par], in0=ps[1][:, :], in1=GAtmp, op=ALU.mult),
             r=["ps1", "GAtmp"], w=[("GA", par)])
        yield
        Q4 = qk4.rearrange("p s (g d) -> p s g d", d=64)
        c4 = cos4[:, tt * 4:(tt + 1) * 4, :].rearrange("p s (g d) -> p s g d", d=8)
        s4 = sin4[:, tt * 4:(tt + 1) * 4, :].rearrange("p s (g d) -> p s g d", d=8)
        QK_ALL = [("qk4", s) for s in range(4)]
        P.op("dve", lambda e: e.tensor_tensor(out=rtmp[0], in0=Q4[:, :, :, 0:8], in1=c4, op=ALU.mult),
             r=QK_ALL + ["cos4"], w=[("rtmp", 0)])
        P.op("dve", lambda e: e.tensor_tensor(out=rtmp[1], in0=Q4[:, :, :, 8:16], in1=s4, op=ALU.mult),
             r=QK_ALL + ["sin4"], w=[("rtmp", 1)])
        P.op("dve", lambda e: e.tensor_tensor(out=rtmp[2], in0=Q4[:, :, :, 8:16], in1=c4, op=ALU.mult),
             r=QK_ALL + ["cos4"], w=[("rtmp", 2)])
        P.op("dve", lambda e: e.tensor_tensor(out=rtmp[3], in0=Q4[:, :, :, 0:8], in1=s4, op=ALU.mult),
             r=QK_ALL + ["sin4"], w=[("rtmp", 3)])
        P.op("dve", lambda e: e.tensor_tensor(out=Q4[:, :, :, 0:8], in0=rtmp[0], in1=rtmp[1], op=ALU.subtract),
             r=[("rtmp", 0), ("rtmp", 1)], w=QK_ALL)
        P.op("dve", lambda e: e.tensor_tensor(out=Q4[:, :, :, 8:16], in0=rtmp[2], in1=rtmp[3], op=ALU.add),
             r=[("rtmp", 2), ("rtmp", 3)], w=QK_ALL)
        yield
        yield
        for s in range(4):
            h2 = s // 2
            qo = (s % 2) * 128
            tbk = s // 2
            P.op("pe", lambda e, s=s, qo=qo, tbk=tbk: e.transpose(out=ps[tbk][:, qo:qo + 128], in_=qk4[:, s, 0:128],
                                                         identity=ident[:, :]),
                 r=[("qk4", s), "ident"], w=["ps%d" % tbk])
            P.op("pe", lambda e, s=s, qo=qo, tbk=tbk: e.transpose(out=ps[tbk][:, 256 + qo:256 + qo + 128], in_=qk4[:, s, 128:256],
                                                         identity=ident[:, :]),
                 r=[("qk4", s), "ident"], w=["ps%d" % tbk])
            if s % 2 == 1:
                hsl = slice(h2 * 256, (h2 + 1) * 256)
                ksl = slice(tt * 512 + h2 * 256, tt * 512 + (h2 + 1) * 256)
                P.op("dve", lambda e, hsl=hsl, tbk=tbk: e.tensor_copy(out=QA[par][0][0:64, hsl], in_=ps[tbk][0:64, 0:256]),
                     r=["ps%d" % tbk], w=[("QAq", par, 0)])
                P.op("dve", lambda e, hsl=hsl, tbk=tbk: e.tensor_copy(out=QA[par][1][0:64, hsl], in_=ps[tbk][64:128, 0:256]),
                     r=["ps%d" % tbk], w=[("QAq", par, 1)])
                P.op("dve", lambda e, ksl=ksl, tbk=tbk: e.tensor_copy(out=KA[0][0:64, ksl], in_=ps[tbk][0:64, 256:512]),
                     r=["ps%d" % tbk, ("KAinit", ks, 0)], w=[("KA", ks, 0, tt)])
                P.op("dve", lambda e, ksl=ksl, tbk=tbk: e.tensor_copy(out=KA[1][0:64, ksl], in_=ps[tbk][64:128, 256:512]),
                     r=["ps%d" % tbk, ("KAinit", ks, 1)], w=[("KA", ks, 1, tt)])
                yield
        bsl = slice(2 * tt, 2 * tt + 2)
        for hh in range(2):
            P.op("dve", lambda e, hh=hh: e.tensor_reduce(
                out=kms[hh][0:64, bsl], in_=KA[hh][0:64, tsl].rearrange("p (n k) -> p n k", k=256),
                axis=AX.X, op=ALU.add), r=[("KA", ks, hh, tt)], w=[("kms", hh)])
            P.op("dve", lambda e, hh=hh: e.tensor_scalar(out=kmh[hh][0:64, bsl], in0=kms[hh][0:64, bsl],
                                                        scalar1=1.0 / 256.0, scalar2=None, op0=ALU.mult),
                 r=[("kms", hh)], w=[("kmh", hh)])
            P.op("dve", lambda e, hh=hh: e.scalar_tensor_tensor(out=kml[hh][0:64, bsl], in0=kms[hh][0:64, bsl],
                                                               scalar=1.0 / 256.0, in1=kmh[hh][0:64, bsl],
                                                               op0=ALU.mult, op1=ALU.subtract),
                 r=[("kms", hh), ("kmh", hh)], w=[("kml", hh)])
        yield
        b0, b1 = 2 * tt, 2 * tt + 1
        if b0 >= 4:
            def f_gate(e):
                ins = None
                for s in range(4):
                    for hh in range(2):
                        j = s * 2 + hh
                        e.matmul(ps[0][:, j * 16:(j + 1) * 16], lhsT=QA[par][hh][0:80, s * 128:(s + 1) * 128],
                                 rhs=kmh[hh][0:80, :], start=True, stop=False)
                        ins = e.matmul(ps[0][:, j * 16:(j + 1) * 16], lhsT=QA[par][hh][0:80, s * 128:(s + 1) * 128],
                                       rhs=kml[hh][0:80, :], start=False, stop=True)
                return ins
            P.op("pe", f_gate, r=[("QAq", par, 0), ("QAq", par, 1), ("kmh", 0), ("kmh", 1), ("kml", 0), ("kml", 1)],
                 w=["ps0"])
            gsb = GSall[:, tt]
            G3 = ps[0][:, 0:128].rearrange("p (j n) -> p j n", n=16)
            P.op("dve", lambda e: e.tensor_copy(out=gsb[:, 0:4, 0:b0], in_=G3[:, 0:4, 0:b0]),
                 r=["ps0", "GSall"], w=[("gsb", tt)])
            P.op("dve", lambda e: e.tensor_copy(out=gsb[:, 4:8, 0:b1], in_=G3[:, 4:8, 0:b1]),
                 r=["ps0", "GSall"], w=[("gsb", tt)])
            for j in range(8):
                P.op("dve", lambda e, j=j: e.max(out=m8[:, j, :], in_=gsb[:, j, :]), r=[("gsb", tt)], w=[("m8", j)])
            P.op("dve", lambda e: e.tensor_tensor(out=tmpm, in0=gsb, in1=m8[:, :, 3:4].to_broadcast([128, 8, 16]),
                                                  op=ALU.is_lt),
                 r=[("gsb", tt)] + [("m8", j) for j in range(8)], w=[("tmpm", j) for j in range(8)])
            MBv = MB[:, :, 64:128].rearrange("p s (h c) -> p s h c", c=32)[:, :, :, 0:16]
            P.op("dve", lambda e: e.tensor_scalar(out=MBv, in0=tmpm.rearrange("p (s h) n -> p s h n", h=2),
                                                  scalar1=NEGB, scalar2=None, op0=ALU.mult),
                 r=[("tmpm", j) for j in range(8)], w=["MB"])
        else:
            P.op("dve", lambda e: e.memset(MB[:, :, 64:128], 0.0), r=AR_R, w=["MB"])
        yield
        yield
        yield
        def f_mk(e):
            ins = None
            for s in range(4):
                ins = e.matmul(ps[1][:, s * 128:(s + 1) * 128], lhsT=MB[:, s, :], rhs=identb[:, :],
                               start=True, stop=True)
            return ins
        P.op("pe", f_mk, r=["MB", "identb"], w=["ps1"])
        P.op("dve", lambda e: e.tensor_copy(out=QA[par][0][64:80, :], in_=ps[1][64:80, :]),
             r=["ps1"], w=[("QAm", par, 0)])
        P.op("dve", lambda e: e.tensor_copy(out=QA[par][1][64:80, :], in_=ps[1][96:112, :]),
             r=["ps1"], w=[("QAm", par, 1)])
        yield

    def attn(p, tt):
        ks = p % 2
        KA = KAs[ks]
        VA = VAs[ks]
        par = (p * NT + tt) % NQ
        QA = QAs
        GA = GAs
        tsl = slice(tt * 512, (tt + 1) * 512)
        nkt = 4 * tt + 4
        SBK = [2, 3, 4, 5]
        LAG = 4
        pend = []

        def emit_pv(info):
            hh, kt, c0, pti = info
            po = 6 + hh
            P.op("pe", lambda e: e.matmul(ps[po][:, c0:512], lhsT=VA[:, kt, hh, :], rhs=PT[pti][:, c0:512],
                                          start=(kt == 0), stop=(kt == nkt - 1)),
                 r=[("VA", ks, kt), ("VAones", ks), ("PT", pti)], w=["ps%d" % po])
            if kt == nkt - 1:
                rb = 0
                rows = slice(hh * 64, hh * 64 + 64)
                P.op("act", lambda e: e.activation(out=Lt[rb][64:128, :], in_=ps[po][64:128, :], func=AF.Ln),
                     r=["ps%d" % po] + AR_R, w=[("Lt", rb)])
                P.op("act", lambda e: e.activation(out=Rt[rb][64:128, :], in_=Lt[rb][64:128, :], func=AF.Exp, scale=-1.0),
                     r=[("Lt", rb)], w=[("Rt", rb)])
                P.op("dve", lambda e: e.tensor_tensor(out=Tt[rows, :], in0=ps[po][0:64, :], in1=Rt[rb][64:128, :],
                                                      op=ALU.mult),
                     r=["ps%d" % po, ("Rt", rb)], w=[("Tt", hh)])
                P.op("pool", lambda e: e.tensor_tensor(out=Yb[tt % 2][rows, :], in0=Tt[rows, :], in1=GA[par][rows, :],
                                                       op=ALU.mult),
                     r=[("Tt", hh), ("GA", par)] + AR_R, w=[("Yb", tt % 2)])

        steps = [(hh, kt) for hh in range(2) for kt in range(nkt)]
        for i, (hh, kt) in enumerate(steps):
            j = kt - 4 * tt
            c0 = max(0, j) * 128
            sbk = SBK[i % len(SBK)]
            pti = pt_ctr[0] % NPT
            pt_ctr[0] += 1

            def f_s(e, hh=hh, kt=kt, j=j, c0=c0, sbk=sbk):
                ins = e.matmul(ps[sbk][:, c0:512], lhsT=KA[hh][0:80, kt * 128:(kt + 1) * 128],
                               rhs=QA[par][hh][0:80, c0:512], start=True, stop=(j < 0))
                if j >= 0:
                    ins = e.matmul(ps[sbk][:, c0:c0 + 128], lhsT=identb[:, :], rhs=tribb[:, :],
                                   start=False, stop=True)
                return ins
            P.op("pe", f_s, r=[("KA", ks, hh, kt // 4), ("KAm", ks, hh), ("QAq", par, hh), ("QAm", par, hh),
                               "identb", "tribb"], w=["ps%d" % sbk])
            P.op("act", lambda e, sbk=sbk, c0=c0, pti=pti: e.activation(
                out=PT[pti][:, c0:512], in_=ps[sbk][:, c0:512], func=AF.Exp, scale=0.125),
                r=["ps%d" % sbk] + AR_R, w=[("PT", pti)])
            pend.append((hh, kt, c0, pti))
            if len(pend) > LAG:
                emit_pv(pend.pop(0))
            yield
        while pend:
            emit_pv(pend.pop(0))
        P.op("sp", lambda e: e.dma_start(out=yT_d[p * 128:(p + 1) * 128, tsl], in_=Yb[tt % 2]),
             r=[("Yb", tt % 2)] + AR_R, w=[("yT", p, tt)], dma=("Yb", tt % 2))

    def interleave(ga, gb, na, nb):
        done_b = 0
        for i, _ in enumerate(ga):
            want = ((i + 1) * nb + na - 1) // na
            while done_b < want:
                try:
                    next(gb)
                except StopIteration:
                    done_b = 10 ** 9
                    break
                done_b += 1
        for _ in gb:
            pass

    NPREP = 15

    def load_wp(p):
        wb = p % 2
        for kc in range(8):
            P.op("pool", lambda e, kc=kc: e.dma_start(out=wp[wb][:, kc, :], in_=wpair_d[p, kc * 128:(kc + 1) * 128, :]),
                 r=AR_R, w=[("wp", wb, kc)], dma=("wp", wb, kc % 2))

    G = 4 * NT
    nsteps = [2 * (4 * tt + 4) for tt in range(NT)]
    SP = sum(nsteps)
    Wn = SP / NT
    Cn = [sum(nsteps[:tt]) for tt in range(NT)]
    lead = max((n + 1) * Wn - Cn[n] for n in range(NT)) + 24
    prep_pos = []
    for g in range(G):
        p, tt = g // NT, g % NT
        start = p * SP + tt * Wn - lead
        for k in range(NPREP + 1):
            prep_pos.append((start + k * Wn / (NPREP + 1), g))
    preps = {}
    pi = [0]
    cur_g = [0]

    def advance_prep(upto_pos):
        while pi[0] < len(prep_pos) and prep_pos[pi[0]][0] <= upto_pos:
            g = prep_pos[pi[0]][1]
            if g - cur_g[0] > NQ - 2:
                break
            if g not in preps:
                if g % NT == 0 and g // NT >= 2:
                    load_wp(g // NT)
                preps[g] = prep(g // NT, g % NT)
            try:
                next(preps[g])
            except StopIteration:
                pass
            pi[0] += 1

    def finish_prep(g):
        while pi[0] < len(prep_pos) and prep_pos[pi[0]][1] <= g:
            gg = prep_pos[pi[0]][1]
            if gg not in preps:
                if gg % NT == 0 and gg // NT >= 2:
                    load_wp(gg // NT)
                preps[gg] = prep(gg // NT, gg % NT)
            try:
                next(preps[gg])
            except StopIteration:
                pass
            pi[0] += 1
        if g in preps:
            for _ in preps[g]:
                pass

    pos = 0
    _pre3 = Arena()
    wo = _pre3.get([128, 8, D], BF16)
    for g in range(G):
        cur_g[0] = g
        finish_prep(g)
        if g == G - 1:
            for kc in range(8):
                for half in range(2):
                    P.op("pool", lambda e, kc=kc, half=half: e.dma_start(
                        out=wo[:, kc, half * 512:(half + 1) * 512],
                        in_=wout_d[kc * 128:(kc + 1) * 128, half * 512:(half + 1) * 512]),
                        r=AR_R,
                        w=[("wo", kc, half)] + (WP_R[0] + WP_R[1] if (kc, half) == (0, 0) else []),
                        dma=("wo", (kc * 2 + half) % 4))
        for _ in attn(g // NT, g % NT):
            pos += 1
            advance_prep(pos)
    for kc in range(8):
        eng = "dve" if kc % 2 == 0 else "pool"
        WO_ALL = [("wo", k_, h_) for k_ in range(8) for h_ in range(2)]
        P.op(eng, lambda e, kc=kc: e.tensor_tensor(out=wo[:, kc, :], in0=wo[:, kc, :], in1=gate_bc[:, :], op=ALU.mult),
             r=[("gate_bc", 0), ("gate_bc", 1)] + AR_R + (WO_ALL if kc == 0 else ["wo_ready"]),
             w=[("wo", kc, 0), ("wo", kc, 1)] + (["wo_ready"] + WO_ALL if kc == 0 else []))
    if upto == 2:
        P.emit(nc, {("Yb", 0): P.dmacnt[("Yb", 0)], ("Yb", 1): P.dmacnt[("Yb", 1)]})
        return nc
    A3 = new_phase()
    wo_ = A3.get([128, 8, D], BF16)
    YT = [A3.get([128, 8, 512], BF16) for _ in range(2)]
    NX3 = 4
    xt3 = [A3.get([128, D]) for _ in range(NX3)]
    NOO = 3
    oo = [A3.get([128, D]) for _ in range(NOO)]
    sq3 = A3.get([128, D])
    sd3 = A3.get([128, NS])
    WO_R = [("wo", kc, h_) for kc in range(8) for h_ in range(2)]
    final_waits = {}

    def load_YT(tt):
        yb = tt % 2
        tsl = slice(tt * 512, (tt + 1) * 512)
        P.op("sp", lambda e: e.dma_start(out=YT[yb][:, 0:4, :], in_=ycT_d.rearrange("(c p) t -> p c t", p=128)[:, :, tsl]),
             r=[("ycT", tt)] + AR_R, w=[("YTa", yb)], dma=("YTa", yb))
        P.op("sp", lambda e: e.dma_start(out=YT[yb][:, 4:8, :], in_=yT_d.rearrange("(c p) t -> p c t", p=128)[:, :, tsl]),
             r=[("yT", p, tt) for p in range(4)] + AR_R, w=[("YTb", yb)], dma=("YTb", yb))

    NRR = 4
    rr = [A3.get([128, D]) for _ in range(NRR)]
    tails = []

    def emit_tail(st_):
        rb_ = st_ % NRR
        ob = st_ % NOO
        P.op("dve", lambda e: e.reciprocal(out=rstd2[:, st_:st_ + 1], in_=sd3[:, st_:st_ + 1]),
             r=[("sd3", st_)], w=[("rstd2", st_)])
        P.op("act", lambda e: e.activation(out=oo[ob], in_=rr[rb_], func=AF.Identity, scale=rstd2[:, st_:st_ + 1]),
             r=[("rr", rb_, 0), ("rr", rb_, 1), ("rstd2", st_)], w=[("oo", ob)])
        P.op("pool", lambda e: e.tensor_tensor(out=oo[ob], in0=oo[ob], in1=gfin_bc[:, :], op=ALU.mult),
             r=[("oo", ob), "gfin"] + AR_R, w=[("oo", ob)])
        P.op("act", lambda e: e.dma_start(out=out_d[st_ * 128:(st_ + 1) * 128, :], in_=oo[ob]),
             r=[("oo", ob)] + AR_R, dma=("oo", ob))

    load_YT(0)
    for tt in range(NT):
        yb = tt % 2
        tsl = slice(tt * 512, (tt + 1) * 512)
        if tt + 1 < NT:
            load_YT(tt + 1)
        for s in range(4):
            st_ = tt * 4 + s
            xb_ = st_ % NX3
            rb_ = st_ % NRR
            P.op("sp", lambda e, st_=st_, xb_=xb_: e.dma_start(out=xt3[xb_], in_=x_d[st_ * 128:(st_ + 1) * 128, :]),
                 r=AR_R, w=[("xt3", xb_)], dma=("xt3", xb_))
            for half in range(2):
                pbk = (st_ % 4) * 2 + half

                def f_o(e, yb=yb, s=s, half=half, pbk=pbk):
                    ins = None
                    for c in range(8):
                        ins = e.matmul(ps[pbk][:, :], lhsT=YT[yb][:, c, s * 128:(s + 1) * 128],
                                       rhs=wo[:, c, half * 512:(half + 1) * 512], start=(c == 0), stop=(c == 7))
                    return ins
                P.op("pe", f_o, r=[("YTa", yb), ("YTb", yb)] + WO_R, w=["ps%d" % pbk])
                hs = slice(half * 512, (half + 1) * 512)
                P.op("dve", lambda e, hs=hs, rb_=rb_, pbk=pbk, xb_=xb_: e.tensor_tensor(
                    out=rr[rb_][:, hs], in0=ps[pbk][:, :], in1=xt3[xb_][:, hs], op=ALU.add),
                    r=["ps%d" % pbk, ("xt3", xb_)] + AR_R, w=[("rr", rb_, half)])
            P.op("act", lambda e, rb_=rb_, st_=st_: e.activation(out=sq3, in_=rr[rb_], func=AF.Square,
                                                                accum_out=ssq2[:, st_:st_ + 1]),
                 r=[("rr", rb_, 0), ("rr", rb_, 1), "ssq2"], w=["sq3", ("ssq2", st_)])
            P.op("act", lambda e, st_=st_: e.activation(out=sd3[:, st_:st_ + 1], in_=ssq2[:, st_:st_ + 1], func=AF.Sqrt,
                                                       bias=EPS, scale=1.0 / D), r=[("ssq2", st_)], w=[("sd3", st_)])
            tails.append(st_)
            if len(tails) > 2:
                emit_tail(tails.pop(0))
    while tails:
        emit_tail(tails.pop(0))
    for ob in range(NOO):
        final_waits[("oo", ob)] = P.dmacnt[("oo", ob)]
    if debug:
        final_waits["dbg_hT"] = 1
    P.emit(nc, final_waits)
    return nc


def host_inputs(S, x, c, positions, w_ada, b_ada, g_norm, w_in, w_dw, b_dw, g_ln_conv, b_ln_conv, w_pw, b_pw,
                w_out, g_final):
    B = x.shape[0]
    NS = S // 128
    f = np.float32
    w_in0 = np.asarray(w_in[0], f)
    w_c = np.ascontiguousarray(w_in0[:, 0:1536])
    w_pair = np.stack([np.concatenate([w_in0[:, 1536 + 128 * p:1536 + 128 * (p + 1)],
                                       w_in0[:, 2048 + 128 * p:2048 + 128 * (p + 1)],
                                       w_in0[:, 2560 + 128 * p:2560 + 128 * (p + 1)],
                                       w_in0[:, 3072 + 128 * p:3072 + 128 * (p + 1)]], axis=1) for p in range(4)])
    ident = np.eye(128, dtype=f)
    half = 8
    inv = (500000.0 ** (-(np.arange(half, dtype=np.float32) * 2.0) / 16.0)).astype(f)
    invf = np.ascontiguousarray(np.broadcast_to(np.tile(inv, 4)[None, :], (128, 32))).astype(f)
    pk = np.arange(128)
    trib = np.where(pk[:, None] <= pk[None, :], 0.0, NEGB).astype(f)
    onehot = (np.arange(16)[:, None] == (np.arange(S)[None, :] // 256)).astype(f)
    maps = []
    for b in range(B):
        vecs = np.concatenate([np.asarray(b_ada[0], f).reshape(24, 128), np.asarray(g_norm[0], f).reshape(8, 128),
                               np.asarray(c[b], f).reshape(8, 128), np.asarray(b_dw[0], f).reshape(4, 128),
                               np.asarray(g_ln_conv[0], f).reshape(4, 128), np.asarray(b_ln_conv[0], f).reshape(4, 128),
                               np.asarray(b_pw[0], f).reshape(4, 128)], axis=0)
        pos = np.ascontiguousarray(np.asarray(positions[b], np.int32).reshape(NS, 128).T)
        maps.append({
            "x": np.ascontiguousarray(np.asarray(x[b], f)), "vecs": np.ascontiguousarray(vecs),
            "b_ada": np.ascontiguousarray(np.asarray(b_ada[0], f)), "pos": pos,
            "w_ada": np.ascontiguousarray(np.asarray(w_ada[0], f)), "w_c": w_c, "w_pair": np.ascontiguousarray(w_pair),
            "w_dw": np.ascontiguousarray(np.asarray(w_dw[0], f)), "w_pw": np.ascontiguousarray(np.asarray(w_pw[0], f)),
            "w_out": np.ascontiguousarray(np.asarray(w_out[0], f)), "g_final": np.ascontiguousarray(np.asarray(g_final, f)),
            "ident": ident, "invf": invf, "trib": trib, "onehot": onehot,
        })
    return maps


_NC_CACHE = {}


def kernel(x, c, positions, w_ada, b_ada, g_norm, w_in, w_dw, b_dw, g_ln_conv, b_ln_conv, w_pw, b_pw, w_out, g_final):
    x = np.asarray(x)
    B, S, _ = x.shape
    maps = host_inputs(S, x, np.asarray(c), np.asarray(positions), np.asarray(w_ada), np.asarray(b_ada),
                       np.asarray(g_norm), np.asarray(w_in), np.asarray(w_dw), np.asarray(b_dw), np.asarray(g_ln_conv),
                       np.asarray(b_ln_conv), np.asarray(w_pw), np.asarray(b_pw), np.asarray(w_out), np.asarray(g_final))
    if S not in _NC_CACHE:
        _NC_CACHE[S] = build(S)
    nc = _NC_CACHE[S]
    res = run_bass_kernel_spmd(nc, maps, core_ids=list(range(B)))
    return np.stack([np.asarray(r["out"], np.float32) for r in res.results], axis=0)
```

```python
import numpy as np
import ml_dtypes
from contextlib import ExitStack
import concourse.bass as bass
import concourse.mybir as mybir
from concourse.bass_utils import run_bass_kernel_spmd

F32, BF16, I32 = mybir.dt.float32, mybir.dt.bfloat16, mybir.dt.int32
AF = mybir.ActivationFunctionType
ALU = mybir.AluOpType
AX = mybir.AxisListType

D = 1024
EPS = 1e-6
NEGB = -30000.0
ENGS = ["sp", "pe", "act", "dve", "pool"]
LAST_PROG = None


class Prog:
    def __init__(self):
        self.ops = {e: [] for e in ENGS}
        self.lastw = {}
        self.readers = {}
        self.dmacnt = {}
        self.bar = None

    def barrier(self, fn):
        deps = set()
        for e in ENGS:
            if self.ops[e]:
                o = self.ops[e][-1]
                deps.add(o["me"])
            for o in self.ops[e]:
                if o["dma"] is not None:
                    deps.add(o["me"])
        me = self.op("pool", fn)
        self.ops["pool"][-1]["deps"] |= deps
        self.bar = me
        return me

    def op(self, eng, fn, r=(), w=(), dma=None):
        idx = len(self.ops[eng])
        w = list(w) + [x for x in r if isinstance(x, str) and x.startswith("ps") and x not in w]
        deps = set()
        if self.bar is not None:
            deps.add(self.bar)
        for x in r:
            if x in self.lastw:
                deps.add(self.lastw[x])
        for x in w:
            if x in self.lastw:
                deps.add(self.lastw[x])
            for rd in self.readers.get(x, ()):
                deps.add(rd)
        if dma is not None:
            n = self.dmacnt.get(dma, 0) + 1
            self.dmacnt[dma] = n
            me = ("dma", dma, n)
        else:
            me = (eng, idx)
        deps.discard(me)
        self.ops[eng].append(dict(fn=fn, deps=deps, dma=dma, sig=False, me=me))
        for x in r:
            self.readers.setdefault(x, []).append(me)
        for x in w:
            self.lastw[x] = me
            self.readers[x] = []
        return me

    def emit(self, nc, final_waits):
        for e in ENGS:
            for o in self.ops[e]:
                for d in o["deps"]:
                    if d[0] != "dma" and not (d[0] == "pe" and e == "pe"):
                        self.ops[d[0]][d[1]]["sig"] = True
        cnt = {}
        for e in ENGS:
            c = 0
            arr = []
            for o in self.ops[e]:
                if o["sig"]:
                    c += 1
                arr.append(c)
            cnt[e] = arr
        with ExitStack() as st:
            esem = {e: st.enter_context(nc.semaphore("es_" + e)) for e in ENGS}
            dsem = {k: st.enter_context(nc.semaphore("ds_%d" % i))
                    for i, k in enumerate(self.dmacnt)}
            block = st.enter_context(nc.Block())

            def run(h, e):
                waited = {}
                for o in self.ops[e]:
                    need = {}
                    for d in o["deps"]:
                        if d[0] == "dma":
                            k = ("dma", d[1])
                            v = 16 * d[2]
                        else:
                            if d[0] == "pe" and e == "pe":
                                continue
                            k = d[0]
                            v = cnt[d[0]][d[1]]
                        need[k] = max(need.get(k, 0), v)
                    for k, v in need.items():
                        if waited.get(k, 0) >= v:
                            continue
                        waited[k] = v
                        h.wait_ge(dsem[k[1]] if isinstance(k, tuple) else esem[k], v)
                    ins = o["fn"](h)
                    if o["dma"] is not None:
                        ins.then_inc(dsem[o["dma"]], 16)
                    elif o["sig"]:
                        ins.then_inc(esem[e], 1)
                if e == "sp":
                    for k, n in final_waits.items():
                        h.wait_ge(dsem[k], 16 * n)

            @block.sync
            def _(h):
                run(h, "sp")

            @block.tensor
            def _(h):
                run(h, "pe")

            @block.scalar
            def _(h):
                run(h, "act")

            @block.vector
            def _(h):
                run(h, "dve")

            @block.gpsimd
            def _(h):
                run(h, "pool")


class _Cut(Exception):
    pass


def build(S, debug=False, upto=3, cut=0):
    try:
        return _build(S, debug, upto, cut)
    except _Cut as c:
        return c.args[0]


def _build(S, debug=False, upto=3, cut=0):
    NT, NS, NB = S // 512, S // 128, S // 256
    nc = bass.Bass("TRN2", target_bir_lowering=False)
    P = Prog()
    global LAST_PROG
    LAST_PROG = P

    def din(name, shape, dt=F32):
        return nc.dram_tensor(name, list(shape), dt, kind="ExternalInput").ap()

    x_d = din("x", [S, D])
    vecs_d = din("vecs", [56, 128])
    bada_d = din("b_ada", [3 * D])
    pos_d = din("pos", [128, NS], I32)
    wada_d = din("w_ada", [D, 3 * D])
    wc_d = din("w_c", [D, 1536])
    wpair_d = din("w_pair", [4, D, 512])
    wdw_d = din("w_dw", [31, 512])
    wpw_d = din("w_pw", [512, 512])
    wout_d = din("w_out", [D, D])
    gfin_d = din("g_final", [D])
    ident_d = din("ident", [128, 128])
    invf_d = din("invf", [128, 32])
    trib_d = din("trib", [128, 128])
    oneh_d = din("onehot", [16, S])
    out_d = nc.dram_tensor("out", [S, D], F32, kind="ExternalOutput").ap()
    skind = "ExternalOutput" if debug else "Internal"
    ycT_d = nc.dram_tensor("ycT", [512, S], BF16, kind=skind).ap()
    yT_d = nc.dram_tensor("yT", [512, S], BF16, kind=skind).ap()
    if debug:
        hT_dbg = nc.dram_tensor("hT_dbg", [128, 8 * S], BF16, kind="ExternalOutput").ap()

    def sb(name, shape, dt=F32):
        return nc.alloc_sbuf_tensor("s_" + name, list(shape), dt)

    ident = sb("ident", [128, 128])
    identb = sb("identb", [128, 128], BF16)
    onesb = sb("onesb", [128, 128], BF16)
    vT = sb("vT", [128, 56])
    sc = sb("sc", [128, 8])
    modc = sb("modc", [128, 16])
    Acol = sb("Acol", [128, 8])
    shc = sb("shc", [128, 8])
    gate_bc = sb("gate_bc", [128, D])
    gfin_bc = sb("gfin_bc", [128, D])
    cos4 = sb("cos4", [128, NS, 32])
    sin4 = sb("sin4", [128, NS, 32])
    tribb = sb("tribb", [128, 128], BF16)
    hT = sb("hT", [128, 8, S], BF16)
    ssq = sb("ssq", [128, NS])
    rstd = sb("rstd", [128, NS])
    ssq2 = sb("ssq2", [128, NS])
    rstd2 = sb("rstd2", [128, NS])
    ARW = 31600
    AR = sb("arena", [128, ARW])
    ps = [nc.alloc_psum_tensor("ps%d" % i, [128, 512], F32) for i in range(8)]

    class Arena:
        def __init__(self):
            self.off = 0

        def get(self, shape, dt=F32):
            n = int(np.prod(shape[1:]))
            words = (n * (4 if dt in (F32, I32) else 2) + 3) // 4
            words = (words + 7) // 8 * 8
            a = AR[:, self.off:self.off + words]
            self.off += words
            assert self.off <= ARW, (self.off, ARW)
            if dt != F32:
                a = a.bitcast(dt)
            a = a[:, 0:n]
            if len(shape) == 3:
                a = a.rearrange("p (a b) -> p a b", b=shape[2])
            elif len(shape) == 4:
                a = a.rearrange("p (a b c) -> p a b c", b=shape[2], c=shape[3])
            return a

    phase_no = [0]

    def AT():
        return ("arena",)

    def new_phase():
        P.barrier(lambda e: e.memset(dummy[:, 0:1], 0.0))
        return Arena()

    dummy = sb("dummy", [128, 8])
    AR_R = [AT()]

    A0 = Arena()
    vraw = A0.get([128, 128])
    onesf = A0.get([128, 128])
    wa = [A0.get([128, 3 * D]) for _ in range(2)]
    gbias = A0.get([128, D])
    posi = A0.get([128, NS], I32)
    posf = A0.get([128, NS])
    invf = A0.get([128, 32])
    ang = A0.get([128, NS, 32])
    ki = A0.get([128, NS, 32], I32)
    kf = A0.get([128, NS, 32])
    tribf = A0.get([128, 128])
    sgn = A0.get([128, 32])
    NXT = 6
    xt = [A0.get([128, D]) for _ in range(NXT)]
    NXN = 8
    xn = [A0.get([128, D]) for _ in range(NXN)]
    sq_junk = A0.get([128, D])
    sdv = A0.get([128, NS])
    scb2 = A0.get([128, 8, 128])
    wg = [A0.get([128, D]) for _ in range(4)]

    P.op("sp", lambda e: e.dma_start(out=ident[:, :], in_=ident_d[:, :]), w=["ident"], dma="c0")
    P.op("sp", lambda e: e.dma_start(out=vraw[0:56, :], in_=vecs_d[:, :]), r=AR_R, w=["vraw"], dma="c1")
    P.op("sp", lambda e: e.dma_start(out=posi, in_=pos_d[:, :]), r=AR_R, w=["posi"], dma="c2")
    P.op("sp", lambda e: e.dma_start(out=invf, in_=invf_d[:, :]), r=AR_R, w=["invf"], dma="c3")
    P.op("sp", lambda e: e.dma_start(out=tribf, in_=trib_d[:, :]), r=AR_R, w=["tribf"], dma="c4")
    P.op("sp", lambda e: e.dma_start(out=gfin_bc[:, :], in_=gfin_d.partition_broadcast(128)), w=["gfin"], dma="c5")
    P.op("sp", lambda e: e.dma_start(out=gbias, in_=bada_d[2 * D:3 * D].partition_broadcast(128)), r=AR_R, w=["gbias"], dma="c6")

    P.op("dve", lambda e: e.tensor_copy(out=identb[:, :], in_=ident[:, :]), r=["ident"], w=["identb"])
    P.op("dve", lambda e: e.tensor_copy(out=tribb[:, :], in_=tribf), r=["tribf"], w=["tribb"])
    P.op("pool", lambda e: e.memset(onesb[:, :], 1.0 / 512.0), w=["onesb"])
    P.op("pool", lambda e: e.memset(onesf, 1.0), r=AR_R, w=["onesf"])
    P.op("pool", lambda e: e.memset(ssq[:, :], 0.0), w=["ssq"])
    P.op("pool", lambda e: e.memset(ssq2[:, :], 0.0), w=["ssq2"])

    P.op("pe", lambda e: e.transpose(out=ps[0][:, 0:56], in_=vraw[0:56, :], identity=ident[0:56, 0:56]),
         r=["vraw", "ident"], w=["ps0"])
    P.op("dve", lambda e: e.tensor_copy(out=vT[:, :], in_=ps[0][:, 0:56]), r=["ps0"], w=["vT"])
    P.op("act", lambda e: e.activation(out=sc[:, :], in_=vT[:, 32:40], func=AF.Silu), r=["vT"], w=["sc"])
    for kc in range(8):
        P.op("dve", lambda e, kc=kc: e.tensor_scalar(out=scb2[:, kc, :], in0=onesf, scalar1=sc[:, kc:kc + 1],
                                                     scalar2=None, op0=ALU.mult),
             r=["onesf", "sc"], w=[("scb", kc)])

    TWO_PI = 2.0 * np.pi
    HI = 6.28125
    LO = TWO_PI - HI
    P.op("dve", lambda e: e.tensor_copy(out=posf, in_=posi), r=["posi"], w=["posf"])
    for st_ in range(NS):
        P.op("dve", lambda e, s=st_: e.tensor_scalar(out=ang[:, s, :], in0=invf, scalar1=posf[:, s:s + 1],
                                                     scalar2=None, op0=ALU.mult),
             r=["posf", "invf"], w=[("ang", st_)])
    ANG_ALL = [("ang", s) for s in range(NS)]

    def trig(dst, shift, tag):
        P.op("dve", lambda e: e.tensor_scalar(out=kf, in0=ang, scalar1=shift, scalar2=1.0 / TWO_PI,
                                              op0=ALU.add, op1=ALU.mult), r=ANG_ALL + ["cs_prev"], w=["kf"])
        P.op("dve", lambda e: e.tensor_copy(out=ki, in_=kf), r=["kf"], w=["ki"])
        P.op("dve", lambda e: e.tensor_copy(out=kf, in_=ki), r=["ki"], w=["kf"])
        P.op("dve", lambda e: e.scalar_tensor_tensor(out=dst[:, :, :], in0=kf, scalar=-HI, in1=ang,
                                                     op0=ALU.mult, op1=ALU.add), r=["kf"] + ANG_ALL, w=[tag])
        P.op("dve", lambda e: e.scalar_tensor_tensor(out=dst[:, :, :], in0=kf, scalar=-LO, in1=dst[:, :, :],
                                                     op0=ALU.mult, op1=ALU.add), r=["kf", tag], w=[tag])
        if shift != 0.0:
            P.op("dve", lambda e: e.tensor_scalar(out=dst[:, :, :], in0=dst[:, :, :], scalar1=shift, scalar2=None,
                                                  op0=ALU.add), r=[tag], w=[tag])
        P.op("dve", lambda e: e.tensor_scalar(out=dst[:, :, :], in0=dst[:, :, :], scalar1=3.14159, scalar2=-3.14159,
                                              op0=ALU.min, op1=ALU.max), r=[tag], w=[tag])
        P.op("act", lambda e: e.activation(out=dst[:, :, :], in_=dst[:, :, :], func=AF.Sin), r=[tag], w=[tag, "cs_prev"])

    trig(sin4, 0.0, "sin4")
    trig(cos4, np.pi / 2.0, "cos4")

    for kc in range(8):
        b = kc % 2
        P.op("act", lambda e, kc=kc, b=b: e.dma_start(out=wa[b][:, 0:2 * D], in_=wada_d[kc * 128:(kc + 1) * 128, 0:2 * D]),
             r=AR_R, w=[("wa", b)], dma=("wa", b))

        def f_cols(e, kc=kc, b=b):
            ins = None
            for m in range(16):
                ins = e.matmul(ps[0][:, kc * 16 + m:kc * 16 + m + 1], lhsT=wa[b][:, m * 128:(m + 1) * 128],
                               rhs=sc[:, kc:kc + 1], start=True, stop=True)
            return ins
        P.op("pe", f_cols, r=[("wa", b), "sc"], w=["ps0"])

    def gate_dma(kc):
        P.op("act", lambda e: e.dma_start(out=wg[kc % 4], in_=wada_d[kc * 128:(kc + 1) * 128, 2 * D:3 * D]),
             r=AR_R, w=[("wg", kc % 4)], dma=("wg", kc % 4))

    def gate_mm(kc):
        for half in range(2):
            P.op("pe", lambda e, half=half: e.matmul(
                ps[1 + half][:, :], lhsT=scb2[:, kc, :], rhs=wg[kc % 4][:, half * 512:(half + 1) * 512],
                start=(kc == 0), stop=(kc == 7)), r=[("wg", kc % 4), ("scb", kc)], w=["ps%d" % (1 + half)])

    gate_dma(0)
    gate_dma(1)
    P.op("dve", lambda e: e.tensor_reduce(out=modc[:, :], in_=ps[0][:, 0:128].rearrange("p (k m) -> p m k", m=16),
                                          axis=AX.X, op=ALU.add), r=["ps0"], w=["modc"])
    P.op("dve", lambda e: e.tensor_tensor(out=modc[:, :], in0=modc[:, :], in1=vT[:, 0:16], op=ALU.add),
         r=["modc", "vT"], w=["modc"])
    P.op("dve", lambda e: e.tensor_copy(out=shc[:, :], in_=modc[:, 0:8]), r=["modc"], w=["shc"])
    P.op("dve", lambda e: e.scalar_tensor_tensor(out=Acol[:, :], in0=modc[:, 8:16], scalar=1.0, in1=vT[:, 24:32],
                                                 op0=ALU.add, op1=ALU.mult), r=["modc", "vT"], w=["Acol"])

    _pre1 = Arena()
    wc = _pre1.get([128, 8, 1536], BF16)
    DEAD0 = ["vraw", "onesf", ("wa", 0), ("wa", 1)]
    for kc in range(8):
        for part in range(3):
            P.op("pool", lambda e, kc=kc, part=part: e.dma_start(
                out=wc[:, kc, part * 512:(part + 1) * 512], in_=wc_d[kc * 128:(kc + 1) * 128, part * 512:(part + 1) * 512]),
                r=AR_R,
                w=[("wc", kc, part)] + (DEAD0 if (kc, part) == (0, 0) else []), dma=("wc", (kc * 3 + part) % 4))

    def xstage1(tt):
        for s in range(4):
            st_ = tt * 4 + s
            xb_, nb_ = st_ % NXT, st_ % NXN
            P.op("sp", lambda e, st_=st_, xb_=xb_: e.dma_start(out=xt[xb_], in_=x_d[st_ * 128:(st_ + 1) * 128, :]),
                 r=AR_R, w=[("xt", xb_)], dma=("xt", xb_))
            P.op("act", lambda e, st_=st_, xb_=xb_: e.activation(out=sq_junk, in_=xt[xb_], func=AF.Square,
                                                                accum_out=ssq[:, st_:st_ + 1]),
                 r=[("xt", xb_), "ssq"], w=["sqj", ("ssq", st_)])
            P.op("act", lambda e, st_=st_: e.activation(out=sdv[:, st_:st_ + 1], in_=ssq[:, st_:st_ + 1], func=AF.Sqrt,
                                                       bias=EPS, scale=1.0 / D), r=[("ssq", st_)], w=[("sdv", st_)])
            P.op("dve", lambda e, st_=st_: e.reciprocal(out=rstd[:, st_:st_ + 1], in_=sdv[:, st_:st_ + 1]),
                 r=[("sdv", st_)], w=[("rstd", st_)])
            P.op("dve", lambda e, st_=st_, xb_=xb_, nb_=nb_: e.tensor_scalar(
                out=xn[nb_], in0=xt[xb_], scalar1=rstd[:, st_:st_ + 1], scalar2=None, op0=ALU.mult),
                r=[("xt", xb_), ("rstd", st_)], w=[("xn", nb_)])

    def xstage2(tt):
        for kc in range(8):
            pb = 3 + kc % 4

            def f_tr(e, kc=kc, pb=pb):
                ins = None
                for s in range(4):
                    nb_ = (tt * 4 + s) % NXN
                    ins = e.transpose(out=ps[pb][:, s * 128:(s + 1) * 128], in_=xn[nb_][:, kc * 128:(kc + 1) * 128],
                                      identity=ident[:, :])
                return ins
            P.op("pe", f_tr, r=[("xn", (tt * 4 + s) % NXN) for s in range(4)] + ["ident"], w=["ps%d" % pb])
            P.op("act", lambda e, kc=kc, pb=pb: e.activation(
                out=hT[:, kc, tt * 512:(tt + 1) * 512], in_=ps[pb][:, :], func=AF.Identity,
                bias=shc[:, kc:kc + 1], scale=Acol[:, kc:kc + 1]),
                r=["ps%d" % pb, "shc", "Acol"], w=[("hT", tt)])

    xstage1(0)
    GPT = 8 // NT if NT <= 8 else 1
    for tt in range(NT):
        if tt + 1 < NT:
            xstage1(tt + 1)
        xstage2(tt)
        for kc in range(tt * GPT, (tt + 1) * GPT):
            if kc + 2 < 8:
                gate_dma(kc + 2)
            gate_mm(kc)
    for half in range(2):
        P.op("dve", lambda e, half=half: e.tensor_tensor(
            out=gate_bc[:, half * 512:(half + 1) * 512], in0=ps[1 + half][:, :],
            in1=gbias[:, half * 512:(half + 1) * 512], op=ALU.add),
            r=["ps%d" % (1 + half), "gbias"], w=[("gate_bc", half)])
    if debug:
        P.op("sp", lambda e: e.dma_start(out=hT_dbg[:, :], in_=hT[:, :, :].rearrange("p a b -> p (a b)")),
             r=[("hT", t) for t in range(NT)], dma="dbg_hT")

    HT_R = lambda tt: [("hT", tt)]

    def ck(n):
        if cut == n:
            P.barrier(lambda e: e.memset(dummy[:, 0:1], 0.0))
            P.emit(nc, {})
            raise _Cut(nc)
    if upto == 0:
        P.emit(nc, {"dbg_hT": 1})
        return nc

    A1 = new_phase()
    wc_ = A1.get([128, 8, 1536], BF16)
    WC_R = [("wc", kc, part) for kc in range(8) for part in range(3)]
    wpw = A1.get([128, 4, 512], BF16)
    wdraw = A1.get([128, 512])
    wdT = A1.get([128, 4, 32])
    Dg = A1.get([128, 4, 31, 128], BF16)
    U = [A1.get([128, 4, 542], BF16) for _ in range(2)]
    sg = [A1.get([128, 512]) for _ in range(2)]
    gc = [A1.get([128, 4, 512], BF16) for _ in range(2)]
    v32 = A1.get([128, 4, 512])
    vb = A1.get([128, 4, 512], BF16)
    sqb = A1.get([128, 4, 512], BF16)
    m2 = A1.get([128, 512])
    var = A1.get([128, 512])
    rs = A1.get([128, 512])
    mean_sb = A1.get([128, 512])
    tz = [A1.get([128, 512]) for _ in range(2)]
    zb = A1.get([128, 4, 512], BF16)
    yc = [A1.get([128, 4, 512], BF16) for _ in range(2)]

    for kc in range(4):
        P.op("pool", lambda e, kc=kc: e.dma_start(out=wpw[:, kc, :], in_=wpw_d[kc * 128:(kc + 1) * 128, :]),
             r=AR_R, w=[("wpw", kc)], dma=("wpw", kc % 2))
    P.op("sp", lambda e: e.dma_start(out=wdraw[0:31, :], in_=wdw_d[:, :]), r=AR_R, w=["wdraw"], dma="wdraw")
    for c in range(4):
        P.op("pe", lambda e, c=c: e.transpose(out=ps[0][:, c * 32:c * 32 + 31], in_=wdraw[0:31, c * 128:(c + 1) * 128],
                                              identity=ident[0:31, 0:31]), r=["wdraw", "ident"], w=["ps0"])
    P.op("dve", lambda e: e.tensor_copy(out=wdT[:, :, 0:31], in_=ps[0][:, 0:128].rearrange("p (c t) -> p c t", t=32)[:, :, 0:31]),
         r=["ps0"], w=["wdT"])
    for c in range(4):
        for tap in range(31):
            if tap % 2 == 0:
                P.op("dve", lambda e, c=c, tap=tap: e.tensor_scalar(out=Dg[:, c, tap, :], in0=ident[:, :],
                                                                    scalar1=wdT[:, c, tap:tap + 1], scalar2=None,
                                                                    op0=ALU.mult),
                     r=["wdT", "ident"] + AR_R, w=[("Dg", c, 0)])
            else:
                P.op("act", lambda e, c=c, tap=tap: e.activation(out=Dg[:, c, tap, :], in_=ident[:, :], func=AF.Identity,
                                                                 scale=wdT[:, c, tap:tap + 1]),
                     r=["wdT", "ident"] + AR_R, w=[("Dg", c, 1)])
    P.op("pool", lambda e: e.memset(U[0][:, :, 0:30], 0.0), r=AR_R, w=[("Uh", 0)])
    ck(1)

    def cA(tt, cs):
        ub = tt % 2
        tsl = slice(tt * 512, (tt + 1) * 512)
        gcb = gc[tt % 2]
        for c in cs:
            pbase = 0 if c % 2 == 0 else 3
            for gi, goff in enumerate((512, 0, 1024)):
                def f_in(e, c=c, goff=goff, pb=pbase + gi):
                    ins = None
                    for kc in range(8):
                        ins = e.matmul(ps[pb][:, :], lhsT=wc[:, kc, goff + c * 128:goff + (c + 1) * 128],
                                       rhs=hT[:, kc, tsl], start=(kc == 0), stop=(kc == 7))
                    return ins
                P.op("pe", f_in, r=WC_R + HT_R(tt), w=["ps%d" % (pbase + gi)])
            sgb = c % 2
            P.op("act", lambda e, pb=pbase, sgb=sgb: e.activation(out=sg[sgb], in_=ps[pb][:, :], func=AF.Sigmoid),
                 r=["ps%d" % pbase] + AR_R, w=[("sg", sgb)])
            P.op("dve", lambda e, pb=pbase + 1, sgb=sgb, c=c: e.tensor_tensor(
                out=U[ub][:, c, 30:542], in0=ps[pb][:, :], in1=sg[sgb], op=ALU.mult),
                r=["ps%d" % (pbase + 1), ("sg", sgb)], w=[("U", ub, c)])
            P.op("act", lambda e, pb=pbase + 2, c=c: e.activation(out=gcb[:, c, :], in_=ps[pb][:, :], func=AF.Sigmoid),
                 r=["ps%d" % (pbase + 2)], w=[("gc", tt % 2, c)])
            P.op("dve", lambda e, pb=pbase + 2, c=c: e.tensor_tensor(
                out=gcb[:, c, :], in0=ps[pb][:, :], in1=gcb[:, c, :], op=ALU.mult),
                r=["ps%d" % (pbase + 2), ("gc", tt % 2, c)], w=[("gc", tt % 2, c)])
        if tt == NT - 1 and 3 in cs:
            _pre2 = Arena()
            wp_pre = [_pre2.get([128, 8, 512], BF16) for _ in range(2)]
            for p_ in range(2):
                for kc in range(8):
                    P.op("pool", lambda e, p_=p_, kc=kc: e.dma_start(out=wp_pre[p_][:, kc, :],
                                                                  in_=wpair_d[p_, kc * 128:(kc + 1) * 128, :]),
                         r=AR_R,
                         w=[("wp", p_, kc)] + (WC_R if (p_, kc) == (0, 0) else []), dma=("wp", p_, kc % 2))

    def cB(tt):
        ub = tt % 2
        P.op("pool", lambda e: e.tensor_copy(out=U[1 - ub][:, :, 0:30], in_=U[ub][:, :, 512:542]),
             r=[("U", ub, c) for c in range(4)] + AR_R, w=[("Uh", 1 - ub)])

    def cC(tt):
        ub = tt % 2
        for c in range(4):
            pb = 6 + c % 2

            def f_cv(e, c=c, pb=pb):
                ins = None
                for tap in range(31):
                    ins = e.matmul(ps[pb][:, :], lhsT=Dg[:, c, tap, :], rhs=U[ub][:, c, tap:tap + 512],
                                   start=(tap == 0), stop=(tap == 30))
                return ins
            P.op("pe", f_cv, r=[("Dg", c, 0), ("Dg", c, 1), ("U", ub, c), ("Uh", ub)], w=["ps%d" % pb])
            P.op("dve", lambda e, c=c, pb=pb: e.tensor_scalar(out=v32[:, c, :], in0=ps[pb][:, :],
                                                             scalar1=vT[:, 40 + c:41 + c], scalar2=None, op0=ALU.add),
                 r=["ps%d" % pb, "vT"], w=[("v32", c)])
            P.op("act", lambda e, c=c, pb=pb: e.activation(out=vb[:, c, :], in_=ps[pb][:, :], func=AF.Identity,
                                                          bias=vT[:, 40 + c:41 + c], scale=1.0),
                 r=["ps%d" % pb, "vT"], w=[("vb", c)])
            P.op("act", lambda e, c=c, pb=pb: e.activation(out=sqb[:, c, :], in_=ps[pb][:, :], func=AF.Square,
                                                          bias=vT[:, 40 + c:41 + c], scale=1.0),
                 r=["ps%d" % pb, "vT"], w=[("sqb", c)])

    def cD(tt):
        def f_st(e, src, pb):
            ins = None
            for c in range(4):
                ins = e.matmul(ps[pb][:, :], lhsT=onesb[:, :], rhs=src[:, c, :], start=(c == 0), stop=(c == 3))
            return ins
        P.op("pe", lambda e: f_st(e, vb, 0), r=[("vb", c) for c in range(4)] + ["onesb"], w=["ps0"])
        P.op("pe", lambda e: f_st(e, sqb, 1), r=[("sqb", c) for c in range(4)] + ["onesb"], w=["ps1"])
        P.op("act", lambda e: e.activation(out=m2, in_=ps[0][:, :], func=AF.Square), r=["ps0"] + AR_R, w=["m2"])
        P.op("act", lambda e: e.activation(out=mean_sb, in_=ps[0][:, :], func=AF.Identity), r=["ps0"], w=["mean_sb"])
        P.op("dve", lambda e: e.tensor_tensor(out=var, in0=ps[1][:, :], in1=m2, op=ALU.subtract),
             r=["ps1", "m2"], w=["var"])
        P.op("act", lambda e: e.activation(out=var, in_=var, func=AF.Sqrt, bias=EPS, scale=1.0), r=["var"], w=["var"])
        P.op("dve", lambda e: e.reciprocal(out=rs, in_=var), r=["var"], w=["rs"])

    def cE(tt):
        for c in range(4):
            tb = c % 2
            P.op("pool", lambda e, c=c, tb=tb: e.tensor_tensor(out=tz[tb], in0=v32[:, c, :], in1=mean_sb, op=ALU.subtract),
                 r=[("v32", c), "mean_sb"] + AR_R, w=[("tz", tb)])
            P.op("dve", lambda e, tb=tb: e.tensor_tensor(out=tz[tb], in0=tz[tb], in1=rs, op=ALU.mult),
                 r=[("tz", tb), "rs"], w=[("tz", tb)])
            P.op("act", lambda e, c=c, tb=tb: e.activation(out=zb[:, c, :], in_=tz[tb], func=AF.Silu,
                                                          bias=vT[:, 48 + c:49 + c], scale=vT[:, 44 + c:45 + c]),
                 r=[("tz", tb), "vT"], w=[("zb", c)])

    def cF(tt):
        yb = tt % 2
        tsl = slice(tt * 512, (tt + 1) * 512)
        gcb = gc[tt % 2]
        for oc in range(4):
            pb = 3 + oc % 3

            def f_pw(e, oc=oc, pb=pb):
                ins = None
                for k4 in range(4):
                    ins = e.matmul(ps[pb][:, :], lhsT=wpw[:, k4, oc * 128:(oc + 1) * 128], rhs=zb[:, k4, :],
                                   start=(k4 == 0), stop=(k4 == 3))
                return ins
            P.op("pe", f_pw, r=[("wpw", k4) for k4 in range(4)] + [("zb", c) for c in range(4)], w=["ps%d" % pb])
            P.op("dve", lambda e, oc=oc, pb=pb: e.scalar_tensor_tensor(
                out=yc[yb][:, oc, :], in0=ps[pb][:, :], scalar=vT[:, 52 + oc:53 + oc], in1=gcb[:, oc, :],
                op0=ALU.add, op1=ALU.mult), r=["ps%d" % pb, "vT", ("gc", tt % 2, oc)], w=[("yc", yb)])
        P.op("sp", lambda e: e.dma_start(out=ycT_d.rearrange("(c p) t -> p c t", p=128)[:, :, tsl], in_=yc[yb]),
             r=[("yc", yb)] + AR_R, w=[("ycT", tt)], dma=("yc", yb))

    cA(0, [0, 1, 2, 3])
    cB(0)
    cC(0)
    cD(0)
    for tt in range(NT):
        nxt = tt + 1 < NT
        if nxt:
            cA(tt + 1, [0, 1])
        cE(tt)
        if nxt:
            cA(tt + 1, [2, 3])
            cB(tt + 1)
        cF(tt)
        if nxt:
            cC(tt + 1)
            cD(tt + 1)

    if upto == 1:
        P.emit(nc, {("yc", 0): P.dmacnt[("yc", 0)], ("yc", 1): P.dmacnt[("yc", 1)]})
        return nc
    A2 = new_phase()
    wp = [A2.get([128, 8, 512], BF16) for _ in range(2)]
    KAs = [[A2.get([128, S], BF16) for _ in range(2)] for _ in range(2)]
    VAs = [A2.get([128, NS, 2, 128], BF16) for _ in range(2)]
    NQ = 5
    QAs = [[A2.get([128, 512], BF16) for _ in range(2)] for _ in range(NQ)]
    GAs = [A2.get([128, 512], BF16) for _ in range(NQ)]
    GAtmp = A2.get([128, 512])
    QK4 = [A2.get([128, 4, 256])]
    rtmp = [A2.get([128, 4, 4, 8]) for _ in range(4)]
    GSall = A2.get([128, NT, 8, 16])
    Lt = [A2.get([128, 512])]
    kms = [A2.get([128, 16]) for _ in range(2)]
    kmh = [A2.get([128, 16], BF16) for _ in range(2)]
    kml = [A2.get([128, 16], BF16) for _ in range(2)]
    m8 = A2.get([128, 8, 8])
    tmpm = A2.get([128, 8, 16])
    MB = A2.get([128, 4, 128], BF16)
    NPT = 5
    PT = [A2.get([128, 512], BF16) for _ in range(NPT)]
    Rt = [A2.get([128, 512])]
    Tt = A2.get([128, 512])
    Yb = [A2.get([128, 512], BF16) for _ in range(2)]

    for ks in range(2):
        for hh in range(2):
            P.op("act", lambda e, ks=ks, hh=hh: e.memzero(KAs[ks][hh][:, :]), r=AR_R, w=[("KAinit", ks, hh)])
            P.op("pool", lambda e, ks=ks, hh=hh: e.dma_start(out=KAs[ks][hh][64:80, :], in_=oneh_d[:, :]),
                 r=[("KAinit", ks, hh)], w=[("KAm", ks, hh)], dma=("oneh", ks, hh))
        P.op("pool", lambda e, ks=ks: e.memset(VAs[ks][:, :, :, 64:128], 1.0), r=AR_R, w=[("VAones", ks)])
    for hh in range(2):
        for qi in range(NQ):
            P.op("act", lambda e, hh=hh, qi=qi: e.memzero(QAs[qi][hh]), r=AR_R,
                 w=[("QAq", qi, hh), ("QAm", qi, hh)])
        P.op("pool", lambda e, hh=hh: e.memset(kmh[hh], 0.0), r=AR_R, w=[("kmh", hh)])
        P.op("pool", lambda e, hh=hh: e.memset(kml[hh], 0.0), r=AR_R, w=[("kml", hh)])
        P.op("pool", lambda e, hh=hh: e.memset(kms[hh], 0.0), r=AR_R, w=[("kms", hh)])
    P.op("pool", lambda e: e.memset(MB, 0.0), r=AR_R, w=["MB"])
    P.op("pool", lambda e: e.memset(GSall, -1e30), r=AR_R, w=["GSall"])
    for tt in range(NT):
        for half_, b_ in ((0, 2 * tt), (1, 2 * tt + 1)):
            P.op("pool", lambda e, tt=tt, half_=half_, b_=b_: e.memset(GSall[:, tt, half_ * 4:half_ * 4 + 4, b_:b_ + 1], 1e30),
                 r=AR_R, w=["GSall"])

    pt_ctr = [0]
    WP_R = [[("wp", wb_, kc) for kc in range(8)] for wb_ in range(2)]

    def prep(p, tt):
        wb = p % 2
        ks = p % 2
        KA = KAs[ks]
        VA = VAs[ks]
        par = (p * NT + tt) % NQ
        QA = QAs
        GA = GAs
        tsl = slice(tt * 512, (tt + 1) * 512)
        qk4 = QK4[0]

        for s in range(4):
            st_ = tt * 4 + s

            tb = 1 - s % 2

            def f_tok(e, st_=st_, tb=tb):
                ins = None
                for kc in range(8):
                    ins = e.matmul(ps[tb][:, 0:384], lhsT=hT[:, kc, st_ * 128:(st_ + 1) * 128],
                                   rhs=wp[wb][:, kc, 0:384], start=(kc == 0), stop=(kc == 7))
                return ins
            P.op("pe", f_tok, r=WP_R[wb] + HT_R(tt), w=["ps%d" % tb])
            P.op("dve", lambda e, s=s, tb=tb: e.tensor_copy(out=qk4[:, s, :], in_=ps[tb][:, 0:256]),
                 r=["ps%d" % tb] + AR_R, w=[("qk4", s)])
            P.op("dve", lambda e, st_=st_, tb=tb: e.tensor_copy(
                out=VA[:, st_, :, 0:64], in_=ps[tb][:, 256:384].rearrange("p (h d) -> p h d", d=64)),
                r=["ps%d" % tb] + AR_R, w=[("VA", ks, st_)])
            yield
        def f_ga(e):
            ins = None
            for kc in range(8):
                ins = e.matmul(ps[1][:, :], lhsT=wp[wb][:, kc, 384:512], rhs=hT[:, kc, tsl],
                               start=(kc == 0), stop=(kc == 7))
            return ins
        P.op("pe", f_ga, r=WP_R[wb] + HT_R(tt), w=["ps1"])
        P.op("act", lambda e: e.activation(out=GAtmp, in_=ps[1][:, :], func=AF.Exp, scale=-1.0),
             r=["ps1"] + AR_R, w=["GAtmp"])
        P.op("act", lambda e: e.activation(out=GAtmp, in_=GAtmp, func=AF.Ln, bias=1.0, scale=1.0),
             r=["GAtmp"], w=["GAtmp"])
        P.op("act", lambda e: e.activation(out=GAtmp, in_=GAtmp, func=AF.Exp, scale=-1.0),
             r=["GAtmp"], w=["GAtmp"])
        P.op("dve", lambda e: e.tensor_tensor(out=GA[par], in0=ps[1][:, :], in1=GAtmp, op=ALU.mult),
             r=["ps1", "GAtmp"], w=[("GA", par)])
        yield
        Q4 = qk4.rearrange("p s (g d) -> p s g d", d=64)
        c4 = cos4[:, tt * 4:(tt + 1) * 4, :].rearrange("p s (g d) -> p s g d", d=8)
        s4 = sin4[:, tt * 4:(tt + 1) * 4, :].rearrange("p s (g d) -> p s g d", d=8)
        QK_ALL = [("qk4", s) for s in range(4)]
        P.op("dve", lambda e: e.tensor_tensor(out=rtmp[0], in0=Q4[:, :, :, 0:8], in1=c4, op=ALU.mult),
             r=QK_ALL + ["cos4"], w=[("rtmp", 0)])
        P.op("dve", lambda e: e.tensor_tensor(out=rtmp[1], in0=Q4[:, :, :, 8:16], in1=s4, op=ALU.mult),
             r=QK_ALL + ["sin4"], w=[("rtmp", 1)])
        P.op("dve", lambda e: e.tensor_tensor(out=rtmp[2], in0=Q4[:, :, :, 8:16], in1=c4, op=ALU.mult),
             r=QK_ALL + ["cos4"], w=[("rtmp", 2)])
        P.op("dve", lambda e: e.tensor_tensor(out=rtmp[3], in0=Q4[:, :, :, 0:8], in1=s4, op=ALU.mult),
             r=QK_ALL + ["sin4"], w=[("rtmp", 3)])
        P.op("dve", lambda e: e.tensor_tensor(out=Q4[:, :, :, 0:8], in0=rtmp[0], in1=rtmp[1], op=ALU.subtract),
             r=[("rtmp", 0), ("rtmp", 1)], w=QK_ALL)
        P.op("dve", lambda e: e.tensor_tensor(out=Q4[:, :, :, 8:16], in0=rtmp[2], in1=rtmp[3], op=ALU.add),
             r=[("rtmp", 2), ("rtmp", 3)], w=QK_ALL)
        yield
        yield
        for s in range(4):
            h2 = s // 2
            qo = (s % 2) * 128
            tbk = s // 2
            P.op("pe", lambda e, s=s, qo=qo, tbk=tbk: e.transpose(out=ps[tbk][:, qo:qo + 128], in_=qk4[:, s, 0:128],
                                                         identity=ident[:, :]),
                 r=[("qk4", s), "ident"], w=["ps%d" % tbk])
            P.op("pe", lambda e, s=s, qo=qo, tbk=tbk: e.transpose(out=ps[tbk][:, 256 + qo:256 + qo + 128], in_=qk4[:, s, 128:256],
                                                         identity=ident[:, :]),
                 r=[("qk4", s), "ident"], w=["ps%d" % tbk])
            if s % 2 == 1:
                hsl = slice(h2 * 256, (h2 + 1) * 256)
                ksl = slice(tt * 512 + h2 * 256, tt * 512 + (h2 + 1) * 256)
                P.op("dve", lambda e, hsl=hsl, tbk=tbk: e.tensor_copy(out=QA[par][0][0:64, hsl], in_=ps[tbk][0:64, 0:256]),
                     r=["ps%d" % tbk], w=[("QAq", par, 0)])
                P.op("dve", lambda e, hsl=hsl, tbk=tbk: e.tensor_copy(out=QA[par][1][0:64, hsl], in_=ps[tbk][64:128, 0:256]),
                     r=["ps%d" % tbk], w=[("QAq", par, 1)])
                P.op("dve", lambda e, ksl=ksl, tbk=tbk: e.tensor_copy(out=KA[0][0:64, ksl], in_=ps[tbk][0:64, 256:512]),
                     r=["ps%d" % tbk, ("KAinit", ks, 0)], w=[("KA", ks, 0, tt)])
                P.op("dve", lambda e, ksl=ksl, tbk=tbk: e.tensor_copy(out=KA[1][0:64, ksl], in_=ps[tbk][64:128, 256:512]),
                     r=["ps%d" % tbk, ("KAinit", ks, 1)], w=[("KA", ks, 1, tt)])
                yield
        bsl = slice(2 * tt, 2 * tt + 2)
        for hh in range(2):
            P.op("dve", lambda e, hh=hh: e.tensor_reduce(
                out=kms[hh][0:64, bsl], in_=KA[hh][0:64, tsl].rearrange("p (n k) -> p n k", k=256),
                axis=AX.X, op=ALU.add), r=[("KA", ks, hh, tt)], w=[("kms", hh)])
            P.op("dve", lambda e, hh=hh: e.tensor_scalar(out=kmh[hh][0:64, bsl], in0=kms[hh][0:64, bsl],
                                                        scalar1=1.0 / 256.0, scalar2=None, op0=ALU.mult),
                 r=[("kms", hh)], w=[("kmh", hh)])
            P.op("dve", lambda e, hh=hh: e.scalar_tensor_tensor(out=kml[hh][0:64, bsl], in0=kms[hh][0:64, bsl],
                                                               scalar=1.0 / 256.0, in1=kmh[hh][0:64, bsl],
                                                               op0=ALU.mult, op1=ALU.subtract),
                 r=[("kms", hh), ("kmh", hh)], w=[("kml", hh)])
        yield
        b0, b1 = 2 * tt, 2 * tt + 1
        if b0 >= 4:
            def f_gate(e):
                ins = None
                for s in range(4):
                    for hh in range(2):
                        j = s * 2 + hh
                        e.matmul(ps[0][:, j * 16:(j + 1) * 16], lhsT=QA[par][hh][0:80, s * 128:(s + 1) * 128],
                                 rhs=kmh[hh][0:80, :], start=True, stop=False)
                        ins = e.matmul(ps[0][:, j * 16:(j + 1) * 16], lhsT=QA[par][hh][0:80, s * 128:(s + 1) * 128],
                                       rhs=kml[hh][0:80, :], start=False, stop=True)
                return ins
            P.op("pe", f_gate, r=[("QAq", par, 0), ("QAq", par, 1), ("kmh", 0), ("kmh", 1), ("kml", 0), ("kml", 1)],
                 w=["ps0"])
            gsb = GSall[:, tt]
            G3 = ps[0][:, 0:128].rearrange("p (j n) -> p j n", n=16)
            P.op("dve", lambda e: e.tensor_copy(out=gsb[:, 0:4, 0:b0], in_=G3[:, 0:4, 0:b0]),
                 r=["ps0", "GSall"], w=[("gsb", tt)])
            P.op("dve", lambda e: e.tensor_copy(out=gsb[:, 4:8, 0:b1], in_=G3[:, 4:8, 0:b1]),
                 r=["ps0", "GSall"], w=[("gsb", tt)])
            for j in range(8):
                P.op("dve", lambda e, j=j: e.max(out=m8[:, j, :], in_=gsb[:, j, :]), r=[("gsb", tt)], w=[("m8", j)])
            P.op("dve", lambda e: e.tensor_tensor(out=tmpm, in0=gsb, in1=m8[:, :, 3:4].to_broadcast([128, 8, 16]),
                                                  op=ALU.is_lt),
                 r=[("gsb", tt)] + [("m8", j) for j in range(8)], w=[("tmpm", j) for j in range(8)])
            MBv = MB[:, :, 64:128].rearrange("p s (h c) -> p s h c", c=32)[:, :, :, 0:16]
            P.op("dve", lambda e: e.tensor_scalar(out=MBv, in0=tmpm.rearrange("p (s h) n -> p s h n", h=2),
                                                  scalar1=NEGB, scalar2=None, op0=ALU.mult),
                 r=[("tmpm", j) for j in range(8)], w=["MB"])
        else:
            P.op("dve", lambda e: e.memset(MB[:, :, 64:128], 0.0), r=AR_R, w=["MB"])
        yield
        yield
        yield
        def f_mk(e):
            ins = None
            for s in range(4):
                ins = e.matmul(ps[1][:, s * 128:(s + 1) * 128], lhsT=MB[:, s, :], rhs=identb[:, :],
                               start=True, stop=True)
            return ins
        P.op("pe", f_mk, r=["MB", "identb"], w=["ps1"])
        P.op("dve", lambda e: e.tensor_copy(out=QA[par][0][64:80, :], in_=ps[1][64:80, :]),
             r=["ps1"], w=[("QAm", par, 0)])
        P.op("dve", lambda e: e.tensor_copy(out=QA[par][1][64:80, :], in_=ps[1][96:112, :]),
             r=["ps1"], w=[("QAm", par, 1)])
        yield

    def attn(p, tt):
        ks = p % 2
        KA = KAs[ks]
        VA = VAs[ks]
        par = (p * NT + tt) % NQ
        QA = QAs
        GA = GAs
        tsl = slice(tt * 512, (tt + 1) * 512)
        nkt = 4 * tt + 4
        SBK = [2, 3, 4, 5]
        LAG = 4
        pend = []

        def emit_pv(info):
            hh, kt, c0, pti = info
            po = 6 + hh
            P.op("pe", lambda e: e.matmul(ps[po][:, c0:512], lhsT=VA[:, kt, hh, :], rhs=PT[pti][:, c0:512],
                                          start=(kt == 0), stop=(kt == nkt - 1)),
                 r=[("VA", ks, kt), ("VAones", ks), ("PT", pti)], w=["ps%d" % po])
            if kt == nkt - 1:
                rb = 0
                rows = slice(hh * 64, hh * 64 + 64)
                P.op("act", lambda e: e.activation(out=Lt[rb][64:128, :], in_=ps[po][64:128, :], func=AF.Ln),
                     r=["ps%d" % po] + AR_R, w=[("Lt", rb)])
                P.op("act", lambda e: e.activation(out=Rt[rb][64:128, :], in_=Lt[rb][64:128, :], func=AF.Exp, scale=-1.0),
                     r=[("Lt", rb)], w=[("Rt", rb)])
                P.op("dve", lambda e: e.tensor_tensor(out=Tt[rows, :], in0=ps[po][0:64, :], in1=Rt[rb][64:128, :],
                                                      op=ALU.mult),
                     r=["ps%d" % po, ("Rt", rb)], w=[("Tt", hh)])
                P.op("pool", lambda e: e.tensor_tensor(out=Yb[tt % 2][rows, :], in0=Tt[rows, :], in1=GA[par][rows, :],
                                                       op=ALU.mult),
                     r=[("Tt", hh), ("GA", par)] + AR_R, w=[("Yb", tt % 2)])

        steps = [(hh, kt) for hh in range(2) for kt in range(nkt)]
        for i, (hh, kt) in enumerate(steps):
            j = kt - 4 * tt
            c0 = max(0, j) * 128
            sbk = SBK[i % len(SBK)]
            pti = pt_ctr[0] % NPT
            pt_ctr[0] += 1

            def f_s(e, hh=hh, kt=kt, j=j, c0=c0, sbk=sbk):
                ins = e.matmul(ps[sbk][:, c0:512], lhsT=KA[hh][0:80, kt * 128:(kt + 1) * 128],
                               rhs=QA[par][hh][0:80, c0:512], start=True, stop=(j < 0))
                if j >= 0:
                    ins = e.matmul(ps[sbk][:, c0:c0 + 128], lhsT=identb[:, :], rhs=tribb[:, :],
                                   start=False, stop=True)
                return ins
            P.op("pe", f_s, r=[("KA", ks, hh, kt // 4), ("KAm", ks, hh), ("QAq", par, hh), ("QAm", par, hh),
                               "identb", "tribb"], w=["ps%d" % sbk])
            P.op("act", lambda e, sbk=sbk, c0=c0, pti=pti: e.activation(
                out=PT[pti][:, c0:512], in_=ps[sbk][:, c0:512], func=AF.Exp, scale=0.125),
                r=["ps%d" % sbk] + AR_R, w=[("PT", pti)])
            pend.append((hh, kt, c0, pti))
            if len(pend) > LAG:
                emit_pv(pend.pop(0))
            yield
        while pend:
            emit_pv(pend.pop(0))
        P.op("sp", lambda e: e.dma_start(out=yT_d[p * 128:(p + 1) * 128, tsl], in_=Yb[tt % 2]),
             r=[("Yb", tt % 2)] + AR_R, w=[("yT", p, tt)], dma=("Yb", tt % 2))

    def interleave(ga, gb, na, nb):
        done_b = 0
        for i, _ in enumerate(ga):
            want = ((i + 1) * nb + na - 1) // na
            while done_b < want:
                try:
                    next(gb)
                except StopIteration:
                    done_b = 10 ** 9
                    break
                done_b += 1
        for _ in gb:
            pass

    NPREP = 15

    def load_wp(p):
        wb = p % 2
        for kc in range(8):
            P.op("pool", lambda e, kc=kc: e.dma_start(out=wp[wb][:, kc, :], in_=wpair_d[p, kc * 128:(kc + 1) * 128, :]),
                 r=AR_R, w=[("wp", wb, kc)], dma=("wp", wb, kc % 2))

    G = 4 * NT
    nsteps = [2 * (4 * tt + 4) for tt in range(NT)]
    SP = sum(nsteps)
    Wn = SP / NT
    Cn = [sum(nsteps[:tt]) for tt in range(NT)]
    lead = max((n + 1) * Wn - Cn[n] for n in range(NT)) + 2
    prep_pos = []
    for g in range(G):
        p, tt = g // NT, g % NT
        start = p * SP + tt * Wn - lead
        for k in range(NPREP + 1):
            prep_pos.append((start + k * Wn / (NPREP + 1), g))
    preps = {}
    pi = [0]
    cur_g = [0]

    def advance_prep(upto_pos):
        while pi[0] < len(prep_pos) and prep_pos[pi[0]][0] <= upto_pos:
            g = prep_pos[pi[0]][1]
            if g - cur_g[0] > NQ - 2:
                break
            if g not in preps:
                if g % NT == 0 and g // NT >= 2:
                    load_wp(g // NT)
                preps[g] = prep(g // NT, g % NT)
            try:
                next(preps[g])
            except StopIteration:
                pass
            pi[0] += 1

    def finish_prep(g):
        while pi[0] < len(prep_pos) and prep_pos[pi[0]][1] <= g:
            gg = prep_pos[pi[0]][1]
            if gg not in preps:
                if gg % NT == 0 and gg // NT >= 2:
                    load_wp(gg // NT)
                preps[gg] = prep(gg // NT, gg % NT)
            try:
                next(preps[gg])
            except StopIteration:
                pass
            pi[0] += 1
        if g in preps:
            for _ in preps[g]:
                pass

    pos = 0
    _pre3 = Arena()
    wo = _pre3.get([128, 8, D], BF16)
    for g in range(G):
        cur_g[0] = g
        finish_prep(g)
        if g == G - 1:
            for kc in range(8):
                for half in range(2):
                    P.op("pool", lambda e, kc=kc, half=half: e.dma_start(
                        out=wo[:, kc, half * 512:(half + 1) * 512],
                        in_=wout_d[kc * 128:(kc + 1) * 128, half * 512:(half + 1) * 512]),
                        r=AR_R,
                        w=[("wo", kc, half)] + (WP_R[0] + WP_R[1] if (kc, half) == (0, 0) else []),
                        dma=("wo", (kc * 2 + half) % 4))
        for _ in attn(g // NT, g % NT):
            pos += 1
            advance_prep(pos)
    for kc in range(8):
        eng = "dve" if kc % 2 == 0 else "pool"
        WO_ALL = [("wo", k_, h_) for k_ in range(8) for h_ in range(2)]
        P.op(eng, lambda e, kc=kc: e.tensor_tensor(out=wo[:, kc, :], in0=wo[:, kc, :], in1=gate_bc[:, :], op=ALU.mult),
             r=[("gate_bc", 0), ("gate_bc", 1)] + AR_R + (WO_ALL if kc == 0 else ["wo_ready"]),
             w=[("wo", kc, 0), ("wo", kc, 1)] + (["wo_ready"] + WO_ALL if kc == 0 else []))
    if upto == 2:
        P.emit(nc, {("Yb", 0): P.dmacnt[("Yb", 0)], ("Yb", 1): P.dmacnt[("Yb", 1)]})
        return nc
    A3 = new_phase()
    wo_ = A3.get([128, 8, D], BF16)
    YT = [A3.get([128, 8, 512], BF16) for _ in range(2)]
    NX3 = 4
    xt3 = [A3.get([128, D]) for _ in range(NX3)]
    NOO = 3
    oo = [A3.get([128, D]) for _ in range(NOO)]
    sq3 = A3.get([128, D])
    sd3 = A3.get([128, NS])
    WO_R = [("wo", kc, h_) for kc in range(8) for h_ in range(2)]
    final_waits = {}

    def load_YT(tt):
        yb = tt % 2
        tsl = slice(tt * 512, (tt + 1) * 512)
        P.op("sp", lambda e: e.dma_start(out=YT[yb][:, 0:4, :], in_=ycT_d.rearrange("(c p) t -> p c t", p=128)[:, :, tsl]),
             r=[("ycT", tt)] + AR_R, w=[("YTa", yb)], dma=("YTa", yb))
        P.op("sp", lambda e: e.dma_start(out=YT[yb][:, 4:8, :], in_=yT_d.rearrange("(c p) t -> p c t", p=128)[:, :, tsl]),
             r=[("yT", p, tt) for p in range(4)] + AR_R, w=[("YTb", yb)], dma=("YTb", yb))

    NRR = 4
    rr = [A3.get([128, D]) for _ in range(NRR)]
    tails = []

    def emit_tail(st_):
        rb_ = st_ % NRR
        ob = st_ % NOO
        P.op("dve", lambda e: e.reciprocal(out=rstd2[:, st_:st_ + 1], in_=sd3[:, st_:st_ + 1]),
             r=[("sd3", st_)], w=[("rstd2", st_)])
        P.op("act", lambda e: e.activation(out=oo[ob], in_=rr[rb_], func=AF.Identity, scale=rstd2[:, st_:st_ + 1]),
             r=[("rr", rb_, 0), ("rr", rb_, 1), ("rstd2", st_)], w=[("oo", ob)])
        P.op("pool", lambda e: e.tensor_tensor(out=oo[ob], in0=oo[ob], in1=gfin_bc[:, :], op=ALU.mult),
             r=[("oo", ob), "gfin"] + AR_R, w=[("oo", ob)])
        P.op("act", lambda e: e.dma_start(out=out_d[st_ * 128:(st_ + 1) * 128, :], in_=oo[ob]),
             r=[("oo", ob)] + AR_R, dma=("oo", ob))

    load_YT(0)
    for tt in range(NT):
        yb = tt % 2
        tsl = slice(tt * 512, (tt + 1) * 512)
        if tt + 1 < NT:
            load_YT(tt + 1)
        for s in range(4):
            st_ = tt * 4 + s
            xb_ = st_ % NX3
            rb_ = st_ % NRR
            P.op("sp", lambda e, st_=st_, xb_=xb_: e.dma_start(out=xt3[xb_], in_=x_d[st_ * 128:(st_ + 1) * 128, :]),
                 r=AR_R, w=[("xt3", xb_)], dma=("xt3", xb_))
            for half in range(2):
                pbk = (st_ % 4) * 2 + half

                def f_o(e, yb=yb, s=s, half=half, pbk=pbk):
                    ins = None
                    for c in range(8):
                        ins = e.matmul(ps[pbk][:, :], lhsT=YT[yb][:, c, s * 128:(s + 1) * 128],
                                       rhs=wo[:, c, half * 512:(half + 1) * 512], start=(c == 0), stop=(c == 7))
                    return ins
                P.op("pe", f_o, r=[("YTa", yb), ("YTb", yb)] + WO_R, w=["ps%d" % pbk])
                hs = slice(half * 512, (half + 1) * 512)
                P.op("dve", lambda e, hs=hs, rb_=rb_, pbk=pbk, xb_=xb_: e.tensor_tensor(
                    out=rr[rb_][:, hs], in0=ps[pbk][:, :], in1=xt3[xb_][:, hs], op=ALU.add),
                    r=["ps%d" % pbk, ("xt3", xb_)] + AR_R, w=[("rr", rb_, half)])
            P.op("act", lambda e, rb_=rb_, st_=st_: e.activation(out=sq3, in_=rr[rb_], func=AF.Square,
                                                                accum_out=ssq2[:, st_:st_ + 1]),
                 r=[("rr", rb_, 0), ("rr", rb_, 1), "ssq2"], w=["sq3", ("ssq2", st_)])
            P.op("act", lambda e, st_=st_: e.activation(out=sd3[:, st_:st_ + 1], in_=ssq2[:, st_:st_ + 1], func=AF.Sqrt,
                                                       bias=EPS, scale=1.0 / D), r=[("ssq2", st_)], w=[("sd3", st_)])
            tails.append(st_)
            if len(tails) > 2:
                emit_tail(tails.pop(0))
    while tails:
        emit_tail(tails.pop(0))
    for ob in range(NOO):
        final_waits[("oo", ob)] = P.dmacnt[("oo", ob)]
    if debug:
        final_waits["dbg_hT"] = 1
    P.emit(nc, final_waits)
    return nc


def host_inputs(S, x, c, positions, w_ada, b_ada, g_norm, w_in, w_dw, b_dw, g_ln_conv, b_ln_conv, w_pw, b_pw,
                w_out, g_final):
    B = x.shape[0]
    NS = S // 128
    f = np.float32
    w_in0 = np.asarray(w_in[0], f)
    w_c = np.ascontiguousarray(w_in0[:, 0:1536])
    w_pair = np.stack([np.concatenate([w_in0[:, 1536 + 128 * p:1536 + 128 * (p + 1)],
                                       w_in0[:, 2048 + 128 * p:2048 + 128 * (p + 1)],
                                       w_in0[:, 2560 + 128 * p:2560 + 128 * (p + 1)],
                                       w_in0[:, 3072 + 128 * p:3072 + 128 * (p + 1)]], axis=1) for p in range(4)])
    ident = np.eye(128, dtype=f)
    half = 8
    inv = (500000.0 ** (-(np.arange(half, dtype=np.float32) * 2.0) / 16.0)).astype(f)
    invf = np.ascontiguousarray(np.broadcast_to(np.tile(inv, 4)[None, :], (128, 32))).astype(f)
    pk = np.arange(128)
    trib = np.where(pk[:, None] <= pk[None, :], 0.0, NEGB).astype(f)
    onehot = (np.arange(16)[:, None] == (np.arange(S)[None, :] // 256)).astype(f)
    maps = []
    for b in range(B):
        vecs = np.concatenate([np.asarray(b_ada[0], f).reshape(24, 128), np.asarray(g_norm[0], f).reshape(8, 128),
                               np.asarray(c[b], f).reshape(8, 128), np.asarray(b_dw[0], f).reshape(4, 128),
                               np.asarray(g_ln_conv[0], f).reshape(4, 128), np.asarray(b_ln_conv[0], f).reshape(4, 128),
                               np.asarray(b_pw[0], f).reshape(4, 128)], axis=0)
        pos = np.ascontiguousarray(np.asarray(positions[b], np.int32).reshape(NS, 128).T)
        maps.append({
            "x": np.ascontiguousarray(np.asarray(x[b], f)), "vecs": np.ascontiguousarray(vecs),
            "b_ada": np.ascontiguousarray(np.asarray(b_ada[0], f)), "pos": pos,
            "w_ada": np.ascontiguousarray(np.asarray(w_ada[0], f)), "w_c": w_c, "w_pair": np.ascontiguousarray(w_pair),
            "w_dw": np.ascontiguousarray(np.asarray(w_dw[0], f)), "w_pw": np.ascontiguousarray(np.asarray(w_pw[0], f)),
            "w_out": np.ascontiguousarray(np.asarray(w_out[0], f)), "g_final": np.ascontiguousarray(np.asarray(g_final, f)),
            "ident": ident, "invf": invf, "trib": trib, "onehot": onehot,
        })
    return maps


_NC_CACHE = {}


def kernel(x, c, positions, w_ada, b_ada, g_norm, w_in, w_dw, b_dw, g_ln_conv, b_ln_conv, w_pw, b_pw, w_out, g_final):
    x = np.asarray(x)
    B, S, _ = x.shape
    maps = host_inputs(S, x, np.asarray(c), np.asarray(positions), np.asarray(w_ada), np.asarray(b_ada),
                       np.asarray(g_norm), np.asarray(w_in), np.asarray(w_dw), np.asarray(b_dw), np.asarray(g_ln_conv),
                       np.asarray(b_ln_conv), np.asarray(w_pw), np.asarray(b_pw), np.asarray(w_out), np.asarray(g_final))
    if S not in _NC_CACHE:
        _NC_CACHE[S] = build(S)
    nc = _NC_CACHE[S]
    res = run_bass_kernel_spmd(nc, maps, core_ids=list(range(B)))
    return np.stack([np.asarray(r["out"], np.float32) for r in res.results], axis=0)
```

```python
import numpy as np
import ml_dtypes
from contextlib import ExitStack
import concourse.bass as bass
import concourse.mybir as mybir
from concourse.bass_utils import run_bass_kernel_spmd

F32, BF16, I32 = mybir.dt.float32, mybir.dt.bfloat16, mybir.dt.int32
AF = mybir.ActivationFunctionType
ALU = mybir.AluOpType
AX = mybir.AxisListType

D = 1024
EPS = 1e-6
NEGB = -30000.0
ENGS = ["sp", "pe", "act", "dve", "pool"]
LAST_PROG = None


class Prog:
    def __init__(self):
        self.ops = {e: [] for e in ENGS}
        self.lastw = {}
        self.readers = {}
        self.dmacnt = {}
        self.bar = None

    def barrier(self, fn):
        deps = set()
        for e in ENGS:
            if self.ops[e]:
                o = self.ops[e][-1]
                deps.add(o["me"])
            for o in self.ops[e]:
                if o["dma"] is not None:
                    deps.add(o["me"])
        me = self.op("pool", fn)
        self.ops["pool"][-1]["deps"] |= deps
        self.bar = me
        return me

    def op(self, eng, fn, r=(), w=(), dma=None):
        idx = len(self.ops[eng])
        w = list(w) + [x for x in r if isinstance(x, str) and x.startswith("ps") and x not in w]
        deps = set()
        if self.bar is not None:
            deps.add(self.bar)
        for x in r:
            if x in self.lastw:
                deps.add(self.lastw[x])
        for x in w:
            if x in self.lastw:
                deps.add(self.lastw[x])
            for rd in self.readers.get(x, ()):
                deps.add(rd)
        if dma is not None:
            n = self.dmacnt.get(dma, 0) + 1
            self.dmacnt[dma] = n
            me = ("dma", dma, n)
        else:
            me = (eng, idx)
        deps.discard(me)
        self.ops[eng].append(dict(fn=fn, deps=deps, dma=dma, sig=False, me=me))
        for x in r:
            self.readers.setdefault(x, []).append(me)
        for x in w:
            self.lastw[x] = me
            self.readers[x] = []
        return me

    def emit(self, nc, final_waits):
        for e in ENGS:
            for o in self.ops[e]:
                for d in o["deps"]:
                    if d[0] != "dma" and not (d[0] == "pe" and e == "pe"):
                        self.ops[d[0]][d[1]]["sig"] = True
        cnt = {}
        for e in ENGS:
            c = 0
            arr = []
            for o in self.ops[e]:
                if o["sig"]:
                    c += 1
                arr.append(c)
            cnt[e] = arr
        with ExitStack() as st:
            esem = {e: st.enter_context(nc.semaphore("es_" + e)) for e in ENGS}
            dsem = {k: st.enter_context(nc.semaphore("ds_%d" % i))
                    for i, k in enumerate(self.dmacnt)}
            block = st.enter_context(nc.Block())

            def run(h, e):
                waited = {}
                for o in self.ops[e]:
                    need = {}
                    for d in o["deps"]:
                        if d[0] == "dma":
                            k = ("dma", d[1])
                            v = 16 * d[2]
                        else:
                            if d[0] == "pe" and e == "pe":
                                continue
                            k = d[0]
                            v = cnt[d[0]][d[1]]
                        need[k] = max(need.get(k, 0), v)
                    for k, v in need.items():
                        if waited.get(k, 0) >= v:
                            continue
                        waited[k] = v
                        h.wait_ge(dsem[k[1]] if isinstance(k, tuple) else esem[k], v)
                    ins = o["fn"](h)
                    if o["dma"] is not None:
                        ins.then_inc(dsem[o["dma"]], 16)
                    elif o["sig"]:
                        ins.then_inc(esem[e], 1)
                if e == "sp":
                    for k, n in final_waits.items():
                        h.wait_ge(dsem[k], 16 * n)

            @block.sync
            def _(h):
                run(h, "sp")

            @block.tensor
            def _(h):
                run(h, "pe")

            @block.scalar
            def _(h):
                run(h, "act")

            @block.vector
            def _(h):
                run(h, "dve")

            @block.gpsimd
            def _(h):
                run(h, "pool")


class _Cut(Exception):
    pass


def build(S, debug=False, upto=3, cut=0):
    try:
        return _build(S, debug, upto, cut)
    except _Cut as c:
        return c.args[0]


def _build(S, debug=False, upto=3, cut=0):
    NT, NS, NB = S // 512, S // 128, S // 256
    nc = bass.Bass("TRN2", target_bir_lowering=False)
    P = Prog()
    global LAST_PROG
    LAST_PROG = P

    def din(name, shape, dt=F32):
        return nc.dram_tensor(name, list(shape), dt, kind="ExternalInput").ap()

    x_d = din("x", [S, D])
    vecs_d = din("vecs", [56, 128])
    bada_d = din("b_ada", [3 * D])
    pos_d = din("pos", [128, NS], I32)
    wada_d = din("w_ada", [D, 3 * D])
    wc_d = din("w_c", [D, 1536])
    wpair_d = din("w_pair", [4, D, 512])
    wdw_d = din("w_dw", [31, 512])
    wpw_d = din("w_pw", [512, 512])
    wout_d = din("w_out", [D, D])
    gfin_d = din("g_final", [D])
    ident_d = din("ident", [128, 128])
    invf_d = din("invf", [128, 32])
    trib_d = din("trib", [128, 128])
    oneh_d = din("onehot", [16, S])
    out_d = nc.dram_tensor("out", [S, D], F32, kind="ExternalOutput").ap()
    skind = "ExternalOutput" if debug else "Internal"
    ycT_d = nc.dram_tensor("ycT", [512, S], BF16, kind=skind).ap()
    yT_d = nc.dram_tensor("yT", [512, S], BF16, kind=skind).ap()
    if debug:
        hT_dbg = nc.dram_tensor("hT_dbg", [128, 8 * S], BF16, kind="ExternalOutput").ap()

    def sb(name, shape, dt=F32):
        return nc.alloc_sbuf_tensor("s_" + name, list(shape), dt)

    ident = sb("ident", [128, 128])
    identb = sb("identb", [128, 128], BF16)
    onesb = sb("onesb", [128, 128], BF16)
    vT = sb("vT", [128, 56])
    sc = sb("sc", [128, 8])
    modc = sb("modc", [128, 16])
    Acol = sb("Acol", [128, 8])
    shc = sb("shc", [128, 8])
    gate_bc = sb("gate_bc", [128, D])
    gfin_bc = sb("gfin_bc", [128, D])
    cos4 = sb("cos4", [128, NS, 32])
    sin4 = sb("sin4", [128, NS, 32])
    tribb = sb("tribb", [128, 128], BF16)
    hT = sb("hT", [128, 8, S], BF16)
    ssq = sb("ssq", [128, NS])
    rstd = sb("rstd", [128, NS])
    ssq2 = sb("ssq2", [128, NS])
    rstd2 = sb("rstd2", [128, NS])
    ARW = 31600
    AR = sb("arena", [128, ARW])
    ps = [nc.alloc_psum_tensor("ps%d" % i, [128, 512], F32) for i in range(8)]

    class Arena:
        def __init__(self):
            self.off = 0

        def get(self, shape, dt=F32):
            n = int(np.prod(shape[1:]))
            words = (n * (4 if dt in (F32, I32) else 2) + 3) // 4
            words = (words + 7) // 8 * 8
            a = AR[:, self.off:self.off + words]
            self.off += words
            assert self.off <= ARW, (self.off, ARW)
            if dt != F32:
                a = a.bitcast(dt)
            a = a[:, 0:n]
            if len(shape) == 3:
                a = a.rearrange("p (a b) -> p a b", b=shape[2])
            elif len(shape) == 4:
                a = a.rearrange("p (a b c) -> p a b c", b=shape[2], c=shape[3])
            return a

    phase_no = [0]

    def AT():
        return ("arena",)

    def new_phase():
        P.barrier(lambda e: e.memset(dummy[:, 0:1], 0.0))
        return Arena()

    dummy = sb("dummy", [128, 8])
    AR_R = [AT()]

    A0 = Arena()
    vraw = A0.get([128, 128])
    onesf = A0.get([128, 128])
    wa = [A0.get([128, 3 * D]) for _ in range(2)]
    gbias = A0.get([128, D])
    posi = A0.get([128, NS], I32)
    posf = A0.get([128, NS])
    invf = A0.get([128, 32])
    ang = A0.get([128, NS, 32])
    ki = A0.get([128, NS, 32], I32)
    kf = A0.get([128, NS, 32])
    tribf = A0.get([128, 128])
    sgn = A0.get([128, 32])
    NXT = 6
    xt = [A0.get([128, D]) for _ in range(NXT)]
    NXN = 8
    xn = [A0.get([128, D]) for _ in range(NXN)]
    sq_junk = A0.get([128, D])
    sdv = A0.get([128, NS])
    scb2 = A0.get([128, 8, 128])
    wg = [A0.get([128, D]) for _ in range(4)]

    P.op("sp", lambda e: e.dma_start(out=ident[:, :], in_=ident_d[:, :]), w=["ident"], dma="c0")
    P.op("sp", lambda e: e.dma_start(out=vraw[0:56, :], in_=vecs_d[:, :]), r=AR_R, w=["vraw"], dma="c1")
    P.op("sp", lambda e: e.dma_start(out=posi, in_=pos_d[:, :]), r=AR_R, w=["posi"], dma="c2")
    P.op("sp", lambda e: e.dma_start(out=invf, in_=invf_d[:, :]), r=AR_R, w=["invf"], dma="c3")
    P.op("sp", lambda e: e.dma_start(out=tribf, in_=trib_d[:, :]), r=AR_R, w=["tribf"], dma="c4")
    P.op("sp", lambda e: e.dma_start(out=gfin_bc[:, :], in_=gfin_d.partition_broadcast(128)), w=["gfin"], dma="c5")
    P.op("sp", lambda e: e.dma_start(out=gbias, in_=bada_d[2 * D:3 * D].partition_broadcast(128)), r=AR_R, w=["gbias"], dma="c6")

    P.op("dve", lambda e: e.tensor_copy(out=identb[:, :], in_=ident[:, :]), r=["ident"], w=["identb"])
    P.op("dve", lambda e: e.tensor_copy(out=tribb[:, :], in_=tribf), r=["tribf"], w=["tribb"])
    P.op("pool", lambda e: e.memset(onesb[:, :], 1.0 / 512.0), w=["onesb"])
    P.op("pool", lambda e: e.memset(onesf, 1.0), r=AR_R, w=["onesf"])
    P.op("pool", lambda e: e.memset(ssq[:, :], 0.0), w=["ssq"])
    P.op("pool", lambda e: e.memset(ssq2[:, :], 0.0), w=["ssq2"])

    P.op("pe", lambda e: e.transpose(out=ps[0][:, 0:56], in_=vraw[0:56, :], identity=ident[0:56, 0:56]),
         r=["vraw", "ident"], w=["ps0"])
    P.op("dve", lambda e: e.tensor_copy(out=vT[:, :], in_=ps[0][:, 0:56]), r=["ps0"], w=["vT"])
    P.op("act", lambda e: e.activation(out=sc[:, :], in_=vT[:, 32:40], func=AF.Silu), r=["vT"], w=["sc"])
    for kc in range(8):
        P.op("dve", lambda e, kc=kc: e.tensor_scalar(out=scb2[:, kc, :], in0=onesf, scalar1=sc[:, kc:kc + 1],
                                                     scalar2=None, op0=ALU.mult),
             r=["onesf", "sc"], w=[("scb", kc)])

    TWO_PI = 2.0 * np.pi
    HI = 6.28125
    LO = TWO_PI - HI
    P.op("dve", lambda e: e.tensor_copy(out=posf, in_=posi), r=["posi"], w=["posf"])
    for st_ in range(NS):
        P.op("dve", lambda e, s=st_: e.tensor_scalar(out=ang[:, s, :], in0=invf, scalar1=posf[:, s:s + 1],
                                                     scalar2=None, op0=ALU.mult),
             r=["posf", "invf"], w=[("ang", st_)])
    ANG_ALL = [("ang", s) for s in range(NS)]

    def trig(dst, shift, tag):
        P.op("dve", lambda e: e.tensor_scalar(out=kf, in0=ang, scalar1=shift, scalar2=1.0 / TWO_PI,
                                              op0=ALU.add, op1=ALU.mult), r=ANG_ALL + ["cs_prev"], w=["kf"])
        P.op("dve", lambda e: e.tensor_copy(out=ki, in_=kf), r=["kf"], w=["ki"])
        P.op("dve", lambda e: e.tensor_copy(out=kf, in_=ki), r=["ki"], w=["kf"])
        P.op("dve", lambda e: e.scalar_tensor_tensor(out=dst[:, :, :], in0=kf, scalar=-HI, in1=ang,
                                                     op0=ALU.mult, op1=ALU.add), r=["kf"] + ANG_ALL, w=[tag])
        P.op("dve", lambda e: e.scalar_tensor_tensor(out=dst[:, :, :], in0=kf, scalar=-LO, in1=dst[:, :, :],
                                                     op0=ALU.mult, op1=ALU.add), r=["kf", tag], w=[tag])
        if shift != 0.0:
            P.op("dve", lambda e: e.tensor_scalar(out=dst[:, :, :], in0=dst[:, :, :], scalar1=shift, scalar2=None,
                                                  op0=ALU.add), r=[tag], w=[tag])
        P.op("dve", lambda e: e.tensor_scalar(out=dst[:, :, :], in0=dst[:, :, :], scalar1=3.14159, scalar2=-3.14159,
                                              op0=ALU.min, op1=ALU.max), r=[tag], w=[tag])
        P.op("act", lambda e: e.activation(out=dst[:, :, :], in_=dst[:, :, :], func=AF.Sin), r=[tag], w=[tag, "cs_prev"])

    trig(sin4, 0.0, "sin4")
    trig(cos4, np.pi / 2.0, "cos4")

    for kc in range(8):
        b = kc % 2
        P.op("act", lambda e, kc=kc, b=b: e.dma_start(out=wa[b][:, 0:2 * D], in_=wada_d[kc * 128:(kc + 1) * 128, 0:2 * D]),
             r=AR_R, w=[("wa", b)], dma=("wa", b))

        def f_cols(e, kc=kc, b=b):
            ins = None
            for m in range(16):
                ins = e.matmul(ps[0][:, kc * 16 + m:kc * 16 + m + 1], lhsT=wa[b][:, m * 128:(m + 1) * 128],
                               rhs=sc[:, kc:kc + 1], start=True, stop=True)
            return ins
        P.op("pe", f_cols, r=[("wa", b), "sc"], w=["ps0"])

    def gate_dma(kc):
        P.op("act", lambda e: e.dma_start(out=wg[kc % 4], in_=wada_d[kc * 128:(kc + 1) * 128, 2 * D:3 * D]),
             r=AR_R, w=[("wg", kc % 4)], dma=("wg", kc % 4))

    def gate_mm(kc):
        for half in range(2):
            P.op("pe", lambda e, half=half: e.matmul(
                ps[1 + half][:, :], lhsT=scb2[:, kc, :], rhs=wg[kc % 4][:, half * 512:(half + 1) * 512],
                start=(kc == 0), stop=(kc == 7)), r=[("wg", kc % 4), ("scb", kc)], w=["ps%d" % (1 + half)])

    gate_dma(0)
    gate_dma(1)
    P.op("dve", lambda e: e.tensor_reduce(out=modc[:, :], in_=ps[0][:, 0:128].rearrange("p (k m) -> p m k", m=16),
                                          axis=AX.X, op=ALU.add), r=["ps0"], w=["modc"])
    P.op("dve", lambda e: e.tensor_tensor(out=modc[:, :], in0=modc[:, :], in1=vT[:, 0:16], op=ALU.add),
         r=["modc", "vT"], w=["modc"])
    P.op("dve", lambda e: e.tensor_copy(out=shc[:, :], in_=modc[:, 0:8]), r=["modc"], w=["shc"])
    P.op("dve", lambda e: e.scalar_tensor_tensor(out=Acol[:, :], in0=modc[:, 8:16], scalar=1.0, in1=vT[:, 24:32],
                                                 op0=ALU.add, op1=ALU.mult), r=["modc", "vT"], w=["Acol"])

    _pre1 = Arena()
    wc = _pre1.get([128, 8, 1536], BF16)
    DEAD0 = ["vraw", "onesf", ("wa", 0), ("wa", 1)]
    for kc in range(8):
        for part in range(3):
            P.op("pool", lambda e, kc=kc, part=part: e.dma_start(
                out=wc[:, kc, part * 512:(part + 1) * 512], in_=wc_d[kc * 128:(kc + 1) * 128, part * 512:(part + 1) * 512]),
                r=AR_R,
                w=[("wc", kc, part)] + (DEAD0 if (kc, part) == (0, 0) else []), dma=("wc", (kc * 3 + part) % 4))

    def xstage1(tt):
        for s in range(4):
            st_ = tt * 4 + s
            xb_, nb_ = st_ % NXT, st_ % NXN
            P.op("sp", lambda e, st_=st_, xb_=xb_: e.dma_start(out=xt[xb_], in_=x_d[st_ * 128:(st_ + 1) * 128, :]),
                 r=AR_R, w=[("xt", xb_)], dma=("xt", xb_))
            P.op("act", lambda e, st_=st_, xb_=xb_: e.activation(out=sq_junk, in_=xt[xb_], func=AF.Square,
                                                                accum_out=ssq[:, st_:st_ + 1]),
                 r=[("xt", xb_), "ssq"], w=["sqj", ("ssq", st_)])
            P.op("act", lambda e, st_=st_: e.activation(out=sdv[:, st_:st_ + 1], in_=ssq[:, st_:st_ + 1], func=AF.Sqrt,
                                                       bias=EPS, scale=1.0 / D), r=[("ssq", st_)], w=[("sdv", st_)])
            P.op("dve", lambda e, st_=st_: e.reciprocal(out=rstd[:, st_:st_ + 1], in_=sdv[:, st_:st_ + 1]),
                 r=[("sdv", st_)], w=[("rstd", st_)])
            P.op("dve", lambda e, st_=st_, xb_=xb_, nb_=nb_: e.tensor_scalar(
                out=xn[nb_], in0=xt[xb_], scalar1=rstd[:, st_:st_ + 1], scalar2=None, op0=ALU.mult),
                r=[("xt", xb_), ("rstd", st_)], w=[("xn", nb_)])

    def xstage2(tt):
        for kc in range(8):
            pb = 3 + kc % 4

            def f_tr(e, kc=kc, pb=pb):
                ins = None
                for s in range(4):
                    nb_ = (tt * 4 + s) % NXN
                    ins = e.transpose(out=ps[pb][:, s * 128:(s + 1) * 128], in_=xn[nb_][:, kc * 128:(kc + 1) * 128],
                                      identity=ident[:, :])
                return ins
            P.op("pe", f_tr, r=[("xn", (tt * 4 + s) % NXN) for s in range(4)] + ["ident"], w=["ps%d" % pb])
            P.op("act", lambda e, kc=kc, pb=pb: e.activation(
                out=hT[:, kc, tt * 512:(tt + 1) * 512], in_=ps[pb][:, :], func=AF.Identity,
                bias=shc[:, kc:kc + 1], scale=Acol[:, kc:kc + 1]),
                r=["ps%d" % pb, "shc", "Acol"], w=[("hT", tt)])

    xstage1(0)
    GPT = 8 // NT if NT <= 8 else 1
    for tt in range(NT):
        if tt + 1 < NT:
            xstage1(tt + 1)
        xstage2(tt)
        for kc in range(tt * GPT, (tt + 1) * GPT):
            if kc + 2 < 8:
                gate_dma(kc + 2)
            gate_mm(kc)
    for half in range(2):
        P.op("dve", lambda e, half=half: e.tensor_tensor(
            out=gate_bc[:, half * 512:(half + 1) * 512], in0=ps[1 + half][:, :],
            in1=gbias[:, half * 512:(half + 1) * 512], op=ALU.add),
            r=["ps%d" % (1 + half), "gbias"], w=[("gate_bc", half)])
    if debug:
        P.op("sp", lambda e: e.dma_start(out=hT_dbg[:, :], in_=hT[:, :, :].rearrange("p a b -> p (a b)")),
             r=[("hT", t) for t in range(NT)], dma="dbg_hT")

    HT_R = lambda tt: [("hT", tt)]

    def ck(n):
        if cut == n:
            P.barrier(lambda e: e.memset(dummy[:, 0:1], 0.0))
            P.emit(nc, {})
            raise _Cut(nc)
    if upto == 0:
        P.emit(nc, {"dbg_hT": 1})
        return nc

    A1 = new_phase()
    wc_ = A1.get([128, 8, 1536], BF16)
    WC_R = [("wc", kc, part) for kc in range(8) for part in range(3)]
    wpw = A1.get([128, 4, 512], BF16)
    wdraw = A1.get([128, 512])
    wdT = A1.get([128, 4, 32])
    Dg = A1.get([128, 4, 31, 128], BF16)
    U = [A1.get([128, 4, 542], BF16) for _ in range(2)]
    sg = [A1.get([128, 512]) for _ in range(2)]
    gc = [A1.get([128, 4, 512], BF16) for _ in range(2)]
    v32 = A1.get([128, 4, 512])
    vb = A1.get([128, 4, 512], BF16)
    sqb = A1.get([128, 4, 512], BF16)
    m2 = A1.get([128, 512])
    var = A1.get([128, 512])
    rs = A1.get([128, 512])
    mean_sb = A1.get([128, 512])
    tz = [A1.get([128, 512]) for _ in range(2)]
    zb = A1.get([128, 4, 512], BF16)
    yc = [A1.get([128, 4, 512], BF16) for _ in range(2)]

    for kc in range(4):
        P.op("pool", lambda e, kc=kc: e.dma_start(out=wpw[:, kc, :], in_=wpw_d[kc * 128:(kc + 1) * 128, :]),
             r=AR_R, w=[("wpw", kc)], dma=("wpw", kc % 2))
    P.op("sp", lambda e: e.dma_start(out=wdraw[0:31, :], in_=wdw_d[:, :]), r=AR_R, w=["wdraw"], dma="wdraw")
    for c in range(4):
        P.op("pe", lambda e, c=c: e.transpose(out=ps[0][:, c * 32:c * 32 + 31], in_=wdraw[0:31, c * 128:(c + 1) * 128],
                                              identity=ident[0:31, 0:31]), r=["wdraw", "ident"], w=["ps0"])
    P.op("dve", lambda e: e.tensor_copy(out=wdT[:, :, 0:31], in_=ps[0][:, 0:128].rearrange("p (c t) -> p c t", t=32)[:, :, 0:31]),
         r=["ps0"], w=["wdT"])
    P.op("pool", lambda e: e.memset(U[0][:, :, 0:30], 0.0), r=AR_R, w=[("Uh", 0)])
    ck(1)

    def cA(tt, cs):
        ub = tt % 2
        tsl = slice(tt * 512, (tt + 1) * 512)
        gcb = gc[tt % 2]
        for c in cs:
            pbase = 0 if c % 2 == 0 else 3
            for gi, goff in enumerate((512, 0, 1024)):
                def f_in(e, c=c, goff=goff, pb=pbase + gi):
                    ins = None
                    for kc in range(8):
                        ins = e.matmul(ps[pb][:, :], lhsT=wc[:, kc, goff + c * 128:goff + (c + 1) * 128],
                                       rhs=hT[:, kc, tsl], start=(kc == 0), stop=(kc == 7))
                    return ins
                P.op("pe", f_in, r=WC_R + HT_R(tt), w=["ps%d" % (pbase + gi)])
            sgb = c % 2
            P.op("act", lambda e, pb=pbase, sgb=sgb: e.activation(out=sg[sgb], in_=ps[pb][:, :], func=AF.Sigmoid),
                 r=["ps%d" % pbase] + AR_R, w=[("sg", sgb)])
            P.op("dve", lambda e, pb=pbase + 1, sgb=sgb, c=c: e.tensor_tensor(
                out=U[ub][:, c, 30:542], in0=ps[pb][:, :], in1=sg[sgb], op=ALU.mult),
                r=["ps%d" % (pbase + 1), ("sg", sgb)], w=[("U", ub, c)])
            P.op("act", lambda e, pb=pbase + 2, c=c: e.activation(out=gcb[:, c, :], in_=ps[pb][:, :], func=AF.Sigmoid),
                 r=["ps%d" % (pbase + 2)], w=[("gc", tt % 2, c)])
            P.op("dve", lambda e, pb=pbase + 2, c=c: e.tensor_tensor(
                out=gcb[:, c, :], in0=ps[pb][:, :], in1=gcb[:, c, :], op=ALU.mult),
                r=["ps%d" % (pbase + 2), ("gc", tt % 2, c)], w=[("gc", tt % 2, c)])
        if tt == NT - 1 and 3 in cs:
            _pre2 = Arena()
            wp_pre = [_pre2.get([128, 8, 512], BF16) for _ in range(2)]
            for p_ in range(2):
                for kc in range(8):
                    P.op("pool", lambda e, p_=p_, kc=kc: e.dma_start(out=wp_pre[p_][:, kc, :],
                                                                  in_=wpair_d[p_, kc * 128:(kc + 1) * 128, :]),
                         r=AR_R,
                         w=[("wp", p_, kc)] + (WC_R if (p_, kc) == (0, 0) else []), dma=("wp", p_, kc % 2))

    def cB(tt):
        ub = tt % 2
        P.op("pool", lambda e: e.tensor_copy(out=U[1 - ub][:, :, 0:30], in_=U[ub][:, :, 512:542]),
             r=[("U", ub, c) for c in range(4)] + AR_R, w=[("Uh", 1 - ub)])

    def cC(tt):
        ub = tt % 2
        for c in range(4):
            pb = 6 + c % 2

            def f_cv(e, c=c, pb=pb):
                ins = None
                for tap in range(31):
                    ins = e.matmul(ps[pb][:, :], lhsT=Dg[:, c, tap, :], rhs=U[ub][:, c, tap:tap + 512],
                                   start=(tap == 0), stop=(tap == 30))
                return ins
            P.op("pe", f_cv, r=[("Dg", c, 0), ("Dg", c, 1), ("U", ub, c), ("Uh", ub)], w=["ps%d" % pb])
            P.op("dve", lambda e, c=c, pb=pb: e.tensor_scalar(out=v32[:, c, :], in0=ps[pb][:, :],
                                                             scalar1=vT[:, 40 + c:41 + c], scalar2=None, op0=ALU.add),
                 r=["ps%d" % pb, "vT"], w=[("v32", c)])
            P.op("act", lambda e, c=c, pb=pb: e.activation(out=vb[:, c, :], in_=ps[pb][:, :], func=AF.Identity,
                                                          bias=vT[:, 40 + c:41 + c], scale=1.0),
                 r=["ps%d" % pb, "vT"], w=[("vb", c)])
            P.op("act", lambda e, c=c, pb=pb: e.activation(out=sqb[:, c, :], in_=ps[pb][:, :], func=AF.Square,
                                                          bias=vT[:, 40 + c:41 + c], scale=1.0),
                 r=["ps%d" % pb, "vT"], w=[("sqb", c)])

    def cD(tt):
        def f_st(e, src, pb):
            ins = None
            for c in range(4):
                ins = e.matmul(ps[pb][:, :], lhsT=onesb[:, :], rhs=src[:, c, :], start=(c == 0), stop=(c == 3))
            return ins
        P.op("pe", lambda e: f_st(e, vb, 0), r=[("vb", c) for c in range(4)] + ["onesb"], w=["ps0"])
        P.op("pe", lambda e: f_st(e, sqb, 1), r=[("sqb", c) for c in range(4)] + ["onesb"], w=["ps1"])
        P.op("act", lambda e: e.activation(out=m2, in_=ps[0][:, :], func=AF.Square), r=["ps0"] + AR_R, w=["m2"])
        P.op("act", lambda e: e.activation(out=mean_sb, in_=ps[0][:, :], func=AF.Identity), r=["ps0"], w=["mean_sb"])
        P.op("dve", lambda e: e.tensor_tensor(out=var, in0=ps[1][:, :], in1=m2, op=ALU.subtract),
             r=["ps1", "m2"], w=["var"])
        P.op("act", lambda e: e.activation(out=var, in_=var, func=AF.Sqrt, bias=EPS, scale=1.0), r=["var"], w=["var"])
        P.op("dve", lambda e: e.reciprocal(out=rs, in_=var), r=["var"], w=["rs"])

    def cE(tt):
        for c in range(4):
            tb = c % 2
            P.op("pool", lambda e, c=c, tb=tb: e.tensor_tensor(out=tz[tb], in0=v32[:, c, :], in1=mean_sb, op=ALU.subtract),
                 r=[("v32", c), "mean_sb"] + AR_R, w=[("tz", tb)])
            P.op("dve", lambda e, tb=tb: e.tensor_tensor(out=tz[tb], in0=tz[tb], in1=rs, op=ALU.mult),
                 r=[("tz", tb), "rs"], w=[("tz", tb)])
            P.op("act", lambda e, c=c, tb=tb: e.activation(out=zb[:, c, :], in_=tz[tb], func=AF.Silu,
                                                          bias=vT[:, 48 + c:49 + c], scale=vT[:, 44 + c:45 + c]),
                 r=[("tz", tb), "vT"], w=[("zb", c)])

    def cF(tt):
        yb = tt % 2
        tsl = slice(tt * 512, (tt + 1) * 512)
        gcb = gc[tt % 2]
        for oc in range(4):
            pb = 3 + oc % 3

            def f_pw(e, oc=oc, pb=pb):
                ins = None
                for k4 in range(4):
                    ins = e.matmul(ps[pb][:, :], lhsT=wpw[:, k4, oc * 128:(oc + 1) * 128], rhs=zb[:, k4, :],
                                   start=(k4 == 0), stop=(k4 == 3))
                return ins
            P.op("pe", f_pw, r=[("wpw", k4) for k4 in range(4)] + [("zb", c) for c in range(4)], w=["ps%d" % pb])
            P.op("dve", lambda e, oc=oc, pb=pb: e.scalar_tensor_tensor(
                out=yc[yb][:, oc, :], in0=ps[pb][:, :], scalar=vT[:, 52 + oc:53 + oc], in1=gcb[:, oc, :],
                op0=ALU.add, op1=ALU.mult), r=["ps%d" % pb, "vT", ("gc", tt % 2, oc)], w=[("yc", yb)])
        P.op("sp", lambda e: e.dma_start(out=ycT_d.rearrange("(c p) t -> p c t", p=128)[:, :, tsl], in_=yc[yb]),
             r=[("yc", yb)] + AR_R, w=[("ycT", tt)], dma=("yc", yb))

    cA(0, [0, 1, 2, 3])
    for c in range(4):
        for tap in range(31):
            if tap % 2 == 0:
                P.op("dve", lambda e, c=c, tap=tap: e.tensor_scalar(out=Dg[:, c, tap, :], in0=ident[:, :],
                                                                    scalar1=wdT[:, c, tap:tap + 1], scalar2=None,
                                                                    op0=ALU.mult),
                     r=["wdT", "ident"] + AR_R, w=[("Dg", c, 0)])
            else:
                P.op("act", lambda e, c=c, tap=tap: e.activation(out=Dg[:, c, tap, :], in_=ident[:, :], func=AF.Identity,
                                                                 scale=wdT[:, c, tap:tap + 1]),
                     r=["wdT", "ident"] + AR_R, w=[("Dg", c, 1)])
    cB(0)
    cC(0)
    cD(0)
    for tt in range(NT):
        nxt = tt + 1 < NT
        if nxt:
            cA(tt + 1, [0, 1])
        cE(tt)
        if nxt:
            cA(tt + 1, [2, 3])
            cB(tt + 1)
        cF(tt)
        if nxt:
            cC(tt + 1)
            cD(tt + 1)

    if upto == 1:
        P.emit(nc, {("yc", 0): P.dmacnt[("yc", 0)], ("yc", 1): P.dmacnt[("yc", 1)]})
        return nc
    A2 = new_phase()
    wp = [A2.get([128, 8, 512], BF16) for _ in range(2)]
    KAs = [[A2.get([128, S], BF16) for _ in range(2)] for _ in range(2)]
    VAs = [A2.get([128, NS, 2, 128], BF16) for _ in range(2)]
    NQ = 5
    QAs = [[A2.get([128, 512], BF16) for _ in range(2)] for _ in range(NQ)]
    GAs = [A2.get([128, 512], BF16) for _ in range(NQ)]
    GAtmp = A2.get([128, 512])
    QK4 = [A2.get([128, 4, 256])]
    rtmp = [A2.get([128, 4, 4, 8]) for _ in range(4)]
    GSall = A2.get([128, NT, 8, 16])
    Lt = [A2.get([128, 512])]
    kms = [A2.get([128, 16]) for _ in range(2)]
    kmh = [A2.get([128, 16], BF16) for _ in range(2)]
    kml = [A2.get([128, 16], BF16) for _ in range(2)]
    m8 = A2.get([128, 8, 8])
    tmpm = A2.get([128, 8, 16])
    MB = A2.get([128, 4, 128], BF16)
    NPT = 5
    PT = [A2.get([128, 512], BF16) for _ in range(NPT)]
    Rt = [A2.get([128, 512])]
    Tt = A2.get([128, 512])
    Yb = [A2.get([128, 512], BF16) for _ in range(2)]

    for ks in range(2):
        for hh in range(2):
            P.op("act", lambda e, ks=ks, hh=hh: e.memzero(KAs[ks][hh][:, :]), r=AR_R, w=[("KAinit", ks, hh)])
            P.op("pool", lambda e, ks=ks, hh=hh: e.dma_start(out=KAs[ks][hh][64:80, :], in_=oneh_d[:, :]),
                 r=[("KAinit", ks, hh)], w=[("KAm", ks, hh)], dma=("oneh", ks, hh))
        P.op("pool", lambda e, ks=ks: e.memset(VAs[ks][:, :, :, 64:128], 1.0), r=AR_R, w=[("VAones", ks)])
    for hh in range(2):
        for qi in range(NQ):
            P.op("act", lambda e, hh=hh, qi=qi: e.memzero(QAs[qi][hh]), r=AR_R,
                 w=[("QAq", qi, hh), ("QAm", qi, hh)])
        P.op("pool", lambda e, hh=hh: e.memset(kmh[hh], 0.0), r=AR_R, w=[("kmh", hh)])
        P.op("pool", lambda e, hh=hh: e.memset(kml[hh], 0.0), r=AR_R, w=[("kml", hh)])
        P.op("pool", lambda e, hh=hh: e.memset(kms[hh], 0.0), r=AR_R, w=[("kms", hh)])
    P.op("pool", lambda e: e.memset(MB, 0.0), r=AR_R, w=["MB"])
    P.op("pool", lambda e: e.memset(GSall, -1e30), r=AR_R, w=["GSall"])
    for tt in range(NT):
        for half_, b_ in ((0, 2 * tt), (1, 2 * tt + 1)):
            P.op("pool", lambda e, tt=tt, half_=half_, b_=b_: e.memset(GSall[:, tt, half_ * 4:half_ * 4 + 4, b_:b_ + 1], 1e30),
                 r=AR_R, w=["GSall"])

    pt_ctr = [0]
    WP_R = [[("wp", wb_, kc) for kc in range(8)] for wb_ in range(2)]

    def prep(p, tt):
        wb = p % 2
        ks = p % 2
        KA = KAs[ks]
        VA = VAs[ks]
        par = (p * NT + tt) % NQ
        QA = QAs
        GA = GAs
        tsl = slice(tt * 512, (tt + 1) * 512)
        qk4 = QK4[0]

        for s in range(4):
            st_ = tt * 4 + s

            tb = 1 - s % 2

            def f_tok(e, st_=st_, tb=tb):
                ins = None
                for kc in range(8):
                    ins = e.matmul(ps[tb][:, 0:384], lhsT=hT[:, kc, st_ * 128:(st_ + 1) * 128],
                                   rhs=wp[wb][:, kc, 0:384], start=(kc == 0), stop=(kc == 7))
                return ins
            P.op("pe", f_tok, r=WP_R[wb] + HT_R(tt), w=["ps%d" % tb])
            P.op("dve", lambda e, s=s, tb=tb: e.tensor_copy(out=qk4[:, s, :], in_=ps[tb][:, 0:256]),
                 r=["ps%d" % tb] + AR_R, w=[("qk4", s)])
            P.op("dve", lambda e, st_=st_, tb=tb: e.tensor_copy(
                out=VA[:, st_, :, 0:64], in_=ps[tb][:, 256:384].rearrange("p (h d) -> p h d", d=64)),
                r=["ps%d" % tb] + AR_R, w=[("VA", ks, st_)])
            yield
        def f_ga(e):
            ins = None
            for kc in range(8):
                ins = e.matmul(ps[1][:, :], lhsT=wp[wb][:, kc, 384:512], rhs=hT[:, kc, tsl],
                               start=(kc == 0), stop=(kc == 7))
            return ins
        P.op("pe", f_ga, r=WP_R[wb] + HT_R(tt), w=["ps1"])
        P.op("act", lambda e: e.activation(out=GAtmp, in_=ps[1][:, :], func=AF.Exp, scale=-1.0),
             r=["ps1"] + AR_R, w=["GAtmp"])
        P.op("act", lambda e: e.activation(out=GAtmp, in_=GAtmp, func=AF.Ln, bias=1.0, scale=1.0),
             r=["GAtmp"], w=["GAtmp"])
        P.op("act", lambda e: e.activation(out=GAtmp, in_=GAtmp, func=AF.Exp, scale=-1.0),
             r=["GAtmp"], w=["GAtmp"])
        P.op("dve", lambda e: e.tensor_tensor(out=GA[par], in0=ps[1][:, :], in1=GAtmp, op=ALU.mult),
             r=["ps1", "GAtmp"], w=[("GA", par)])
        yield
        Q4 = qk4.rearrange("p s (g d) -> p s g d", d=64)
        c4 = cos4[:, tt * 4:(tt + 1) * 4, :].rearrange("p s (g d) -> p s g d", d=8)
        s4 = sin4[:, tt * 4:(tt + 1) * 4, :].rearrange("p s (g d) -> p s g d", d=8)
        QK_ALL = [("qk4", s) for s in range(4)]
        P.op("dve", lambda e: e.tensor_tensor(out=rtmp[0], in0=Q4[:, :, :, 0:8], in1=c4, op=ALU.mult),
             r=QK_ALL + ["cos4"], w=[("rtmp", 0)])
        P.op("dve", lambda e: e.tensor_tensor(out=rtmp[1], in0=Q4[:, :, :, 8:16], in1=s4, op=ALU.mult),
             r=QK_ALL + ["sin4"], w=[("rtmp", 1)])
        P.op("dve", lambda e: e.tensor_tensor(out=rtmp[2], in0=Q4[:, :, :, 8:16], in1=c4, op=ALU.mult),
             r=QK_ALL + ["cos4"], w=[("rtmp", 2)])
        P.op("dve", lambda e: e.tensor_tensor(out=rtmp[3], in0=Q4[:, :, :, 0:8], in1=s4, op=ALU.mult),
             r=QK_ALL + ["sin4"], w=[("rtmp", 3)])
        P.op("dve", lambda e: e.tensor_tensor(out=Q4[:, :, :, 0:8], in0=rtmp[0], in1=rtmp[1], op=ALU.subtract),
             r=[("rtmp", 0), ("rtmp", 1)], w=QK_ALL)
        P.op("dve", lambda e: e.tensor_tensor(out=Q4[:, :, :, 8:16], in0=rtmp[2], in1=rtmp[3], op=ALU.add),
             r=[("rtmp", 2), ("rtmp", 3)], w=QK_ALL)
        yield
        yield
        for s in range(4):
            h2 = s // 2
            qo = (s % 2) * 128
            tbk = s // 2
            P.op("pe", lambda e, s=s, qo=qo, tbk=tbk: e.transpose(out=ps[tbk][:, qo:qo + 128], in_=qk4[:, s, 0:128],
                                                         identity=ident[:, :]),
                 r=[("qk4", s), "ident"], w=["ps%d" % tbk])
            P.op("pe", lambda e, s=s, qo=qo, tbk=tbk: e.transpose(out=ps[tbk][:, 256 + qo:256 + qo + 128], in_=qk4[:, s, 128:256],
                                                         identity=ident[:, :]),
                 r=[("qk4", s), "ident"], w=["ps%d" % tbk])
            if s % 2 == 1:
                hsl = slice(h2 * 256, (h2 + 1) * 256)
                ksl = slice(tt * 512 + h2 * 256, tt * 512 + (h2 + 1) * 256)
                P.op("dve", lambda e, hsl=hsl, tbk=tbk: e.tensor_copy(out=QA[par][0][0:64, hsl], in_=ps[tbk][0:64, 0:256]),
                     r=["ps%d" % tbk], w=[("QAq", par, 0)])
                P.op("dve", lambda e, hsl=hsl, tbk=tbk: e.tensor_copy(out=QA[par][1][0:64, hsl], in_=ps[tbk][64:128, 0:256]),
                     r=["ps%d" % tbk], w=[("QAq", par, 1)])
                P.op("dve", lambda e, ksl=ksl, tbk=tbk: e.tensor_copy(out=KA[0][0:64, ksl], in_=ps[tbk][0:64, 256:512]),
                     r=["ps%d" % tbk, ("KAinit", ks, 0)], w=[("KA", ks, 0, tt)])
                P.op("dve", lambda e, ksl=ksl, tbk=tbk: e.tensor_copy(out=KA[1][0:64, ksl], in_=ps[tbk][64:128, 256:512]),
                     r=["ps%d" % tbk, ("KAinit", ks, 1)], w=[("KA", ks, 1, tt)])
                yield
        bsl = slice(2 * tt, 2 * tt + 2)
        for hh in range(2):
            P.op("dve", lambda e, hh=hh: e.tensor_reduce(
                out=kms[hh][0:64, bsl], in_=KA[hh][0:64, tsl].rearrange("p (n k) -> p n k", k=256),
                axis=AX.X, op=ALU.add), r=[("KA", ks, hh, tt)], w=[("kms", hh)])
            P.op("dve", lambda e, hh=hh: e.tensor_scalar(out=kmh[hh][0:64, bsl], in0=kms[hh][0:64, bsl],
                                                        scalar1=1.0 / 256.0, scalar2=None, op0=ALU.mult),
                 r=[("kms", hh)], w=[("kmh", hh)])
            P.op("dve", lambda e, hh=hh: e.scalar_tensor_tensor(out=kml[hh][0:64, bsl], in0=kms[hh][0:64, bsl],
                                                               scalar=1.0 / 256.0, in1=kmh[hh][0:64, bsl],
                                                               op0=ALU.mult, op1=ALU.subtract),
                 r=[("kms", hh), ("kmh", hh)], w=[("kml", hh)])
        yield
        b0, b1 = 2 * tt, 2 * tt + 1
        if b0 >= 4:
            def f_gate(e):
                ins = None
                for s in range(4):
                    for hh in range(2):
                        j = s * 2 + hh
                        e.matmul(ps[0][:, j * 16:(j + 1) * 16], lhsT=QA[par][hh][0:80, s * 128:(s + 1) * 128],
                                 rhs=kmh[hh][0:80, :], start=True, stop=False)
                        ins = e.matmul(ps[0][:, j * 16:(j + 1) * 16], lhsT=QA[par][hh][0:80, s * 128:(s + 1) * 128],
                                       rhs=kml[hh][0:80, :], start=False, stop=True)
                return ins
            P.op("pe", f_gate, r=[("QAq", par, 0), ("QAq", par, 1), ("kmh", 0), ("kmh", 1), ("kml", 0), ("kml", 1)],
                 w=["ps0"])
            gsb = GSall[:, tt]
            G3 = ps[0][:, 0:128].rearrange("p (j n) -> p j n", n=16)
            P.op("dve", lambda e: e.tensor_copy(out=gsb[:, 0:4, 0:b0], in_=G3[:, 0:4, 0:b0]),
                 r=["ps0", "GSall"], w=[("gsb", tt)])
            P.op("dve", lambda e: e.tensor_copy(out=gsb[:, 4:8, 0:b1], in_=G3[:, 4:8, 0:b1]),
                 r=["ps0", "GSall"], w=[("gsb", tt)])
            for j in range(8):
                P.op("dve", lambda e, j=j: e.max(out=m8[:, j, :], in_=gsb[:, j, :]), r=[("gsb", tt)], w=[("m8", j)])
            P.op("dve", lambda e: e.tensor_tensor(out=tmpm, in0=gsb, in1=m8[:, :, 3:4].to_broadcast([128, 8, 16]),
                                                  op=ALU.is_lt),
                 r=[("gsb", tt)] + [("m8", j) for j in range(8)], w=[("tmpm", j) for j in range(8)])
            MBv = MB[:, :, 64:128].rearrange("p s (h c) -> p s h c", c=32)[:, :, :, 0:16]
            P.op("dve", lambda e: e.tensor_scalar(out=MBv, in0=tmpm.rearrange("p (s h) n -> p s h n", h=2),
                                                  scalar1=NEGB, scalar2=None, op0=ALU.mult),
                 r=[("tmpm", j) for j in range(8)], w=["MB"])
        else:
            P.op("dve", lambda e: e.memset(MB[:, :, 64:128], 0.0), r=AR_R, w=["MB"])
        yield
        yield
        yield
        def f_mk(e):
            ins = None
            for s in range(4):
                ins = e.matmul(ps[1][:, s * 128:(s + 1) * 128], lhsT=MB[:, s, :], rhs=identb[:, :],
                               start=True, stop=True)
            return ins
        P.op("pe", f_mk, r=["MB", "identb"], w=["ps1"])
        P.op("dve", lambda e: e.tensor_copy(out=QA[par][0][64:80, :], in_=ps[1][64:80, :]),
             r=["ps1"], w=[("QAm", par, 0)])
        P.op("dve", lambda e: e.tensor_copy(out=QA[par][1][64:80, :], in_=ps[1][96:112, :]),
             r=["ps1"], w=[("QAm", par, 1)])
        yield

    def attn(p, tt):
        ks = p % 2
        KA = KAs[ks]
        VA = VAs[ks]
        par = (p * NT + tt) % NQ
        QA = QAs
        GA = GAs
        tsl = slice(tt * 512, (tt + 1) * 512)
        nkt = 4 * tt + 4
        SBK = [2, 3, 4, 5]
        LAG = 4
        pend = []

        def emit_pv(info):
            hh, kt, c0, pti = info
            po = 6 + hh
            P.op("pe", lambda e: e.matmul(ps[po][:, c0:512], lhsT=VA[:, kt, hh, :], rhs=PT[pti][:, c0:512],
                                          start=(kt == 0), stop=(kt == nkt - 1)),
                 r=[("VA", ks, kt), ("VAones", ks), ("PT", pti)], w=["ps%d" % po])
            if kt == nkt - 1:
                rb = 0
                rows = slice(hh * 64, hh * 64 + 64)
                P.op("act", lambda e: e.activation(out=Lt[rb][64:128, :], in_=ps[po][64:128, :], func=AF.Ln),
                     r=["ps%d" % po] + AR_R, w=[("Lt", rb)])
                P.op("act", lambda e: e.activation(out=Rt[rb][64:128, :], in_=Lt[rb][64:128, :], func=AF.Exp, scale=-1.0),
                     r=[("Lt", rb)], w=[("Rt", rb)])
                P.op("dve", lambda e: e.tensor_tensor(out=Tt[rows, :], in0=ps[po][0:64, :], in1=Rt[rb][64:128, :],
                                                      op=ALU.mult),
                     r=["ps%d" % po, ("Rt", rb)], w=[("Tt", hh)])
                P.op("pool", lambda e: e.tensor_tensor(out=Yb[tt % 2][rows, :], in0=Tt[rows, :], in1=GA[par][rows, :],
                                                       op=ALU.mult),
                     r=[("Tt", hh), ("GA", par)] + AR_R, w=[("Yb", tt % 2)])

        steps = [(hh, kt) for hh in range(2) for kt in range(nkt)]
        for i, (hh, kt) in enumerate(steps):
            j = kt - 4 * tt
            c0 = max(0, j) * 128
            sbk = SBK[i % len(SBK)]
            pti = pt_ctr[0] % NPT
            pt_ctr[0] += 1

            def f_s(e, hh=hh, kt=kt, j=j, c0=c0, sbk=sbk):
                ins = e.matmul(ps[sbk][:, c0:512], lhsT=KA[hh][0:80, kt * 128:(kt + 1) * 128],
                               rhs=QA[par][hh][0:80, c0:512], start=True, stop=(j < 0))
                if j >= 0:
                    ins = e.matmul(ps[sbk][:, c0:c0 + 128], lhsT=identb[:, :], rhs=tribb[:, :],
                                   start=False, stop=True)
                return ins
            P.op("pe", f_s, r=[("KA", ks, hh, kt // 4), ("KAm", ks, hh), ("QAq", par, hh), ("QAm", par, hh),
                               "identb", "tribb"], w=["ps%d" % sbk])
            P.op("act", lambda e, sbk=sbk, c0=c0, pti=pti: e.activation(
                out=PT[pti][:, c0:512], in_=ps[sbk][:, c0:512], func=AF.Exp, scale=0.125),
                r=["ps%d" % sbk] + AR_R, w=[("PT", pti)])
            pend.append((hh, kt, c0, pti))
            if len(pend) > LAG:
                emit_pv(pend.pop(0))
            yield
        while pend:
            emit_pv(pend.pop(0))
        P.op("sp", lambda e: e.dma_start(out=yT_d[p * 128:(p + 1) * 128, tsl], in_=Yb[tt % 2]),
             r=[("Yb", tt % 2)] + AR_R, w=[("yT", p, tt)], dma=("Yb", tt % 2))

    def interleave(ga, gb, na, nb):
        done_b = 0
        for i, _ in enumerate(ga):
            want = ((i + 1) * nb + na - 1) // na
            while done_b < want:
                try:
                    next(gb)
                except StopIteration:
                    done_b = 10 ** 9
                    break
                done_b += 1
        for _ in gb:
            pass

    NPREP = 15

    def load_wp(p):
        wb = p % 2
        for kc in range(8):
            P.op("pool", lambda e, kc=kc: e.dma_start(out=wp[wb][:, kc, :], in_=wpair_d[p, kc * 128:(kc + 1) * 128, :]),
                 r=AR_R, w=[("wp", wb, kc)], dma=("wp", wb, kc % 2))

    G = 4 * NT
    nsteps = [2 * (4 * tt + 4) for tt in range(NT)]
    SP = sum(nsteps)
    Wn = SP / NT
    Cn = [sum(nsteps[:tt]) for tt in range(NT)]
    lead = max((n + 1) * Wn - Cn[n] for n in range(NT)) + 2
    prep_pos = []
    for g in range(G):
        p, tt = g // NT, g % NT
        start = p * SP + tt * Wn - lead
        for k in range(NPREP + 1):
            prep_pos.append((start + k * Wn / (NPREP + 1), g))
    preps = {}
    pi = [0]
    cur_g = [0]

    def advance_prep(upto_pos):
        while pi[0] < len(prep_pos) and prep_pos[pi[0]][0] <= upto_pos:
            g = prep_pos[pi[0]][1]
            if g - cur_g[0] > NQ - 2:
                break
            if g not in preps:
                if g % NT == 0 and g // NT >= 2:
                    load_wp(g // NT)
                preps[g] = prep(g // NT, g % NT)
            try:
                next(preps[g])
            except StopIteration:
                pass
            pi[0] += 1

    def finish_prep(g):
        while pi[0] < len(prep_pos) and prep_pos[pi[0]][1] <= g:
            gg = prep_pos[pi[0]][1]
            if gg not in preps:
                if gg % NT == 0 and gg // NT >= 2:
                    load_wp(gg // NT)
                preps[gg] = prep(gg // NT, gg % NT)
            try:
                next(preps[gg])
            except StopIteration:
                pass
            pi[0] += 1
        if g in preps:
            for _ in preps[g]:
                pass

    pos = 0
    _pre3 = Arena()
    wo = _pre3.get([128, 8, D], BF16)
    for g in range(G):
        cur_g[0] = g
        finish_prep(g)
        if g == G - 1:
            for kc in range(8):
                for half in range(2):
                    P.op("pool", lambda e, kc=kc, half=half: e.dma_start(
                        out=wo[:, kc, half * 512:(half + 1) * 512],
                        in_=wout_d[kc * 128:(kc + 1) * 128, half * 512:(half + 1) * 512]),
                        r=AR_R,
                        w=[("wo", kc, half)] + (WP_R[0] + WP_R[1] if (kc, half) == (0, 0) else []),
                        dma=("wo", (kc * 2 + half) % 4))
        for _ in attn(g // NT, g % NT):
            pos += 1
            advance_prep(pos)
    for kc in range(8):
        eng = "dve" if kc % 2 == 0 else "pool"
        WO_ALL = [("wo", k_, h_) for k_ in range(8) for h_ in range(2)]
        P.op(eng, lambda e, kc=kc: e.tensor_tensor(out=wo[:, kc, :], in0=wo[:, kc, :], in1=gate_bc[:, :], op=ALU.mult),
             r=[("gate_bc", 0), ("gate_bc", 1)] + AR_R + (WO_ALL if kc == 0 else ["wo_ready"]),
             w=[("wo", kc, 0), ("wo", kc, 1)] + (["wo_ready"] + WO_ALL if kc == 0 else []))
    if upto == 2:
        P.emit(nc, {("Yb", 0): P.dmacnt[("Yb", 0)], ("Yb", 1): P.dmacnt[("Yb", 1)]})
        return nc
    A3 = new_phase()
    wo_ = A3.get([128, 8, D], BF16)
    YT = [A3.get([128, 8, 512], BF16) for _ in range(2)]
    NX3 = 4
    xt3 = [A3.get([128, D]) for _ in range(NX3)]
    NOO = 3
    oo = [A3.get([128, D]) for _ in range(NOO)]
    sq3 = A3.get([128, D])
    sd3 = A3.get([128, NS])
    WO_R = [("wo", kc, h_) for kc in range(8) for h_ in range(2)]
    final_waits = {}

    def load_YT(tt):
        yb = tt % 2
        tsl = slice(tt * 512, (tt + 1) * 512)
        P.op("sp", lambda e: e.dma_start(out=YT[yb][:, 0:4, :], in_=ycT_d.rearrange("(c p) t -> p c t", p=128)[:, :, tsl]),
             r=[("ycT", tt)] + AR_R, w=[("YTa", yb)], dma=("YTa", yb))
        P.op("sp", lambda e: e.dma_start(out=YT[yb][:, 4:8, :], in_=yT_d.rearrange("(c p) t -> p c t", p=128)[:, :, tsl]),
             r=[("yT", p, tt) for p in range(4)] + AR_R, w=[("YTb", yb)], dma=("YTb", yb))

    NRR = 4
    rr = [A3.get([128, D]) for _ in range(NRR)]
    tails = []

    def emit_tail(st_):
        rb_ = st_ % NRR
        ob = st_ % NOO
        P.op("dve", lambda e: e.reciprocal(out=rstd2[:, st_:st_ + 1], in_=sd3[:, st_:st_ + 1]),
             r=[("sd3", st_)], w=[("rstd2", st_)])
        P.op("act", lambda e: e.activation(out=oo[ob], in_=rr[rb_], func=AF.Identity, scale=rstd2[:, st_:st_ + 1]),
             r=[("rr", rb_, 0), ("rr", rb_, 1), ("rstd2", st_)], w=[("oo", ob)])
        P.op("pool", lambda e: e.tensor_tensor(out=oo[ob], in0=oo[ob], in1=gfin_bc[:, :], op=ALU.mult),
             r=[("oo", ob), "gfin"] + AR_R, w=[("oo", ob)])
        P.op("act", lambda e: e.dma_start(out=out_d[st_ * 128:(st_ + 1) * 128, :], in_=oo[ob]),
             r=[("oo", ob)] + AR_R, dma=("oo", ob))

    load_YT(0)
    for tt in range(NT):
        yb = tt % 2
        tsl = slice(tt * 512, (tt + 1) * 512)
        if tt + 1 < NT:
            load_YT(tt + 1)
        for s in range(4):
            st_ = tt * 4 + s
            xb_ = st_ % NX3
            rb_ = st_ % NRR
            P.op("sp", lambda e, st_=st_, xb_=xb_: e.dma_start(out=xt3[xb_], in_=x_d[st_ * 128:(st_ + 1) * 128, :]),
                 r=AR_R, w=[("xt3", xb_)], dma=("xt3", xb_))
            for half in range(2):
                pbk = (st_ % 4) * 2 + half

                def f_o(e, yb=yb, s=s, half=half, pbk=pbk):
                    ins = None
                    for c in range(8):
                        ins = e.matmul(ps[pbk][:, :], lhsT=YT[yb][:, c, s * 128:(s + 1) * 128],
                                       rhs=wo[:, c, half * 512:(half + 1) * 512], start=(c == 0), stop=(c == 7))
                    return ins
                P.op("pe", f_o, r=[("YTa", yb), ("YTb", yb)] + WO_R, w=["ps%d" % pbk])
                hs = slice(half * 512, (half + 1) * 512)
                P.op("dve", lambda e, hs=hs, rb_=rb_, pbk=pbk, xb_=xb_: e.tensor_tensor(
                    out=rr[rb_][:, hs], in0=ps[pbk][:, :], in1=xt3[xb_][:, hs], op=ALU.add),
                    r=["ps%d" % pbk, ("xt3", xb_)] + AR_R, w=[("rr", rb_, half)])
            P.op("act", lambda e, rb_=rb_, st_=st_: e.activation(out=sq3, in_=rr[rb_], func=AF.Square,
                                                                accum_out=ssq2[:, st_:st_ + 1]),
                 r=[("rr", rb_, 0), ("rr", rb_, 1), "ssq2"], w=["sq3", ("ssq2", st_)])
            P.op("act", lambda e, st_=st_: e.activation(out=sd3[:, st_:st_ + 1], in_=ssq2[:, st_:st_ + 1], func=AF.Sqrt,
                                                       bias=EPS, scale=1.0 / D), r=[("ssq2", st_)], w=[("sd3", st_)])
            tails.append(st_)
            if len(tails) > 2:
                emit_tail(tails.pop(0))
    while tails:
        emit_tail(tails.pop(0))
    for ob in range(NOO):
        final_waits[("oo", ob)] = P.dmacnt[("oo", ob)]
    if debug:
        final_waits["dbg_hT"] = 1
    P.emit(nc, final_waits)
    return nc


def host_inputs(S, x, c, positions, w_ada, b_ada, g_norm, w_in, w_dw, b_dw, g_ln_conv, b_ln_conv, w_pw, b_pw,
                w_out, g_final):
    B = x.shape[0]
    NS = S // 128
    f = np.float32
    w_in0 = np.asarray(w_in[0], f)
    w_c = np.ascontiguousarray(w_in0[:, 0:1536])
    w_pair = np.stack([np.concatenate([w_in0[:, 1536 + 128 * p:1536 + 128 * (p + 1)],
                                       w_in0[:, 2048 + 128 * p:2048 + 128 * (p + 1)],
                                       w_in0[:, 2560 + 128 * p:2560 + 128 * (p + 1)],
                                       w_in0[:, 3072 + 128 * p:3072 + 128 * (p + 1)]], axis=1) for p in range(4)])
    ident = np.eye(128, dtype=f)
    half = 8
    inv = (500000.0 ** (-(np.arange(half, dtype=np.float32) * 2.0) / 16.0)).astype(f)
    invf = np.ascontiguousarray(np.broadcast_to(np.tile(inv, 4)[None, :], (128, 32))).astype(f)
    pk = np.arange(128)
    trib = np.where(pk[:, None] <= pk[None, :], 0.0, NEGB).astype(f)
    onehot = (np.arange(16)[:, None] == (np.arange(S)[None, :] // 256)).astype(f)
    maps = []
    for b in range(B):
        vecs = np.concatenate([np.asarray(b_ada[0], f).reshape(24, 128), np.asarray(g_norm[0], f).reshape(8, 128),
                               np.asarray(c[b], f).reshape(8, 128), np.asarray(b_dw[0], f).reshape(4, 128),
                               np.asarray(g_ln_conv[0], f).reshape(4, 128), np.asarray(b_ln_conv[0], f).reshape(4, 128),
                               np.asarray(b_pw[0], f).reshape(4, 128)], axis=0)
        pos = np.ascontiguousarray(np.asarray(positions[b], np.int32).reshape(NS, 128).T)
        maps.append({
            "x": np.ascontiguousarray(np.asarray(x[b], f)), "vecs": np.ascontiguousarray(vecs),
            "b_ada": np.ascontiguousarray(np.asarray(b_ada[0], f)), "pos": pos,
            "w_ada": np.ascontiguousarray(np.asarray(w_ada[0], f)), "w_c": w_c, "w_pair": np.ascontiguousarray(w_pair),
            "w_dw": np.ascontiguousarray(np.asarray(w_dw[0], f)), "w_pw": np.ascontiguousarray(np.asarray(w_pw[0], f)),
            "w_out": np.ascontiguousarray(np.asarray(w_out[0], f)), "g_final": np.ascontiguousarray(np.asarray(g_final, f)),
            "ident": ident, "invf": invf, "trib": trib, "onehot": onehot,
        })
    return maps


_NC_CACHE = {}


def kernel(x, c, positions, w_ada, b_ada, g_norm, w_in, w_dw, b_dw, g_ln_conv, b_ln_conv, w_pw, b_pw, w_out, g_final):
    x = np.asarray(x)
    B, S, _ = x.shape
    maps = host_inputs(S, x, np.asarray(c), np.asarray(positions), np.asarray(w_ada), np.asarray(b_ada),
                       np.asarray(g_norm), np.asarray(w_in), np.asarray(w_dw), np.asarray(b_dw), np.asarray(g_ln_conv),
                       np.asarray(b_ln_conv), np.asarray(w_pw), np.asarray(b_pw), np.asarray(w_out), np.asarray(g_final))
    if S not in _NC_CACHE:
        _NC_CACHE[S] = build(S)
    nc = _NC_CACHE[S]
    res = run_bass_kernel_spmd(nc, maps, core_ids=list(range(B)))
    return np.stack([np.asarray(r["out"], np.float32) for r in res.results], axis=0)
```

```python
import numpy as np
import ml_dtypes
from contextlib import ExitStack
import concourse.bass as bass
import concourse.mybir as mybir
from concourse.bass_utils import run_bass_kernel_spmd

F32, BF16, I32 = mybir.dt.float32, mybir.dt.bfloat16, mybir.dt.int32
AF = mybir.ActivationFunctionType
ALU = mybir.AluOpType
AX = mybir.AxisListType

D = 1024
EPS = 1e-6
NEGB = -30000.0
ENGS = ["sp", "pe", "act", "dve", "pool"]
LAST_PROG = None


class Prog:
    def __init__(self):
        self.ops = {e: [] for e in ENGS}
        self.lastw = {}
        self.readers = {}
        self.dmacnt = {}
        self.bar = None

    def barrier(self, fn):
        deps = set()
        for e in ENGS:
            if self.ops[e]:
                o = self.ops[e][-1]
                deps.add(o["me"])
            for o in self.ops[e]:
                if o["dma"] is not None:
                    deps.add(o["me"])
        me = self.op("pool", fn)
        self.ops["pool"][-1]["deps"] |= deps
        self.bar = me
        return me

    def op(self, eng, fn, r=(), w=(), dma=None):
        idx = len(self.ops[eng])
        w = list(w) + [x for x in r if isinstance(x, str) and x.startswith("ps") and x not in w]
        deps = set()
        if self.bar is not None:
            deps.add(self.bar)
        for x in r:
            if x in self.lastw:
                deps.add(self.lastw[x])
        for x in w:
            if x in self.lastw:
                deps.add(self.lastw[x])
            for rd in self.readers.get(x, ()):
                deps.add(rd)
        if dma is not None:
            n = self.dmacnt.get(dma, 0) + 1
            self.dmacnt[dma] = n
            me = ("dma", dma, n)
        else:
            me = (eng, idx)
        deps.discard(me)
        self.ops[eng].append(dict(fn=fn, deps=deps, dma=dma, sig=False, me=me))
        for x in r:
            self.readers.setdefault(x, []).append(me)
        for x in w:
            self.lastw[x] = me
            self.readers[x] = []
        return me

    def emit(self, nc, final_waits):
        for e in ENGS:
            for o in self.ops[e]:
                for d in o["deps"]:
                    if d[0] != "dma" and not (d[0] == "pe" and e == "pe"):
                        self.ops[d[0]][d[1]]["sig"] = True
        cnt = {}
        for e in ENGS:
            c = 0
            arr = []
            for o in self.ops[e]:
                if o["sig"]:
                    c += 1
                arr.append(c)
            cnt[e] = arr
        with ExitStack() as st:
            esem = {e: st.enter_context(nc.semaphore("es_" + e)) for e in ENGS}
            dsem = {k: st.enter_context(nc.semaphore("ds_%d" % i))
                    for i, k in enumerate(self.dmacnt)}
            block = st.enter_context(nc.Block())

            def run(h, e):
                waited = {}
                for o in self.ops[e]:
                    need = {}
                    for d in o["deps"]:
                        if d[0] == "dma":
                            k = ("dma", d[1])
                            v = 16 * d[2]
                        else:
                            if d[0] == "pe" and e == "pe":
                                continue
                            k = d[0]
                            v = cnt[d[0]][d[1]]
                        need[k] = max(need.get(k, 0), v)
                    for k, v in need.items():
                        if waited.get(k, 0) >= v:
                            continue
                        waited[k] = v
                        h.wait_ge(dsem[k[1]] if isinstance(k, tuple) else esem[k], v)
                    ins = o["fn"](h)
                    if o["dma"] is not None:
                        ins.then_inc(dsem[o["dma"]], 16)
                    elif o["sig"]:
                        ins.then_inc(esem[e], 1)
                if e == "sp":
                    for k, n in final_waits.items():
                        h.wait_ge(dsem[k], 16 * n)

            @block.sync
            def _(h):
                run(h, "sp")

            @block.tensor
            def _(h):
                run(h, "pe")

            @block.scalar
            def _(h):
                run(h, "act")

            @block.vector
            def _(h):
                run(h, "dve")

            @block.gpsimd
            def _(h):
                run(h, "pool")


class _Cut(Exception):
    pass


def build(S, debug=False, upto=3, cut=0):
    try:
        return _build(S, debug, upto, cut)
    except _Cut as c:
        return c.args[0]


def _build(S, debug=False, upto=3, cut=0):
    NT, NS, NB = S // 512, S // 128, S // 256
    nc = bass.Bass("TRN2", target_bir_lowering=False)
    P = Prog()
    global LAST_PROG
    LAST_PROG = P

    def din(name, shape, dt=F32):
        return nc.dram_tensor(name, list(shape), dt, kind="ExternalInput").ap()

    x_d = din("x", [S, D])
    vecs_d = din("vecs", [56, 128])
    bada_d = din("b_ada", [3 * D])
    pos_d = din("pos", [128, NS], I32)
    wada_d = din("w_ada", [D, 3 * D])
    wc_d = din("w_c", [D, 1536])
    wpair_d = din("w_pair", [4, D, 512])
    wdw_d = din("w_dw", [31, 512])
    wpw_d = din("w_pw", [512, 512])
    wout_d = din("w_out", [D, D])
    gfin_d = din("g_final", [D])
    ident_d = din("ident", [128, 128])
    invf_d = din("invf", [128, 32])
    trib_d = din("trib", [128, 128])
    oneh_d = din("onehot", [16, S])
    out_d = nc.dram_tensor("out", [S, D], F32, kind="ExternalOutput").ap()
    skind = "ExternalOutput" if debug else "Internal"
    ycT_d = nc.dram_tensor("ycT", [512, S], BF16, kind=skind).ap()
    yT_d = nc.dram_tensor("yT", [512, S], BF16, kind=skind).ap()
    if debug:
        hT_dbg = nc.dram_tensor("hT_dbg", [128, 8 * S], BF16, kind="ExternalOutput").ap()

    def sb(name, shape, dt=F32):
        return nc.alloc_sbuf_tensor("s_" + name, list(shape), dt)

    ident = sb("ident", [128, 128])
    identb = sb("identb", [128, 128], BF16)
    onesb = sb("onesb", [128, 128], BF16)
    vT = sb("vT", [128, 56])
    sc = sb("sc", [128, 8])
    modc = sb("modc", [128, 16])
    Acol = sb("Acol", [128, 8])
    shc = sb("shc", [128, 8])
    gate_bc = sb("gate_bc", [128, D])
    gfin_bc = sb("gfin_bc", [128, D])
    cos4 = sb("cos4", [128, NS, 32])
    sin4 = sb("sin4", [128, NS, 32])
    tribb = sb("tribb", [128, 128], BF16)
    hT = sb("hT", [128, 8, S], BF16)
    ssq = sb("ssq", [128, NS])
    rstd = sb("rstd", [128, NS])
    ssq2 = sb("ssq2", [128, NS])
    rstd2 = sb("rstd2", [128, NS])
    ARW = 31600
    AR = sb("arena", [128, ARW])
    ps = [nc.alloc_psum_tensor("ps%d" % i, [128, 512], F32) for i in range(8)]

    class Arena:
        def __init__(self):
            self.off = 0

        def get(self, shape, dt=F32):
            n = int(np.prod(shape[1:]))
            words = (n * (4 if dt in (F32, I32) else 2) + 3) // 4
            words = (words + 7) // 8 * 8
            a = AR[:, self.off:self.off + words]
            self.off += words
            assert self.off <= ARW, (self.off, ARW)
            if dt != F32:
                a = a.bitcast(dt)
            a = a[:, 0:n]
            if len(shape) == 3:
                a = a.rearrange("p (a b) -> p a b", b=shape[2])
            elif len(shape) == 4:
                a = a.rearrange("p (a b c) -> p a b c", b=shape[2], c=shape[3])
            return a

    phase_no = [0]

    def AT():
        return ("arena",)

    def new_phase():
        P.barrier(lambda e: e.memset(dummy[:, 0:1], 0.0))
        return Arena()

    dummy = sb("dummy", [128, 8])
    AR_R = [AT()]

    A0 = Arena()
    vraw = A0.get([128, 128])
    onesf = A0.get([128, 128])
    wa = [A0.get([128, 3 * D]) for _ in range(2)]
    gbias = A0.get([128, D])
    posi = A0.get([128, NS], I32)
    posf = A0.get([128, NS])
    invf = A0.get([128, 32])
    ang = A0.get([128, NS, 32])
    ki = A0.get([128, NS, 32], I32)
    kf = A0.get([128, NS, 32])
    tribf = A0.get([128, 128])
    sgn = A0.get([128, 32])
    NXT = 6
    xt = [A0.get([128, D]) for _ in range(NXT)]
    NXN = 8
    xn = [A0.get([128, D]) for _ in range(NXN)]
    sq_junk = A0.get([128, D])
    sdv = A0.get([128, NS])
    scb2 = A0.get([128, 8, 128])
    wg = [A0.get([128, D]) for _ in range(4)]

    P.op("sp", lambda e: e.dma_start(out=ident[:, :], in_=ident_d[:, :]), w=["ident"], dma="c0")
    P.op("sp", lambda e: e.dma_start(out=vraw[0:56, :], in_=vecs_d[:, :]), r=AR_R, w=["vraw"], dma="c1")
    P.op("sp", lambda e: e.dma_start(out=posi, in_=pos_d[:, :]), r=AR_R, w=["posi"], dma="c2")
    P.op("sp", lambda e: e.dma_start(out=invf, in_=invf_d[:, :]), r=AR_R, w=["invf"], dma="c3")
    P.op("sp", lambda e: e.dma_start(out=tribf, in_=trib_d[:, :]), r=AR_R, w=["tribf"], dma="c4")
    P.op("sp", lambda e: e.dma_start(out=gfin_bc[:, :], in_=gfin_d.partition_broadcast(128)), w=["gfin"], dma="c5")
    P.op("sp", lambda e: e.dma_start(out=gbias, in_=bada_d[2 * D:3 * D].partition_broadcast(128)), r=AR_R, w=["gbias"], dma="c6")

    P.op("dve", lambda e: e.tensor_copy(out=identb[:, :], in_=ident[:, :]), r=["ident"], w=["identb"])
    P.op("dve", lambda e: e.tensor_copy(out=tribb[:, :], in_=tribf), r=["tribf"], w=["tribb"])
    P.op("pool", lambda e: e.memset(onesb[:, :], 1.0 / 512.0), w=["onesb"])
    P.op("pool", lambda e: e.memset(onesf, 1.0), r=AR_R, w=["onesf"])
    P.op("pool", lambda e: e.memset(ssq[:, :], 0.0), w=["ssq"])
    P.op("pool", lambda e: e.memset(ssq2[:, :], 0.0), w=["ssq2"])

    P.op("pe", lambda e: e.transpose(out=ps[0][:, 0:56], in_=vraw[0:56, :], identity=ident[0:56, 0:56]),
         r=["vraw", "ident"], w=["ps0"])
    P.op("dve", lambda e: e.tensor_copy(out=vT[:, :], in_=ps[0][:, 0:56]), r=["ps0"], w=["vT"])
    P.op("act", lambda e: e.activation(out=sc[:, :], in_=vT[:, 32:40], func=AF.Silu), r=["vT"], w=["sc"])
    for kc in range(8):
        P.op("dve", lambda e, kc=kc: e.tensor_scalar(out=scb2[:, kc, :], in0=onesf, scalar1=sc[:, kc:kc + 1],
                                                     scalar2=None, op0=ALU.mult),
             r=["onesf", "sc"], w=[("scb", kc)])

    TWO_PI = 2.0 * np.pi
    HI = 6.28125
    LO = TWO_PI - HI
    P.op("dve", lambda e: e.tensor_copy(out=posf, in_=posi), r=["posi"], w=["posf"])
    for st_ in range(NS):
        P.op("dve", lambda e, s=st_: e.tensor_scalar(out=ang[:, s, :], in0=invf, scalar1=posf[:, s:s + 1],
                                                     scalar2=None, op0=ALU.mult),
             r=["posf", "invf"], w=[("ang", st_)])
    ANG_ALL = [("ang", s) for s in range(NS)]

    def trig(dst, shift, tag):
        P.op("dve", lambda e: e.tensor_scalar(out=kf, in0=ang, scalar1=shift, scalar2=1.0 / TWO_PI,
                                              op0=ALU.add, op1=ALU.mult), r=ANG_ALL + ["cs_prev"], w=["kf"])
        P.op("dve", lambda e: e.tensor_copy(out=ki, in_=kf), r=["kf"], w=["ki"])
        P.op("dve", lambda e: e.tensor_copy(out=kf, in_=ki), r=["ki"], w=["kf"])
        P.op("dve", lambda e: e.scalar_tensor_tensor(out=dst[:, :, :], in0=kf, scalar=-HI, in1=ang,
                                                     op0=ALU.mult, op1=ALU.add), r=["kf"] + ANG_ALL, w=[tag])
        P.op("dve", lambda e: e.scalar_tensor_tensor(out=dst[:, :, :], in0=kf, scalar=-LO, in1=dst[:, :, :],
                                                     op0=ALU.mult, op1=ALU.add), r=["kf", tag], w=[tag])
        if shift != 0.0:
            P.op("dve", lambda e: e.tensor_scalar(out=dst[:, :, :], in0=dst[:, :, :], scalar1=shift, scalar2=None,
                                                  op0=ALU.add), r=[tag], w=[tag])
        P.op("dve", lambda e: e.tensor_scalar(out=dst[:, :, :], in0=dst[:, :, :], scalar1=3.14159, scalar2=-3.14159,
                                              op0=ALU.min, op1=ALU.max), r=[tag], w=[tag])
        P.op("act", lambda e: e.activation(out=dst[:, :, :], in_=dst[:, :, :], func=AF.Sin), r=[tag], w=[tag, "cs_prev"])

    trig(sin4, 0.0, "sin4")
    trig(cos4, np.pi / 2.0, "cos4")

    for kc in range(8):
        b = kc % 2
        P.op("act", lambda e, kc=kc, b=b: e.dma_start(out=wa[b][:, 0:2 * D], in_=wada_d[kc * 128:(kc + 1) * 128, 0:2 * D]),
             r=AR_R, w=[("wa", b)], dma=("wa", b))

        def f_cols(e, kc=kc, b=b):
            ins = None
            for m in range(16):
                ins = e.matmul(ps[0][:, kc * 16 + m:kc * 16 + m + 1], lhsT=wa[b][:, m * 128:(m + 1) * 128],
                               rhs=sc[:, kc:kc + 1], start=True, stop=True)
            return ins
        P.op("pe", f_cols, r=[("wa", b), "sc"], w=["ps0"])

    def gate_dma(kc):
        P.op("act", lambda e: e.dma_start(out=wg[kc % 4], in_=wada_d[kc * 128:(kc + 1) * 128, 2 * D:3 * D]),
             r=AR_R, w=[("wg", kc % 4)], dma=("wg", kc % 4))

    def gate_mm(kc):
        for half in range(2):
            P.op("pe", lambda e, half=half: e.matmul(
                ps[1 + half][:, :], lhsT=scb2[:, kc, :], rhs=wg[kc % 4][:, half * 512:(half + 1) * 512],
                start=(kc == 0), stop=(kc == 7)), r=[("wg", kc % 4), ("scb", kc)], w=["ps%d" % (1 + half)])

    gate_dma(0)
    gate_dma(1)
    P.op("dve", lambda e: e.tensor_reduce(out=modc[:, :], in_=ps[0][:, 0:128].rearrange("p (k m) -> p m k", m=16),
                                          axis=AX.X, op=ALU.add), r=["ps0"], w=["modc"])
    P.op("dve", lambda e: e.tensor_tensor(out=modc[:, :], in0=modc[:, :], in1=vT[:, 0:16], op=ALU.add),
         r=["modc", "vT"], w=["modc"])
    P.op("dve", lambda e: e.tensor_copy(out=shc[:, :], in_=modc[:, 0:8]), r=["modc"], w=["shc"])
    P.op("dve", lambda e: e.scalar_tensor_tensor(out=Acol[:, :], in0=modc[:, 8:16], scalar=1.0, in1=vT[:, 24:32],
                                                 op0=ALU.add, op1=ALU.mult), r=["modc", "vT"], w=["Acol"])

    _pre1 = Arena()
    wc = _pre1.get([128, 8, 1536], BF16)
    DEAD0 = ["vraw", "onesf", ("wa", 0), ("wa", 1)]
    for kc in range(8):
        for part in range(3):
            P.op("pool", lambda e, kc=kc, part=part: e.dma_start(
                out=wc[:, kc, part * 512:(part + 1) * 512], in_=wc_d[kc * 128:(kc + 1) * 128, part * 512:(part + 1) * 512]),
                r=AR_R,
                w=[("wc", kc, part)] + (DEAD0 if (kc, part) == (0, 0) else []), dma=("wc", (kc * 3 + part) % 4))

    def xstage1(tt):
        for s in range(4):
            st_ = tt * 4 + s
            xb_, nb_ = st_ % NXT, st_ % NXN
            P.op("sp", lambda e, st_=st_, xb_=xb_: e.dma_start(out=xt[xb_], in_=x_d[st_ * 128:(st_ + 1) * 128, :]),
                 r=AR_R, w=[("xt", xb_)], dma=("xt", xb_))
            P.op("act", lambda e, st_=st_, xb_=xb_: e.activation(out=sq_junk, in_=xt[xb_], func=AF.Square,
                                                                accum_out=ssq[:, st_:st_ + 1]),
                 r=[("xt", xb_), "ssq"], w=["sqj", ("ssq", st_)])
            P.op("act", lambda e, st_=st_: e.activation(out=sdv[:, st_:st_ + 1], in_=ssq[:, st_:st_ + 1], func=AF.Sqrt,
                                                       bias=EPS, scale=1.0 / D), r=[("ssq", st_)], w=[("sdv", st_)])
            P.op("dve", lambda e, st_=st_: e.reciprocal(out=rstd[:, st_:st_ + 1], in_=sdv[:, st_:st_ + 1]),
                 r=[("sdv", st_)], w=[("rstd", st_)])
            P.op("dve", lambda e, st_=st_, xb_=xb_, nb_=nb_: e.tensor_scalar(
                out=xn[nb_], in0=xt[xb_], scalar1=rstd[:, st_:st_ + 1], scalar2=None, op0=ALU.mult),
                r=[("xt", xb_), ("rstd", st_)], w=[("xn", nb_)])

    def xstage2(tt):
        for kc in range(8):
            pb = 3 + kc % 4

            def f_tr(e, kc=kc, pb=pb):
                ins = None
                for s in range(4):
                    nb_ = (tt * 4 + s) % NXN
                    ins = e.transpose(out=ps[pb][:, s * 128:(s + 1) * 128], in_=xn[nb_][:, kc * 128:(kc + 1) * 128],
                                      identity=ident[:, :])
                return ins
            P.op("pe", f_tr, r=[("xn", (tt * 4 + s) % NXN) for s in range(4)] + ["ident"], w=["ps%d" % pb])
            P.op("act", lambda e, kc=kc, pb=pb: e.activation(
                out=hT[:, kc, tt * 512:(tt + 1) * 512], in_=ps[pb][:, :], func=AF.Identity,
                bias=shc[:, kc:kc + 1], scale=Acol[:, kc:kc + 1]),
                r=["ps%d" % pb, "shc", "Acol"], w=[("hT", tt)])

    xstage1(0)
    GPT = 8 // NT if NT <= 8 else 1
    for tt in range(NT):
        if tt + 1 < NT:
            xstage1(tt + 1)
        xstage2(tt)
        for kc in range(tt * GPT, (tt + 1) * GPT):
            if kc + 2 < 8:
                gate_dma(kc + 2)
            gate_mm(kc)
    for half in range(2):
        P.op("dve", lambda e, half=half: e.tensor_tensor(
            out=gate_bc[:, half * 512:(half + 1) * 512], in0=ps[1 + half][:, :],
            in1=gbias[:, half * 512:(half + 1) * 512], op=ALU.add),
            r=["ps%d" % (1 + half), "gbias"], w=[("gate_bc", half)])
    if debug:
        P.op("sp", lambda e: e.dma_start(out=hT_dbg[:, :], in_=hT[:, :, :].rearrange("p a b -> p (a b)")),
             r=[("hT", t) for t in range(NT)], dma="dbg_hT")

    HT_R = lambda tt: [("hT", tt)]

    def ck(n):
        if cut == n:
            P.barrier(lambda e: e.memset(dummy[:, 0:1], 0.0))
            P.emit(nc, {})
            raise _Cut(nc)
    if upto == 0:
        P.emit(nc, {"dbg_hT": 1})
        return nc

    A1 = new_phase()
    wc_ = A1.get([128, 8, 1536], BF16)
    WC_R = [("wc", kc, part) for kc in range(8) for part in range(3)]
    wpw = A1.get([128, 4, 512], BF16)
    wdraw = A1.get([128, 512])
    wdT = A1.get([128, 4, 32])
    Dg = A1.get([128, 4, 31, 128], BF16)
    U = [A1.get([128, 4, 542], BF16) for _ in range(2)]
    sg = [A1.get([128, 512]) for _ in range(2)]
    gc = [A1.get([128, 4, 512], BF16) for _ in range(2)]
    v32 = A1.get([128, 4, 512])
    vb = A1.get([128, 4, 512], BF16)
    sqb = A1.get([128, 4, 512], BF16)
    m2 = A1.get([128, 512])
    var = A1.get([128, 512])
    rs = A1.get([128, 512])
    mean_sb = A1.get([128, 512])
    tz = [A1.get([128, 512]) for _ in range(2)]
    zb = A1.get([128, 4, 512], BF16)
    yc = [A1.get([128, 4, 512], BF16) for _ in range(2)]

    for kc in range(4):
        P.op("pool", lambda e, kc=kc: e.dma_start(out=wpw[:, kc, :], in_=wpw_d[kc * 128:(kc + 1) * 128, :]),
             r=AR_R, w=[("wpw", kc)], dma=("wpw", kc % 2))
    P.op("sp", lambda e: e.dma_start(out=wdraw[0:31, :], in_=wdw_d[:, :]), r=AR_R, w=["wdraw"], dma="wdraw")
    for c in range(4):
        P.op("pe", lambda e, c=c: e.transpose(out=ps[0][:, c * 32:c * 32 + 31], in_=wdraw[0:31, c * 128:(c + 1) * 128],
                                              identity=ident[0:31, 0:31]), r=["wdraw", "ident"], w=["ps0"])
    P.op("dve", lambda e: e.tensor_copy(out=wdT[:, :, 0:31], in_=ps[0][:, 0:128].rearrange("p (c t) -> p c t", t=32)[:, :, 0:31]),
         r=["ps0"], w=["wdT"])
    for c in range(4):
        for tap in range(31):
            if tap % 2 == 0:
                P.op("dve", lambda e, c=c, tap=tap: e.tensor_scalar(out=Dg[:, c, tap, :], in0=ident[:, :],
                                                                    scalar1=wdT[:, c, tap:tap + 1], scalar2=None,
                                                                    op0=ALU.mult),
                     r=["wdT", "ident"] + AR_R, w=[("Dg", c, 0)])
            else:
                P.op("act", lambda e, c=c, tap=tap: e.activation(out=Dg[:, c, tap, :], in_=ident[:, :], func=AF.Identity,
                                                                 scale=wdT[:, c, tap:tap + 1]),
                     r=["wdT", "ident"] + AR_R, w=[("Dg", c, 1)])
    P.op("pool", lambda e: e.memset(U[0][:, :, 0:30], 0.0), r=AR_R, w=[("Uh", 0)])
    ck(1)

    def cA(tt, cs):
        ub = tt % 2
        tsl = slice(tt * 512, (tt + 1) * 512)
        gcb = gc[tt % 2]
        for c in cs:
            pbase = 0 if c % 2 == 0 else 3
            for gi, goff in enumerate((512, 0, 1024)):
                def f_in(e, c=c, goff=goff, pb=pbase + gi):
                    ins = None
                    for kc in range(8):
                        ins = e.matmul(ps[pb][:, :], lhsT=wc[:, kc, goff + c * 128:goff + (c + 1) * 128],
                                       rhs=hT[:, kc, tsl], start=(kc == 0), stop=(kc == 7))
                    return ins
                P.op("pe", f_in, r=WC_R + HT_R(tt), w=["ps%d" % (pbase + gi)])
            sgb = c % 2
            P.op("act", lambda e, pb=pbase, sgb=sgb: e.activation(out=sg[sgb], in_=ps[pb][:, :], func=AF.Sigmoid),
                 r=["ps%d" % pbase] + AR_R, w=[("sg", sgb)])
            P.op("dve", lambda e, pb=pbase + 1, sgb=sgb, c=c: e.tensor_tensor(
                out=U[ub][:, c, 30:542], in0=ps[pb][:, :], in1=sg[sgb], op=ALU.mult),
                r=["ps%d" % (pbase + 1), ("sg", sgb)], w=[("U", ub, c)])
            P.op("act", lambda e, pb=pbase + 2, c=c: e.activation(out=gcb[:, c, :], in_=ps[pb][:, :], func=AF.Sigmoid),
                 r=["ps%d" % (pbase + 2)], w=[("gc", tt % 2, c)])
            P.op("dve", lambda e, pb=pbase + 2, c=c: e.tensor_tensor(
                out=gcb[:, c, :], in0=ps[pb][:, :], in1=gcb[:, c, :], op=ALU.mult),
                r=["ps%d" % (pbase + 2), ("gc", tt % 2, c)], w=[("gc", tt % 2, c)])
        if tt == NT - 1 and 3 in cs:
            _pre2 = Arena()
            wp_pre = [_pre2.get([128, 8, 512], BF16) for _ in range(2)]
            for p_ in range(2):
                for kc in range(8):
                    P.op("pool", lambda e, p_=p_, kc=kc: e.dma_start(out=wp_pre[p_][:, kc, :],
                                                                  in_=wpair_d[p_, kc * 128:(kc + 1) * 128, :]),
                         r=AR_R,
                         w=[("wp", p_, kc)] + (WC_R if (p_, kc) == (0, 0) else []), dma=("wp", p_, kc % 2))

    def cB(tt):
        ub = tt % 2
        P.op("pool", lambda e: e.tensor_copy(out=U[1 - ub][:, :, 0:30], in_=U[ub][:, :, 512:542]),
             r=[("U", ub, c) for c in range(4)] + AR_R, w=[("Uh", 1 - ub)])

    def cC(tt):
        ub = tt % 2
        for c in range(4):
            pb = 6 + c % 2

            def f_cv(e, c=c, pb=pb):
                ins = None
                for tap in range(31):
                    ins = e.matmul(ps[pb][:, :], lhsT=Dg[:, c, tap, :], rhs=U[ub][:, c, tap:tap + 512],
                                   start=(tap == 0), stop=(tap == 30))
                return ins
            P.op("pe", f_cv, r=[("Dg", c, 0), ("Dg", c, 1), ("U", ub, c), ("Uh", ub)], w=["ps%d" % pb])
            P.op("dve", lambda e, c=c, pb=pb: e.tensor_scalar(out=v32[:, c, :], in0=ps[pb][:, :],
                                                             scalar1=vT[:, 40 + c:41 + c], scalar2=None, op0=ALU.add),
                 r=["ps%d" % pb, "vT"], w=[("v32", c)])
            P.op("act", lambda e, c=c, pb=pb: e.activation(out=vb[:, c, :], in_=ps[pb][:, :], func=AF.Identity,
                                                          bias=vT[:, 40 + c:41 + c], scale=1.0),
                 r=["ps%d" % pb, "vT"], w=[("vb", c)])
            P.op("act", lambda e, c=c, pb=pb: e.activation(out=sqb[:, c, :], in_=ps[pb][:, :], func=AF.Square,
                                                          bias=vT[:, 40 + c:41 + c], scale=1.0),
                 r=["ps%d" % pb, "vT"], w=[("sqb", c)])

    def cD(tt):
        def f_st(e, src, pb):
            ins = None
            for c in range(4):
                ins = e.matmul(ps[pb][:, :], lhsT=onesb[:, :], rhs=src[:, c, :], start=(c == 0), stop=(c == 3))
            return ins
        P.op("pe", lambda e: f_st(e, vb, 0), r=[("vb", c) for c in range(4)] + ["onesb"], w=["ps0"])
        P.op("pe", lambda e: f_st(e, sqb, 1), r=[("sqb", c) for c in range(4)] + ["onesb"], w=["ps1"])
        P.op("act", lambda e: e.activation(out=m2, in_=ps[0][:, :], func=AF.Square), r=["ps0"] + AR_R, w=["m2"])
        P.op("act", lambda e: e.activation(out=mean_sb, in_=ps[0][:, :], func=AF.Identity), r=["ps0"], w=["mean_sb"])
        P.op("dve", lambda e: e.tensor_tensor(out=var, in0=ps[1][:, :], in1=m2, op=ALU.subtract),
             r=["ps1", "m2"], w=["var"])
        P.op("act", lambda e: e.activation(out=var, in_=var, func=AF.Sqrt, bias=EPS, scale=1.0), r=["var"], w=["var"])
        P.op("dve", lambda e: e.reciprocal(out=rs, in_=var), r=["var"], w=["rs"])

    def cE(tt):
        for c in range(4):
            tb = c % 2
            P.op("pool", lambda e, c=c, tb=tb: e.tensor_tensor(out=tz[tb], in0=v32[:, c, :], in1=mean_sb, op=ALU.subtract),
                 r=[("v32", c), "mean_sb"] + AR_R, w=[("tz", tb)])
            P.op("dve", lambda e, tb=tb: e.tensor_tensor(out=tz[tb], in0=tz[tb], in1=rs, op=ALU.mult),
                 r=[("tz", tb), "rs"], w=[("tz", tb)])
            P.op("act", lambda e, c=c, tb=tb: e.activation(out=zb[:, c, :], in_=tz[tb], func=AF.Silu,
                                                          bias=vT[:, 48 + c:49 + c], scale=vT[:, 44 + c:45 + c]),
                 r=[("tz", tb), "vT"], w=[("zb", c)])

    def cF(tt):
        yb = tt % 2
        tsl = slice(tt * 512, (tt + 1) * 512)
        gcb = gc[tt % 2]
        for oc in range(4):
            pb = 3 + oc % 3

            def f_pw(e, oc=oc, pb=pb):
                ins = None
                for k4 in range(4):
                    ins = e.matmul(ps[pb][:, :], lhsT=wpw[:, k4, oc * 128:(oc + 1) * 128], rhs=zb[:, k4, :],
                                   start=(k4 == 0), stop=(k4 == 3))
                return ins
            P.op("pe", f_pw, r=[("wpw", k4) for k4 in range(4)] + [("zb", c) for c in range(4)], w=["ps%d" % pb])
            P.op("dve", lambda e, oc=oc, pb=pb: e.scalar_tensor_tensor(
                out=yc[yb][:, oc, :], in0=ps[pb][:, :], scalar=vT[:, 52 + oc:53 + oc], in1=gcb[:, oc, :],
                op0=ALU.add, op1=ALU.mult), r=["ps%d" % pb, "vT", ("gc", tt % 2, oc)], w=[("yc", yb)])
        P.op("sp", lambda e: e.dma_start(out=ycT_d.rearrange("(c p) t -> p c t", p=128)[:, :, tsl], in_=yc[yb]),
             r=[("yc", yb)] + AR_R, w=[("ycT", tt)], dma=("yc", yb))

    cA(0, [0, 1, 2, 3])
    cB(0)
    cC(0)
    cD(0)
    for tt in range(NT):
        nxt = tt + 1 < NT
        if nxt:
            cA(tt + 1, [0, 1])
        cE(tt)
        if nxt:
            cA(tt + 1, [2, 3])
            cB(tt + 1)
        cF(tt)
        if nxt:
            cC(tt + 1)
            cD(tt + 1)

    if upto == 1:
        P.emit(nc, {("yc", 0): P.dmacnt[("yc", 0)], ("yc", 1): P.dmacnt[("yc", 1)]})
        return nc
    A2 = new_phase()
    wp = [A2.get([128, 8, 512], BF16) for _ in range(2)]
    KAs = [[A2.get([128, S], BF16) for _ in range(2)] for _ in range(2)]
    VAs = [A2.get([128, NS, 2, 128], BF16) for _ in range(2)]
    NQ = 5
    QAs = [[A2.get([128, 512], BF16) for _ in range(2)] for _ in range(NQ)]
    GAs = [A2.get([128, 512], BF16) for _ in range(NQ)]
    GAtmp = A2.get([128, 512])
    QK4 = [A2.get([128, 4, 256])]
    rtmp = [A2.get([128, 4, 4, 8]) for _ in range(4)]
    GSall = A2.get([128, NT, 8, 16])
    Lt = [A2.get([128, 512])]
    kms = [A2.get([128, 16]) for _ in range(2)]
    kmh = [A2.get([128, 16], BF16) for _ in range(2)]
    kml = [A2.get([128, 16], BF16) for _ in range(2)]
    m8 = A2.get([128, 8, 8])
    tmpm = A2.get([128, 8, 16])
    MB = A2.get([128, 4, 128], BF16)
    NPT = 5
    PT = [A2.get([128, 512], BF16) for _ in range(NPT)]
    Rt = [A2.get([128, 512])]
    Tt = A2.get([128, 512])
    Yb = [A2.get([128, 512], BF16) for _ in range(2)]

    for ks in range(2):
        for hh in range(2):
            P.op("act", lambda e, ks=ks, hh=hh: e.memzero(KAs[ks][hh][:, :]), r=AR_R, w=[("KAinit", ks, hh)])
            P.op("pool", lambda e, ks=ks, hh=hh: e.dma_start(out=KAs[ks][hh][64:80, :], in_=oneh_d[:, :]),
                 r=[("KAinit", ks, hh)], w=[("KAm", ks, hh)], dma=("oneh", ks, hh))
        P.op("pool", lambda e, ks=ks: e.memset(VAs[ks][:, :, :, 64:128], 1.0), r=AR_R, w=[("VAones", ks)])
    for hh in range(2):
        for qi in range(NQ):
            P.op("act", lambda e, hh=hh, qi=qi: e.memzero(QAs[qi][hh]), r=AR_R,
                 w=[("QAq", qi, hh), ("QAm", qi, hh)])
        P.op("pool", lambda e, hh=hh: e.memset(kmh[hh], 0.0), r=AR_R, w=[("kmh", hh)])
        P.op("pool", lambda e, hh=hh: e.memset(kml[hh], 0.0), r=AR_R, w=[("kml", hh)])
        P.op("pool", lambda e, hh=hh: e.memset(kms[hh], 0.0), r=AR_R, w=[("kms", hh)])
    P.op("pool", lambda e: e.memset(MB, 0.0), r=AR_R, w=["MB"])
    P.op("pool", lambda e: e.memset(GSall, -1e30), r=AR_R, w=["GSall"])
    for tt in range(NT):
        for half_, b_ in ((0, 2 * tt), (1, 2 * tt + 1)):
            P.op("pool", lambda e, tt=tt, half_=half_, b_=b_: e.memset(GSall[:, tt, half_ * 4:half_ * 4 + 4, b_:b_ + 1], 1e30),
                 r=AR_R, w=["GSall"])

    pt_ctr = [0]
    WP_R = [[("wp", wb_, kc) for kc in range(8)] for wb_ in range(2)]

    def prep(p, tt):
        wb = p % 2
        ks = p % 2
        KA = KAs[ks]
        VA = VAs[ks]
        par = (p * NT + tt) % NQ
        QA = QAs
        GA = GAs
        tsl = slice(tt * 512, (tt + 1) * 512)
        qk4 = QK4[0]

        for s in range(4):
            st_ = tt * 4 + s

            tb = 1 - s % 2

            def f_tok(e, st_=st_, tb=tb):
                ins = None
                for kc in range(8):
                    ins = e.matmul(ps[tb][:, 0:384], lhsT=hT[:, kc, st_ * 128:(st_ + 1) * 128],
                                   rhs=wp[wb][:, kc, 0:384], start=(kc == 0), stop=(kc == 7))
                return ins
            P.op("pe", f_tok, r=WP_R[wb] + HT_R(tt), w=["ps%d" % tb])
            P.op("dve", lambda e, s=s, tb=tb: e.tensor_copy(out=qk4[:, s, :], in_=ps[tb][:, 0:256]),
                 r=["ps%d" % tb] + AR_R, w=[("qk4", s)])
            P.op("dve", lambda e, st_=st_, tb=tb: e.tensor_copy(
                out=VA[:, st_, :, 0:64], in_=ps[tb][:, 256:384].rearrange("p (h d) -> p h d", d=64)),
                r=["ps%d" % tb] + AR_R, w=[("VA", ks, st_)])
            yield
        def f_ga(e):
            ins = None
            for kc in range(8):
                ins = e.matmul(ps[1][:, :], lhsT=wp[wb][:, kc, 384:512], rhs=hT[:, kc, tsl],
                               start=(kc == 0), stop=(kc == 7))
            return ins
        P.op("pe", f_ga, r=WP_R[wb] + HT_R(tt), w=["ps1"])
        P.op("act", lambda e: e.activation(out=GAtmp, in_=ps[1][:, :], func=AF.Exp, scale=-1.0),
             r=["ps1"] + AR_R, w=["GAtmp"])
        P.op("act", lambda e: e.activation(out=GAtmp, in_=GAtmp, func=AF.Ln, bias=1.0, scale=1.0),
             r=["GAtmp"], w=["GAtmp"])
        P.op("act", lambda e: e.activation(out=GAtmp, in_=GAtmp, func=AF.Exp, scale=-1.0),
             r=["GAtmp"], w=["GAtmp"])
        P.op("dve", lambda e: e.tensor_tensor(out=GA[par], in0=ps[1][:, :], in1=GAtmp, op=ALU.mult),
             r=["ps1", "GAtmp"], w=[("GA", par)])
        yield
        Q4 = qk4.rearrange("p s (g d) -> p s g d", d=64)
        c4 = cos4[:, tt * 4:(tt + 1) * 4, :].rearrange("p s (g d) -> p s g d", d=8)
        s4 = sin4[:, tt * 4:(tt + 1) * 4, :].rearrange("p s (g d) -> p s g d", d=8)
        QK_ALL = [("qk4", s) for s in range(4)]
        P.op("dve", lambda e: e.tensor_tensor(out=rtmp[0], in0=Q4[:, :, :, 0:8], in1=c4, op=ALU.mult),
             r=QK_ALL + ["cos4"], w=[("rtmp", 0)])
        P.op("dve", lambda e: e.tensor_tensor(out=rtmp[1], in0=Q4[:, :, :, 8:16], in1=s4, op=ALU.mult),
             r=QK_ALL + ["sin4"], w=[("rtmp", 1)])
        P.op("dve", lambda e: e.tensor_tensor(out=rtmp[2], in0=Q4[:, :, :, 8:16], in1=c4, op=ALU.mult),
             r=QK_ALL + ["cos4"], w=[("rtmp", 2)])
        P.op("dve", lambda e: e.tensor_tensor(out=rtmp[3], in0=Q4[:, :, :, 0:8], in1=s4, op=ALU.mult),
             r=QK_ALL + ["sin4"], w=[("rtmp", 3)])
        P.op("dve", lambda e: e.tensor_tensor(out=Q4[:, :, :, 0:8], in0=rtmp[0], in1=rtmp[1], op=ALU.subtract),
             r=[("rtmp", 0), ("rtmp", 1)], w=QK_ALL)
        P.op("dve", lambda e: e.tensor_tensor(out=Q4[:, :, :, 8:16], in0=rtmp[2], in1=rtmp[3], op=ALU.add),
             r=[("rtmp", 2), ("rtmp", 3)], w=QK_ALL)
        yield
        yield
        for s in range(4):
            h2 = s // 2
            qo = (s % 2) * 128
            tbk = s // 2
            P.op("pe", lambda e, s=s, qo=qo, tbk=tbk: e.transpose(out=ps[tbk][:, qo:qo + 128], in_=qk4[:, s, 0:128],
                                                         identity=ident[:, :]),
                 r=[("qk4", s), "ident"], w=["ps%d" % tbk])
            P.op("pe", lambda e, s=s, qo=qo, tbk=tbk: e.transpose(out=ps[tbk][:, 256 + qo:256 + qo + 128], in_=qk4[:, s, 128:256],
                                                         identity=ident[:, :]),
                 r=[("qk4", s), "ident"], w=["ps%d" % tbk])
            if s % 2 == 1:
                hsl = slice(h2 * 256, (h2 + 1) * 256)
                ksl = slice(tt * 512 + h2 * 256, tt * 512 + (h2 + 1) * 256)
                P.op("dve", lambda e, hsl=hsl, tbk=tbk: e.tensor_copy(out=QA[par][0][0:64, hsl], in_=ps[tbk][0:64, 0:256]),
                     r=["ps%d" % tbk], w=[("QAq", par, 0)])
                P.op("dve", lambda e, hsl=hsl, tbk=tbk: e.tensor_copy(out=QA[par][1][0:64, hsl], in_=ps[tbk][64:128, 0:256]),
                     r=["ps%d" % tbk], w=[("QAq", par, 1)])
                P.op("dve", lambda e, ksl=ksl, tbk=tbk: e.tensor_copy(out=KA[0][0:64, ksl], in_=ps[tbk][0:64, 256:512]),
                     r=["ps%d" % tbk, ("KAinit", ks, 0)], w=[("KA", ks, 0, tt)])
                P.op("dve", lambda e, ksl=ksl, tbk=tbk: e.tensor_copy(out=KA[1][0:64, ksl], in_=ps[tbk][64:128, 256:512]),
                     r=["ps%d" % tbk, ("KAinit", ks, 1)], w=[("KA", ks, 1, tt)])
                yield
        bsl = slice(2 * tt, 2 * tt + 2)
        for hh in range(2):
            P.op("dve", lambda e, hh=hh: e.tensor_reduce(
                out=kms[hh][0:64, bsl], in_=KA[hh][0:64, tsl].rearrange("p (n k) -> p n k", k=256),
                axis=AX.X, op=ALU.add), r=[("KA", ks, hh, tt)], w=[("kms", hh)])
            P.op("dve", lambda e, hh=hh: e.tensor_scalar(out=kmh[hh][0:64, bsl], in0=kms[hh][0:64, bsl],
                                                        scalar1=1.0 / 256.0, scalar2=None, op0=ALU.mult),
                 r=[("kms", hh)], w=[("kmh", hh)])
            P.op("dve", lambda e, hh=hh: e.scalar_tensor_tensor(out=kml[hh][0:64, bsl], in0=kms[hh][0:64, bsl],
                                                               scalar=1.0 / 256.0, in1=kmh[hh][0:64, bsl],
                                                               op0=ALU.mult, op1=ALU.subtract),
                 r=[("kms", hh), ("kmh", hh)], w=[("kml", hh)])
        yield
        b0, b1 = 2 * tt, 2 * tt + 1
        if b0 >= 4:
            def f_gate(e):
                ins = None
                for s in range(4):
                    for hh in range(2):
                        j = s * 2 + hh
                        e.matmul(ps[0][:, j * 16:(j + 1) * 16], lhsT=QA[par][hh][0:80, s * 128:(s + 1) * 128],
                                 rhs=kmh[hh][0:80, :], start=True, stop=False)
                        ins = e.matmul(ps[0][:, j * 16:(j + 1) * 16], lhsT=QA[par][hh][0:80, s * 128:(s + 1) * 128],
                                       rhs=kml[hh][0:80, :], start=False, stop=True)
                return ins
            P.op("pe", f_gate, r=[("QAq", par, 0), ("QAq", par, 1), ("kmh", 0), ("kmh", 1), ("kml", 0), ("kml", 1)],
                 w=["ps0"])
            gsb = GSall[:, tt]
            G3 = ps[0][:, 0:128].rearrange("p (j n) -> p j n", n=16)
            P.op("dve", lambda e: e.tensor_copy(out=gsb[:, 0:4, 0:b0], in_=G3[:, 0:4, 0:b0]),
                 r=["ps0", "GSall"], w=[("gsb", tt)])
            P.op("dve", lambda e: e.tensor_copy(out=gsb[:, 4:8, 0:b1], in_=G3[:, 4:8, 0:b1]),
                 r=["ps0", "GSall"], w=[("gsb", tt)])
            for j in range(8):
                P.op("dve", lambda e, j=j: e.max(out=m8[:, j, :], in_=gsb[:, j, :]), r=[("gsb", tt)], w=[("m8", j)])
            P.op("dve", lambda e: e.tensor_tensor(out=tmpm, in0=gsb, in1=m8[:, :, 3:4].to_broadcast([128, 8, 16]),
                                                  op=ALU.is_lt),
                 r=[("gsb", tt)] + [("m8", j) for j in range(8)], w=[("tmpm", j) for j in range(8)])
            MBv = MB[:, :, 64:128].rearrange("p s (h c) -> p s h c", c=32)[:, :, :, 0:16]
            P.op("dve", lambda e: e.tensor_scalar(out=MBv, in0=tmpm.rearrange("p (s h) n -> p s h n", h=2),
                                                  scalar1=NEGB, scalar2=None, op0=ALU.mult),
                 r=[("tmpm", j) for j in range(8)], w=["MB"])
        else:
            P.op("dve", lambda e: e.memset(MB[:, :, 64:128], 0.0), r=AR_R, w=["MB"])
        yield
        yield
        yield
        def f_mk(e):
            ins = None
            for s in range(4):
                ins = e.matmul(ps[1][:, s * 128:(s + 1) * 128], lhsT=MB[:, s, :], rhs=identb[:, :],
                               start=True, stop=True)
            return ins
        P.op("pe", f_mk, r=["MB", "identb"], w=["ps1"])
        P.op("dve", lambda e: e.tensor_copy(out=QA[par][0][64:80, :], in_=ps[1][64:80, :]),
             r=["ps1"], w=[("QAm", par, 0)])
        P.op("dve", lambda e: e.tensor_copy(out=QA[par][1][64:80, :], in_=ps[1][96:112, :]),
             r=["ps1"], w=[("QAm", par, 1)])
        yield

    def attn(p, tt):
        ks = p % 2
        KA = KAs[ks]
        VA = VAs[ks]
        par = (p * NT + tt) % NQ
        QA = QAs
        GA = GAs
        tsl = slice(tt * 512, (tt + 1) * 512)
        nkt = 4 * tt + 4
        SBK = [2, 3, 4, 5]
        LAG = 4
        pend = []

        def emit_pv(info):
            hh, kt, c0, pti = info
            po = 6 + hh
            P.op("pe", lambda e: e.matmul(ps[po][:, c0:512], lhsT=VA[:, kt, hh, :], rhs=PT[pti][:, c0:512],
                                          start=(kt == 0), stop=(kt == nkt - 1)),
                 r=[("VA", ks, kt), ("VAones", ks), ("PT", pti)], w=["ps%d" % po])
            if kt == nkt - 1:
                rb = 0
                rows = slice(hh * 64, hh * 64 + 64)
                P.op("act", lambda e: e.activation(out=Lt[rb][64:128, :], in_=ps[po][64:128, :], func=AF.Ln),
                     r=["ps%d" % po] + AR_R, w=[("Lt", rb)])
                P.op("act", lambda e: e.activation(out=Rt[rb][64:128, :], in_=Lt[rb][64:128, :], func=AF.Exp, scale=-1.0),
                     r=[("Lt", rb)], w=[("Rt", rb)])
                P.op("dve", lambda e: e.tensor_tensor(out=Tt[rows, :], in0=ps[po][0:64, :], in1=Rt[rb][64:128, :],
                                                      op=ALU.mult),
                     r=["ps%d" % po, ("Rt", rb)], w=[("Tt", hh)])
                P.op("pool", lambda e: e.tensor_tensor(out=Yb[tt % 2][rows, :], in0=Tt[rows, :], in1=GA[par][rows, :],
                                                       op=ALU.mult),
                     r=[("Tt", hh), ("GA", par)] + AR_R, w=[("Yb", tt % 2)])

        steps = [(hh, kt) for hh in range(2) for kt in range(nkt)]
        for i, (hh, kt) in enumerate(steps):
            j = kt - 4 * tt
            c0 = max(0, j) * 128
            sbk = SBK[i % len(SBK)]
            pti = pt_ctr[0] % NPT
            pt_ctr[0] += 1

            def f_s(e, hh=hh, kt=kt, j=j, c0=c0, sbk=sbk):
                ins = e.matmul(ps[sbk][:, c0:512], lhsT=KA[hh][0:80, kt * 128:(kt + 1) * 128],
                               rhs=QA[par][hh][0:80, c0:512], start=True, stop=(j < 0))
                if j >= 0:
                    ins = e.matmul(ps[sbk][:, c0:c0 + 128], lhsT=identb[:, :], rhs=tribb[:, :],
                                   start=False, stop=True)
                return ins
            P.op("pe", f_s, r=[("KA", ks, hh, kt // 4), ("KAm", ks, hh), ("QAq", par, hh), ("QAm", par, hh),
                               "identb", "tribb"], w=["ps%d" % sbk])
            P.op("act", lambda e, sbk=sbk, c0=c0, pti=pti: e.activation(
                out=PT[pti][:, c0:512], in_=ps[sbk][:, c0:512], func=AF.Exp, scale=0.125),
                r=["ps%d" % sbk] + AR_R, w=[("PT", pti)])
            pend.append((hh, kt, c0, pti))
            if len(pend) > LAG:
                emit_pv(pend.pop(0))
            yield
        while pend:
            emit_pv(pend.pop(0))
        P.op("sp", lambda e: e.dma_start(out=yT_d[p * 128:(p + 1) * 128, tsl], in_=Yb[tt % 2]),
             r=[("Yb", tt % 2)] + AR_R, w=[("yT", p, tt)], dma=("Yb", tt % 2))

    def interleave(ga, gb, na, nb):
        done_b = 0
        for i, _ in enumerate(ga):
            want = ((i + 1) * nb + na - 1) // na
            while done_b < want:
                try:
                    next(gb)
                except StopIteration:
                    done_b = 10 ** 9
                    break
                done_b += 1
        for _ in gb:
            pass

    NPREP = 15

    def load_wp(p):
        wb = p % 2
        for kc in range(8):
            P.op("pool", lambda e, kc=kc: e.dma_start(out=wp[wb][:, kc, :], in_=wpair_d[p, kc * 128:(kc + 1) * 128, :]),
                 r=AR_R, w=[("wp", wb, kc)], dma=("wp", wb, kc % 2))

    G = 4 * NT
    nsteps = [2 * (4 * tt + 4) for tt in range(NT)]
    SP = sum(nsteps)
    Wn = SP / NT
    Cn = [sum(nsteps[:tt]) for tt in range(NT)]
    lead = max((n + 1) * Wn - Cn[n] for n in range(NT)) + 2
    prep_pos = []
    for g in range(G):
        p, tt = g // NT, g % NT
        start = p * SP + tt * Wn - lead
        for k in range(NPREP + 1):
            prep_pos.append((start + k * Wn / (NPREP + 1), g))
    preps = {}
    pi = [0]
    cur_g = [0]

    def advance_prep(upto_pos):
        while pi[0] < len(prep_pos) and prep_pos[pi[0]][0] <= upto_pos:
            g = prep_pos[pi[0]][1]
            if g - cur_g[0] > NQ - 2:
                break
            if g not in preps:
                if g % NT == 0 and g // NT >= 2:
                    load_wp(g // NT)
                preps[g] = prep(g // NT, g % NT)
            try:
                next(preps[g])
            except StopIteration:
                pass
            pi[0] += 1

    def finish_prep(g):
        while pi[0] < len(prep_pos) and prep_pos[pi[0]][1] <= g:
            gg = prep_pos[pi[0]][1]
            if gg not in preps:
                if gg % NT == 0 and gg // NT >= 2:
                    load_wp(gg // NT)
                preps[gg] = prep(gg // NT, gg % NT)
            try:
                next(preps[gg])
            except StopIteration:
                pass
            pi[0] += 1
        if g in preps:
            for _ in preps[g]:
                pass

    pos = 0
    _pre3 = Arena()
    wo = _pre3.get([128, 8, D], BF16)
    YT_pre = [_pre3.get([128, 8, 512], BF16) for _ in range(2)]
    yt_pref = set()
    for g in range(G):
        cur_g[0] = g
        finish_prep(g)
        if g == G - 1:
            for kc in range(8):
                for half in range(2):
                    P.op("pool", lambda e, kc=kc, half=half: e.dma_start(
                        out=wo[:, kc, half * 512:(half + 1) * 512],
                        in_=wout_d[kc * 128:(kc + 1) * 128, half * 512:(half + 1) * 512]),
                        r=AR_R,
                        w=[("wo", kc, half)] + (WP_R[0] + WP_R[1] if (kc, half) == (0, 0) else []),
                        dma=("wo", (kc * 2 + half) % 4))
        if g == G - 1:
            for t0 in range(min(2, NT - 1)):
                yb0 = t0 % 2
                tsl0 = slice(t0 * 512, (t0 + 1) * 512)
                dead = [("KA", 0, yb0, t) for t in range(NT)] + [("KAm", 0, yb0), ("KAinit", 0, yb0)]
                P.op("sp", lambda e, yb0=yb0, tsl0=tsl0: e.dma_start(
                    out=YT_pre[yb0][:, 0:4, :], in_=ycT_d.rearrange("(c p) t -> p c t", p=128)[:, :, tsl0]),
                    r=[("ycT", t0)] + AR_R, w=[("YTa", yb0)] + dead, dma=("YTa", yb0))
                P.op("sp", lambda e, yb0=yb0, tsl0=tsl0: e.dma_start(
                    out=YT_pre[yb0][:, 4:8, :], in_=yT_d.rearrange("(c p) t -> p c t", p=128)[:, :, tsl0]),
                    r=[("yT", p_, t0) for p_ in range(4)] + AR_R, w=[("YTb", yb0)], dma=("YTb", yb0))
                yt_pref.add(t0)
        for _ in attn(g // NT, g % NT):
            pos += 1
            advance_prep(pos)
    for kc in range(8):
        eng = "dve" if kc % 2 == 0 else "pool"
        WO_ALL = [("wo", k_, h_) for k_ in range(8) for h_ in range(2)]
        P.op(eng, lambda e, kc=kc: e.tensor_tensor(out=wo[:, kc, :], in0=wo[:, kc, :], in1=gate_bc[:, :], op=ALU.mult),
             r=[("gate_bc", 0), ("gate_bc", 1)] + AR_R + (WO_ALL if kc == 0 else ["wo_ready"]),
             w=[("wo", kc, 0), ("wo", kc, 1)] + (["wo_ready"] + WO_ALL if kc == 0 else []))
    if upto == 2:
        P.emit(nc, {("Yb", 0): P.dmacnt[("Yb", 0)], ("Yb", 1): P.dmacnt[("Yb", 1)]})
        return nc
    A3 = new_phase()
    wo_ = A3.get([128, 8, D], BF16)
    YT = [A3.get([128, 8, 512], BF16) for _ in range(2)]
    NX3 = 4
    xt3 = [A3.get([128, D]) for _ in range(NX3)]
    NOO = 3
    oo = [A3.get([128, D]) for _ in range(NOO)]
    sq3 = A3.get([128, D])
    sd3 = A3.get([128, NS])
    WO_R = [("wo", kc, h_) for kc in range(8) for h_ in range(2)]
    final_waits = {}

    def load_YT(tt):
        if tt in yt_pref:
            return
        yb = tt % 2
        tsl = slice(tt * 512, (tt + 1) * 512)
        P.op("sp", lambda e: e.dma_start(out=YT[yb][:, 0:4, :], in_=ycT_d.rearrange("(c p) t -> p c t", p=128)[:, :, tsl]),
             r=[("ycT", tt)] + AR_R, w=[("YTa", yb)], dma=("YTa", yb))
        P.op("sp", lambda e: e.dma_start(out=YT[yb][:, 4:8, :], in_=yT_d.rearrange("(c p) t -> p c t", p=128)[:, :, tsl]),
             r=[("yT", p, tt) for p in range(4)] + AR_R, w=[("YTb", yb)], dma=("YTb", yb))

    NRR = 4
    rr = [A3.get([128, D]) for _ in range(NRR)]
    tails = []

    def emit_tail(st_):
        rb_ = st_ % NRR
        ob = st_ % NOO
        P.op("dve", lambda e: e.reciprocal(out=rstd2[:, st_:st_ + 1], in_=sd3[:, st_:st_ + 1]),
             r=[("sd3", st_)], w=[("rstd2", st_)])
        P.op("act", lambda e: e.activation(out=oo[ob], in_=rr[rb_], func=AF.Identity, scale=rstd2[:, st_:st_ + 1]),
             r=[("rr", rb_, 0), ("rr", rb_, 1), ("rstd2", st_)], w=[("oo", ob)])
        P.op("pool", lambda e: e.tensor_tensor(out=oo[ob], in0=oo[ob], in1=gfin_bc[:, :], op=ALU.mult),
             r=[("oo", ob), "gfin"] + AR_R, w=[("oo", ob)])
        P.op("act", lambda e: e.dma_start(out=out_d[st_ * 128:(st_ + 1) * 128, :], in_=oo[ob]),
             r=[("oo", ob)] + AR_R, dma=("oo", ob))

    load_YT(0)
    for tt in range(NT):
        yb = tt % 2
        tsl = slice(tt * 512, (tt + 1) * 512)
        if tt + 1 < NT:
            load_YT(tt + 1)
        for s in range(4):
            st_ = tt * 4 + s
            xb_ = st_ % NX3
            rb_ = st_ % NRR
            P.op("sp", lambda e, st_=st_, xb_=xb_: e.dma_start(out=xt3[xb_], in_=x_d[st_ * 128:(st_ + 1) * 128, :]),
                 r=AR_R, w=[("xt3", xb_)], dma=("xt3", xb_))
            for half in range(2):
                pbk = (st_ % 4) * 2 + half

                def f_o(e, yb=yb, s=s, half=half, pbk=pbk):
                    ins = None
                    for c in range(8):
                        ins = e.matmul(ps[pbk][:, :], lhsT=YT[yb][:, c, s * 128:(s + 1) * 128],
                                       rhs=wo[:, c, half * 512:(half + 1) * 512], start=(c == 0), stop=(c == 7))
                    return ins
                P.op("pe", f_o, r=[("YTa", yb), ("YTb", yb)] + WO_R, w=["ps%d" % pbk])
                hs = slice(half * 512, (half + 1) * 512)
                P.op("dve", lambda e, hs=hs, rb_=rb_, pbk=pbk, xb_=xb_: e.tensor_tensor(
                    out=rr[rb_][:, hs], in0=ps[pbk][:, :], in1=xt3[xb_][:, hs], op=ALU.add),
                    r=["ps%d" % pbk, ("xt3", xb_)] + AR_R, w=[("rr", rb_, half)])
            P.op("act", lambda e, rb_=rb_, st_=st_: e.activation(out=sq3, in_=rr[rb_], func=AF.Square,
                                                                accum_out=ssq2[:, st_:st_ + 1]),
                 r=[("rr", rb_, 0), ("rr", rb_, 1), "ssq2"], w=["sq3", ("ssq2", st_)])
            P.op("act", lambda e, st_=st_: e.activation(out=sd3[:, st_:st_ + 1], in_=ssq2[:, st_:st_ + 1], func=AF.Sqrt,
                                                       bias=EPS, scale=1.0 / D), r=[("ssq2", st_)], w=[("sd3", st_)])
            tails.append(st_)
            if len(tails) > 2:
                emit_tail(tails.pop(0))
    while tails:
        emit_tail(tails.pop(0))
    for ob in range(NOO):
        final_waits[("oo", ob)] = P.dmacnt[("oo", ob)]
    if debug:
        final_waits["dbg_hT"] = 1
    P.emit(nc, final_waits)
    return nc


def host_inputs(S, x, c, positions, w_ada, b_ada, g_norm, w_in, w_dw, b_dw, g_ln_conv, b_ln_conv, w_pw, b_pw,
                w_out, g_final):
    B = x.shape[0]
    NS = S // 128
    f = np.float32
    w_in0 = np.asarray(w_in[0], f)
    w_c = np.ascontiguousarray(w_in0[:, 0:1536])
    w_pair = np.stack([np.concatenate([w_in0[:, 1536 + 128 * p:1536 + 128 * (p + 1)],
                                       w_in0[:, 2048 + 128 * p:2048 + 128 * (p + 1)],
                                       w_in0[:, 2560 + 128 * p:2560 + 128 * (p + 1)],
                                       w_in0[:, 3072 + 128 * p:3072 + 128 * (p + 1)]], axis=1) for p in range(4)])
    ident = np.eye(128, dtype=f)
    half = 8
    inv = (500000.0 ** (-(np.arange(half, dtype=np.float32) * 2.0) / 16.0)).astype(f)
    invf = np.ascontiguousarray(np.broadcast_to(np.tile(inv, 4)[None, :], (128, 32))).astype(f)
    pk = np.arange(128)
    trib = np.where(pk[:, None] <= pk[None, :], 0.0, NEGB).astype(f)
    onehot = (np.arange(16)[:, None] == (np.arange(S)[None, :] // 256)).astype(f)
    maps = []
    for b in range(B):
        vecs = np.concatenate([np.asarray(b_ada[0], f).reshape(24, 128), np.asarray(g_norm[0], f).reshape(8, 128),
                               np.asarray(c[b], f).reshape(8, 128), np.asarray(b_dw[0], f).reshape(4, 128),
                               np.asarray(g_ln_conv[0], f).reshape(4, 128), np.asarray(b_ln_conv[0], f).reshape(4, 128),
                               np.asarray(b_pw[0], f).reshape(4, 128)], axis=0)
        pos = np.ascontiguousarray(np.asarray(positions[b], np.int32).reshape(NS, 128).T)
        maps.append({
            "x": np.ascontiguousarray(np.asarray(x[b], f)), "vecs": np.ascontiguousarray(vecs),
            "b_ada": np.ascontiguousarray(np.asarray(b_ada[0], f)), "pos": pos,
            "w_ada": np.ascontiguousarray(np.asarray(w_ada[0], f)), "w_c": w_c, "w_pair": np.ascontiguousarray(w_pair),
            "w_dw": np.ascontiguousarray(np.asarray(w_dw[0], f)), "w_pw": np.ascontiguousarray(np.asarray(w_pw[0], f)),
            "w_out": np.ascontiguousarray(np.asarray(w_out[0], f)), "g_final": np.ascontiguousarray(np.asarray(g_final, f)),
            "ident": ident, "invf": invf, "trib": trib, "onehot": onehot,
        })
    return maps


_NC_CACHE = {}


def kernel(x, c, positions, w_ada, b_ada, g_norm, w_in, w_dw, b_dw, g_ln_conv, b_ln_conv, w_pw, b_pw, w_out, g_final):
    x = np.asarray(x)
    B, S, _ = x.shape
    maps = host_inputs(S, x, np.asarray(c), np.asarray(positions), np.asarray(w_ada), np.asarray(b_ada),
                       np.asarray(g_norm), np.asarray(w_in), np.asarray(w_dw), np.asarray(b_dw), np.asarray(g_ln_conv),
                       np.asarray(b_ln_conv), np.asarray(w_pw), np.asarray(b_pw), np.asarray(w_out), np.asarray(g_final))
    if S not in _NC_CACHE:
        _NC_CACHE[S] = build(S)
    nc = _NC_CACHE[S]
    res = run_bass_kernel_spmd(nc, maps, core_ids=list(range(B)))
    return np.stack([np.asarray(r["out"], np.float32) for r in res.results], axis=0)
```

```python
import numpy as np
import ml_dtypes
from contextlib import ExitStack
import concourse.bass as bass
import concourse.mybir as mybir
from concourse.bass_utils import run_bass_kernel_spmd

F32, BF16, I32 = mybir.dt.float32, mybir.dt.bfloat16, mybir.dt.int32
AF = mybir.ActivationFunctionType
ALU = mybir.AluOpType
AX = mybir.AxisListType

D = 1024
EPS = 1e-6
NEGB = -30000.0
ENGS = ["sp", "pe", "act", "dve", "pool"]
LAST_PROG = None


class Prog:
    def __init__(self):
        self.ops = {e: [] for e in ENGS}
        self.lastw = {}
        self.readers = {}
        self.dmacnt = {}
        self.bar = None

    def barrier(self, fn):
        deps = set()
        for e in ENGS:
            if self.ops[e]:
                o = self.ops[e][-1]
                deps.add(o["me"])
            for o in self.ops[e]:
                if o["dma"] is not None:
                    deps.add(o["me"])
        me = self.op("pool", fn)
        self.ops["pool"][-1]["deps"] |= deps
        self.bar = me
        return me

    def op(self, eng, fn, r=(), w=(), dma=None):
        idx = len(self.ops[eng])
        w = list(w) + [x for x in r if isinstance(x, str) and x.startswith("ps") and x not in w]
        deps = set()
        if self.bar is not None:
            deps.add(self.bar)
        for x in r:
            if x in self.lastw:
                deps.add(self.lastw[x])
        for x in w:
            if x in self.lastw:
                deps.add(self.lastw[x])
            for rd in self.readers.get(x, ()):
                deps.add(rd)
        if dma is not None:
            n = self.dmacnt.get(dma, 0) + 1
            self.dmacnt[dma] = n
            me = ("dma", dma, n)
        else:
            me = (eng, idx)
        deps.discard(me)
        self.ops[eng].append(dict(fn=fn, deps=deps, dma=dma, sig=False, me=me))
        for x in r:
            self.readers.setdefault(x, []).append(me)
        for x in w:
            self.lastw[x] = me
            self.readers[x] = []
        return me

    def emit(self, nc, final_waits):
        for e in ENGS:
            for o in self.ops[e]:
                for d in o["deps"]:
                    if d[0] != "dma" and not (d[0] == "pe" and e == "pe"):
                        self.ops[d[0]][d[1]]["sig"] = True
        cnt = {}
        for e in ENGS:
            c = 0
            arr = []
            for o in self.ops[e]:
                if o["sig"]:
                    c += 1
                arr.append(c)
            cnt[e] = arr
        with ExitStack() as st:
            esem = {e: st.enter_context(nc.semaphore("es_" + e)) for e in ENGS}
            dsem = {k: st.enter_context(nc.semaphore("ds_%d" % i))
                    for i, k in enumerate(self.dmacnt)}
            block = st.enter_context(nc.Block())

            def run(h, e):
                waited = {}
                for o in self.ops[e]:
                    need = {}
                    for d in o["deps"]:
                        if d[0] == "dma":
                            k = ("dma", d[1])
                            v = 16 * d[2]
                        else:
                            if d[0] == "pe" and e == "pe":
                                continue
                            k = d[0]
                            v = cnt[d[0]][d[1]]
                        need[k] = max(need.get(k, 0), v)
                    for k, v in need.items():
                        if waited.get(k, 0) >= v:
                            continue
                        waited[k] = v
                        h.wait_ge(dsem[k[1]] if isinstance(k, tuple) else esem[k], v)
                    ins = o["fn"](h)
                    if o["dma"] is not None:
                        ins.then_inc(dsem[o["dma"]], 16)
                    elif o["sig"]:
                        ins.then_inc(esem[e], 1)
                if e == "sp":
                    for k, n in final_waits.items():
                        h.wait_ge(dsem[k], 16 * n)

            @block.sync
            def _(h):
                run(h, "sp")

            @block.tensor
            def _(h):
                run(h, "pe")

            @block.scalar
            def _(h):
                run(h, "act")

            @block.vector
            def _(h):
                run(h, "dve")

            @block.gpsimd
            def _(h):
                run(h, "pool")


class _Cut(Exception):
    pass


def build(S, debug=False, upto=3, cut=0):
    try:
        return _build(S, debug, upto, cut)
    except _Cut as c:
        return c.args[0]


def _build(S, debug=False, upto=3, cut=0):
    NT, NS, NB = S // 512, S // 128, S // 256
    nc = bass.Bass("TRN2", target_bir_lowering=False)
    P = Prog()
    global LAST_PROG
    LAST_PROG = P

    def din(name, shape, dt=F32):
        return nc.dram_tensor(name, list(shape), dt, kind="ExternalInput").ap()

    x_d = din("x", [S, D])
    vecs_d = din("vecs", [56, 128])
    bada_d = din("b_ada", [3 * D])
    pos_d = din("pos", [128, NS], I32)
    wada_d = din("w_ada", [D, 3 * D])
    wc_d = din("w_c", [D, 1536])
    wpair_d = din("w_pair", [4, D, 512])
    wdw_d = din("w_dw", [31, 512])
    wpw_d = din("w_pw", [512, 512])
    wout_d = din("w_out", [D, D])
    gfin_d = din("g_final", [D])
    ident_d = din("ident", [128, 128])
    invf_d = din("invf", [128, 32])
    trib_d = din("trib", [128, 128])
    oneh_d = din("onehot", [16, S])
    out_d = nc.dram_tensor("out", [S, D], F32, kind="ExternalOutput").ap()
    skind = "ExternalOutput" if debug else "Internal"
    ycT_d = nc.dram_tensor("ycT", [512, S], BF16, kind=skind).ap()
    yT_d = nc.dram_tensor("yT", [512, S], BF16, kind=skind).ap()
    if debug:
        hT_dbg = nc.dram_tensor("hT_dbg", [128, 8 * S], BF16, kind="ExternalOutput").ap()

    def sb(name, shape, dt=F32):
        return nc.alloc_sbuf_tensor("s_" + name, list(shape), dt)

    ident = sb("ident", [128, 128])
    identb = sb("identb", [128, 128], BF16)
    onesb = sb("onesb", [128, 128], BF16)
    vT = sb("vT", [128, 56])
    sc = sb("sc", [128, 8])
    modc = sb("modc", [128, 16])
    Acol = sb("Acol", [128, 8])
    shc = sb("shc", [128, 8])
    gate_bc = sb("gate_bc", [128, D])
    gfin_bc = sb("gfin_bc", [128, D])
    cos4 = sb("cos4", [128, NS, 32])
    sin4 = sb("sin4", [128, NS, 32])
    tribb = sb("tribb", [128, 128], BF16)
    hT = sb("hT", [128, 8, S], BF16)
    ssq = sb("ssq", [128, NS])
    rstd = sb("rstd", [128, NS])
    ssq2 = sb("ssq2", [128, NS])
    rstd2 = sb("rstd2", [128, NS])
    ARW = 31600
    AR = sb("arena", [128, ARW])
    ps = [nc.alloc_psum_tensor("ps%d" % i, [128, 512], F32) for i in range(8)]

    class Arena:
        def __init__(self):
            self.off = 0

        def get(self, shape, dt=F32):
            n = int(np.prod(shape[1:]))
            words = (n * (4 if dt in (F32, I32) else 2) + 3) // 4
            words = (words + 7) // 8 * 8
            a = AR[:, self.off:self.off + words]
            self.off += words
            assert self.off <= ARW, (self.off, ARW)
            if dt != F32:
                a = a.bitcast(dt)
            a = a[:, 0:n]
            if len(shape) == 3:
                a = a.rearrange("p (a b) -> p a b", b=shape[2])
            elif len(shape) == 4:
                a = a.rearrange("p (a b c) -> p a b c", b=shape[2], c=shape[3])
            return a

    phase_no = [0]

    def AT():
        return ("arena",)

    def new_phase():
        P.barrier(lambda e: e.memset(dummy[:, 0:1], 0.0))
        return Arena()

    dummy = sb("dummy", [128, 8])
    AR_R = [AT()]

    A0 = Arena()
    vraw = A0.get([128, 128])
    onesf = A0.get([128, 128])
    wa = [A0.get([128, 3 * D]) for _ in range(2)]
    gbias = A0.get([128, D])
    posi = A0.get([128, NS], I32)
    posf = A0.get([128, NS])
    invf = A0.get([128, 32])
    ang = A0.get([128, NS, 32])
    ki = A0.get([128, NS, 32], I32)
    kf = A0.get([128, NS, 32])
    tribf = A0.get([128, 128])
    sgn = A0.get([128, 32])
    NXT = 6
    xt = [A0.get([128, D]) for _ in range(NXT)]
    NXN = 8
    xn = [A0.get([128, D]) for _ in range(NXN)]
    sq_junk = A0.get([128, D])
    sdv = A0.get([128, NS])
    scb2 = A0.get([128, 8, 128])
    wg = [A0.get([128, D]) for _ in range(4)]

    P.op("sp", lambda e: e.dma_start(out=ident[:, :], in_=ident_d[:, :]), w=["ident"], dma="c0")
    P.op("sp", lambda e: e.dma_start(out=vraw[0:56, :], in_=vecs_d[:, :]), r=AR_R, w=["vraw"], dma="c1")
    P.op("sp", lambda e: e.dma_start(out=posi, in_=pos_d[:, :]), r=AR_R, w=["posi"], dma="c2")
    P.op("sp", lambda e: e.dma_start(out=invf, in_=invf_d[:, :]), r=AR_R, w=["invf"], dma="c3")
    P.op("sp", lambda e: e.dma_start(out=tribf, in_=trib_d[:, :]), r=AR_R, w=["tribf"], dma="c4")
    P.op("sp", lambda e: e.dma_start(out=gfin_bc[:, :], in_=gfin_d.partition_broadcast(128)), w=["gfin"], dma="c5")
    P.op("sp", lambda e: e.dma_start(out=gbias, in_=bada_d[2 * D:3 * D].partition_broadcast(128)), r=AR_R, w=["gbias"], dma="c6")

    P.op("dve", lambda e: e.tensor_copy(out=identb[:, :], in_=ident[:, :]), r=["ident"], w=["identb"])
    P.op("dve", lambda e: e.tensor_copy(out=tribb[:, :], in_=tribf), r=["tribf"], w=["tribb"])
    P.op("pool", lambda e: e.memset(onesb[:, :], 1.0 / 512.0), w=["onesb"])
    P.op("pool", lambda e: e.memset(onesf, 1.0), r=AR_R, w=["onesf"])
    P.op("pool", lambda e: e.memset(ssq[:, :], 0.0), w=["ssq"])
    P.op("pool", lambda e: e.memset(ssq2[:, :], 0.0), w=["ssq2"])

    P.op("pe", lambda e: e.transpose(out=ps[0][:, 0:56], in_=vraw[0:56, :], identity=ident[0:56, 0:56]),
         r=["vraw", "ident"], w=["ps0"])
    P.op("dve", lambda e: e.tensor_copy(out=vT[:, :], in_=ps[0][:, 0:56]), r=["ps0"], w=["vT"])
    P.op("act", lambda e: e.activation(out=sc[:, :], in_=vT[:, 32:40], func=AF.Silu), r=["vT"], w=["sc"])
    for kc in range(8):
        P.op("dve", lambda e, kc=kc: e.tensor_scalar(out=scb2[:, kc, :], in0=onesf, scalar1=sc[:, kc:kc + 1],
                                                     scalar2=None, op0=ALU.mult),
             r=["onesf", "sc"], w=[("scb", kc)])

    TWO_PI = 2.0 * np.pi
    HI = 6.28125
    LO = TWO_PI - HI
    P.op("dve", lambda e: e.tensor_copy(out=posf, in_=posi), r=["posi"], w=["posf"])
    for st_ in range(NS):
        P.op("dve", lambda e, s=st_: e.tensor_scalar(out=ang[:, s, :], in0=invf, scalar1=posf[:, s:s + 1],
                                                     scalar2=None, op0=ALU.mult),
             r=["posf", "invf"], w=[("ang", st_)])
    ANG_ALL = [("ang", s) for s in range(NS)]

    def trig(dst, shift, tag):
        P.op("dve", lambda e: e.tensor_scalar(out=kf, in0=ang, scalar1=shift, scalar2=1.0 / TWO_PI,
                                              op0=ALU.add, op1=ALU.mult), r=ANG_ALL + ["cs_prev"], w=["kf"])
        P.op("dve", lambda e: e.tensor_copy(out=ki, in_=kf), r=["kf"], w=["ki"])
        P.op("dve", lambda e: e.tensor_copy(out=kf, in_=ki), r=["ki"], w=["kf"])
        P.op("dve", lambda e: e.scalar_tensor_tensor(out=dst[:, :, :], in0=kf, scalar=-HI, in1=ang,
                                                     op0=ALU.mult, op1=ALU.add), r=["kf"] + ANG_ALL, w=[tag])
        P.op("dve", lambda e: e.scalar_tensor_tensor(out=dst[:, :, :], in0=kf, scalar=-LO, in1=dst[:, :, :],
                                                     op0=ALU.mult, op1=ALU.add), r=["kf", tag], w=[tag])
        if shift != 0.0:
            P.op("dve", lambda e: e.tensor_scalar(out=dst[:, :, :], in0=dst[:, :, :], scalar1=shift, scalar2=None,
                                                  op0=ALU.add), r=[tag], w=[tag])
        P.op("dve", lambda e: e.tensor_scalar(out=dst[:, :, :], in0=dst[:, :, :], scalar1=3.14159, scalar2=-3.14159,
                                              op0=ALU.min, op1=ALU.max), r=[tag], w=[tag])
        P.op("act", lambda e: e.activation(out=dst[:, :, :], in_=dst[:, :, :], func=AF.Sin), r=[tag], w=[tag, "cs_prev"])

    trig(sin4, 0.0, "sin4")
    trig(cos4, np.pi / 2.0, "cos4")

    for kc in range(8):
        b = kc % 2
        P.op("act", lambda e, kc=kc, b=b: e.dma_start(out=wa[b][:, 0:2 * D], in_=wada_d[kc * 128:(kc + 1) * 128, 0:2 * D]),
             r=AR_R, w=[("wa", b)], dma=("wa", b))

        def f_cols(e, kc=kc, b=b):
            ins = None
            for m in range(16):
                ins = e.matmul(ps[0][:, kc * 16 + m:kc * 16 + m + 1], lhsT=wa[b][:, m * 128:(m + 1) * 128],
                               rhs=sc[:, kc:kc + 1], start=True, stop=True)
            return ins
        P.op("pe", f_cols, r=[("wa", b), "sc"], w=["ps0"])

    def gate_dma(kc):
        P.op("act", lambda e: e.dma_start(out=wg[kc % 4], in_=wada_d[kc * 128:(kc + 1) * 128, 2 * D:3 * D]),
             r=AR_R, w=[("wg", kc % 4)], dma=("wg", kc % 4))

    def gate_mm(kc):
        for half in range(2):
            P.op("pe", lambda e, half=half: e.matmul(
                ps[1 + half][:, :], lhsT=scb2[:, kc, :], rhs=wg[kc % 4][:, half * 512:(half + 1) * 512],
                start=(kc == 0), stop=(kc == 7)), r=[("wg", kc % 4), ("scb", kc)], w=["ps%d" % (1 + half)])

    gate_dma(0)
    gate_dma(1)
    P.op("dve", lambda e: e.tensor_reduce(out=modc[:, :], in_=ps[0][:, 0:128].rearrange("p (k m) -> p m k", m=16),
                                          axis=AX.X, op=ALU.add), r=["ps0"], w=["modc"])
    P.op("dve", lambda e: e.tensor_tensor(out=modc[:, :], in0=modc[:, :], in1=vT[:, 0:16], op=ALU.add),
         r=["modc", "vT"], w=["modc"])
    P.op("dve", lambda e: e.tensor_copy(out=shc[:, :], in_=modc[:, 0:8]), r=["modc"], w=["shc"])
    P.op("dve", lambda e: e.scalar_tensor_tensor(out=Acol[:, :], in0=modc[:, 8:16], scalar=1.0, in1=vT[:, 24:32],
                                                 op0=ALU.add, op1=ALU.mult), r=["modc", "vT"], w=["Acol"])

    _pre1 = Arena()
    wc = _pre1.get([128, 8, 1536], BF16)
    DEAD0 = ["vraw", "onesf", ("wa", 0), ("wa", 1)]
    for kc in range(8):
        for part in range(3):
            P.op("pool", lambda e, kc=kc, part=part: e.dma_start(
                out=wc[:, kc, part * 512:(part + 1) * 512], in_=wc_d[kc * 128:(kc + 1) * 128, part * 512:(part + 1) * 512]),
                r=AR_R,
                w=[("wc", kc, part)] + (DEAD0 if (kc, part) == (0, 0) else []), dma=("wc", (kc * 3 + part) % 4))

    def xstage1(tt):
        for s in range(4):
            st_ = tt * 4 + s
            xb_, nb_ = st_ % NXT, st_ % NXN
            P.op("sp", lambda e, st_=st_, xb_=xb_: e.dma_start(out=xt[xb_], in_=x_d[st_ * 128:(st_ + 1) * 128, :]),
                 r=AR_R, w=[("xt", xb_)], dma=("xt", xb_))
            P.op("act", lambda e, st_=st_, xb_=xb_: e.activation(out=sq_junk, in_=xt[xb_], func=AF.Square,
                                                                accum_out=ssq[:, st_:st_ + 1]),
                 r=[("xt", xb_), "ssq"], w=["sqj", ("ssq", st_)])
            P.op("act", lambda e, st_=st_: e.activation(out=sdv[:, st_:st_ + 1], in_=ssq[:, st_:st_ + 1], func=AF.Sqrt,
                                                       bias=EPS, scale=1.0 / D), r=[("ssq", st_)], w=[("sdv", st_)])
            P.op("dve", lambda e, st_=st_: e.reciprocal(out=rstd[:, st_:st_ + 1], in_=sdv[:, st_:st_ + 1]),
                 r=[("sdv", st_)], w=[("rstd", st_)])
            P.op("dve", lambda e, st_=st_, xb_=xb_, nb_=nb_: e.tensor_scalar(
                out=xn[nb_], in0=xt[xb_], scalar1=rstd[:, st_:st_ + 1], scalar2=None, op0=ALU.mult),
                r=[("xt", xb_), ("rstd", st_)], w=[("xn", nb_)])

    def xstage2(tt):
        for kc in range(8):
            pb = 3 + kc % 4

            def f_tr(e, kc=kc, pb=pb):
                ins = None
                for s in range(4):
                    nb_ = (tt * 4 + s) % NXN
                    ins = e.transpose(out=ps[pb][:, s * 128:(s + 1) * 128], in_=xn[nb_][:, kc * 128:(kc + 1) * 128],
                                      identity=ident[:, :])
                return ins
            P.op("pe", f_tr, r=[("xn", (tt * 4 + s) % NXN) for s in range(4)] + ["ident"], w=["ps%d" % pb])
            P.op("act", lambda e, kc=kc, pb=pb: e.activation(
                out=hT[:, kc, tt * 512:(tt + 1) * 512], in_=ps[pb][:, :], func=AF.Identity,
                bias=shc[:, kc:kc + 1], scale=Acol[:, kc:kc + 1]),
                r=["ps%d" % pb, "shc", "Acol"], w=[("hT", tt)])

    xstage1(0)
    GPT = 8 // NT if NT <= 8 else 1
    for tt in range(NT):
        if tt + 1 < NT:
            xstage1(tt + 1)
        xstage2(tt)
        for kc in range(tt * GPT, (tt + 1) * GPT):
            if kc + 2 < 8:
                gate_dma(kc + 2)
            gate_mm(kc)
    for half in range(2):
        P.op("dve", lambda e, half=half: e.tensor_tensor(
            out=gate_bc[:, half * 512:(half + 1) * 512], in0=ps[1 + half][:, :],
            in1=gbias[:, half * 512:(half + 1) * 512], op=ALU.add),
            r=["ps%d" % (1 + half), "gbias"], w=[("gate_bc", half)])
    if debug:
        P.op("sp", lambda e: e.dma_start(out=hT_dbg[:, :], in_=hT[:, :, :].rearrange("p a b -> p (a b)")),
             r=[("hT", t) for t in range(NT)], dma="dbg_hT")

    HT_R = lambda tt: [("hT", tt)]

    def ck(n):
        if cut == n:
            P.barrier(lambda e: e.memset(dummy[:, 0:1], 0.0))
            P.emit(nc, {})
            raise _Cut(nc)
    if upto == 0:
        P.emit(nc, {"dbg_hT": 1})
        return nc

    A1 = new_phase()
    wc_ = A1.get([128, 8, 1536], BF16)
    WC_R = [("wc", kc, part) for kc in range(8) for part in range(3)]
    wpw = A1.get([128, 4, 512], BF16)
    wdraw = A1.get([128, 512])
    wdT = A1.get([128, 4, 32])
    Dg = A1.get([128, 4, 31, 128], BF16)
    U = [A1.get([128, 4, 542], BF16) for _ in range(2)]
    sg = [A1.get([128, 512]) for _ in range(2)]
    gc = [A1.get([128, 4, 512], BF16) for _ in range(2)]
    v32 = A1.get([128, 4, 512])
    vb = A1.get([128, 4, 512], BF16)
    sqb = A1.get([128, 4, 512], BF16)
    m2 = A1.get([128, 512])
    var = A1.get([128, 512])
    rs = A1.get([128, 512])
    mean_sb = A1.get([128, 512])
    tz = [A1.get([128, 512]) for _ in range(2)]
    zb = A1.get([128, 4, 512], BF16)
    yc = [A1.get([128, 4, 512], BF16) for _ in range(2)]

    for kc in range(4):
        P.op("pool", lambda e, kc=kc: e.dma_start(out=wpw[:, kc, :], in_=wpw_d[kc * 128:(kc + 1) * 128, :]),
             r=AR_R, w=[("wpw", kc)], dma=("wpw", kc % 2))
    P.op("sp", lambda e: e.dma_start(out=wdraw[0:31, :], in_=wdw_d[:, :]), r=AR_R, w=["wdraw"], dma="wdraw")
    for c in range(4):
        P.op("pe", lambda e, c=c: e.transpose(out=ps[0][:, c * 32:c * 32 + 31], in_=wdraw[0:31, c * 128:(c + 1) * 128],
                                              identity=ident[0:31, 0:31]), r=["wdraw", "ident"], w=["ps0"])
    P.op("dve", lambda e: e.tensor_copy(out=wdT[:, :, 0:31], in_=ps[0][:, 0:128].rearrange("p (c t) -> p c t", t=32)[:, :, 0:31]),
         r=["ps0"], w=["wdT"])
    for c in range(4):
        for tap in range(31):
            if tap % 2 == 0:
                P.op("dve", lambda e, c=c, tap=tap: e.tensor_scalar(out=Dg[:, c, tap, :], in0=ident[:, :],
                                                                    scalar1=wdT[:, c, tap:tap + 1], scalar2=None,
                                                                    op0=ALU.mult),
                     r=["wdT", "ident"] + AR_R, w=[("Dg", c, 0)])
            else:
                P.op("act", lambda e, c=c, tap=tap: e.activation(out=Dg[:, c, tap, :], in_=ident[:, :], func=AF.Identity,
                                                                 scale=wdT[:, c, tap:tap + 1]),
                     r=["wdT", "ident"] + AR_R, w=[("Dg", c, 1)])
    P.op("pool", lambda e: e.memset(U[0][:, :, 0:30], 0.0), r=AR_R, w=[("Uh", 0)])
    ck(1)

    def cA(tt, cs):
        ub = tt % 2
        tsl = slice(tt * 512, (tt + 1) * 512)
        gcb = gc[tt % 2]
        for c in cs:
            pbase = 0 if c % 2 == 0 else 3
            for gi, goff in enumerate((512, 0, 1024)):
                def f_in(e, c=c, goff=goff, pb=pbase + gi):
                    ins = None
                    for kc in range(8):
                        ins = e.matmul(ps[pb][:, :], lhsT=wc[:, kc, goff + c * 128:goff + (c + 1) * 128],
                                       rhs=hT[:, kc, tsl], start=(kc == 0), stop=(kc == 7))
                    return ins
                P.op("pe", f_in, r=WC_R + HT_R(tt), w=["ps%d" % (pbase + gi)])
            sgb = c % 2
            P.op("act", lambda e, pb=pbase, sgb=sgb: e.activation(out=sg[sgb], in_=ps[pb][:, :], func=AF.Sigmoid),
                 r=["ps%d" % pbase] + AR_R, w=[("sg", sgb)])
            P.op("dve", lambda e, pb=pbase + 1, sgb=sgb, c=c: e.tensor_tensor(
                out=U[ub][:, c, 30:542], in0=ps[pb][:, :], in1=sg[sgb], op=ALU.mult),
                r=["ps%d" % (pbase + 1), ("sg", sgb)], w=[("U", ub, c)])
            P.op("act", lambda e, pb=pbase + 2, c=c: e.activation(out=gcb[:, c, :], in_=ps[pb][:, :], func=AF.Sigmoid),
                 r=["ps%d" % (pbase + 2)], w=[("gc", tt % 2, c)])
            P.op("dve", lambda e, pb=pbase + 2, c=c: e.tensor_tensor(
                out=gcb[:, c, :], in0=ps[pb][:, :], in1=gcb[:, c, :], op=ALU.mult),
                r=["ps%d" % (pbase + 2), ("gc", tt % 2, c)], w=[("gc", tt % 2, c)])
        if tt == NT - 1 and 3 in cs:
            _pre2 = Arena()
            wp_pre = [_pre2.get([128, 8, 512], BF16) for _ in range(2)]
            for p_ in range(2):
                for kc in range(8):
                    P.op("pool", lambda e, p_=p_, kc=kc: e.dma_start(out=wp_pre[p_][:, kc, :],
                                                                  in_=wpair_d[p_, kc * 128:(kc + 1) * 128, :]),
                         r=AR_R,
                         w=[("wp", p_, kc)] + (WC_R if (p_, kc) == (0, 0) else []), dma=("wp", p_, kc % 2))

    def cB(tt):
        ub = tt % 2
        P.op("pool", lambda e: e.tensor_copy(out=U[1 - ub][:, :, 0:30], in_=U[ub][:, :, 512:542]),
             r=[("U", ub, c) for c in range(4)] + AR_R, w=[("Uh", 1 - ub)])

    def cC(tt):
        ub = tt % 2
        for c in range(4):
            pb = 6 + c % 2

            def f_cv(e, c=c, pb=pb):
                ins = None
                for tap in range(31):
                    ins = e.matmul(ps[pb][:, :], lhsT=Dg[:, c, tap, :], rhs=U[ub][:, c, tap:tap + 512],
                                   start=(tap == 0), stop=(tap == 30))
                return ins
            P.op("pe", f_cv, r=[("Dg", c, 0), ("Dg", c, 1), ("U", ub, c), ("Uh", ub)], w=["ps%d" % pb])
            P.op("dve", lambda e, c=c, pb=pb: e.tensor_scalar(out=v32[:, c, :], in0=ps[pb][:, :],
                                                             scalar1=vT[:, 40 + c:41 + c], scalar2=None, op0=ALU.add),
                 r=["ps%d" % pb, "vT"], w=[("v32", c)])
            P.op("act", lambda e, c=c, pb=pb: e.activation(out=vb[:, c, :], in_=ps[pb][:, :], func=AF.Identity,
                                                          bias=vT[:, 40 + c:41 + c], scale=1.0),
                 r=["ps%d" % pb, "vT"], w=[("vb", c)])
            P.op("act", lambda e, c=c, pb=pb: e.activation(out=sqb[:, c, :], in_=ps[pb][:, :], func=AF.Square,
                                                          bias=vT[:, 40 + c:41 + c], scale=1.0),
                 r=["ps%d" % pb, "vT"], w=[("sqb", c)])

    def cD(tt):
        def f_st(e, src, pb):
            ins = None
            for c in range(4):
                ins = e.matmul(ps[pb][:, :], lhsT=onesb[:, :], rhs=src[:, c, :], start=(c == 0), stop=(c == 3))
            return ins
        P.op("pe", lambda e: f_st(e, vb, 0), r=[("vb", c) for c in range(4)] + ["onesb"], w=["ps0"])
        P.op("pe", lambda e: f_st(e, sqb, 1), r=[("sqb", c) for c in range(4)] + ["onesb"], w=["ps1"])
        P.op("act", lambda e: e.activation(out=m2, in_=ps[0][:, :], func=AF.Square), r=["ps0"] + AR_R, w=["m2"])
        P.op("act", lambda e: e.activation(out=mean_sb, in_=ps[0][:, :], func=AF.Identity), r=["ps0"], w=["mean_sb"])
        P.op("dve", lambda e: e.tensor_tensor(out=var, in0=ps[1][:, :], in1=m2, op=ALU.subtract),
             r=["ps1", "m2"], w=["var"])
        P.op("act", lambda e: e.activation(out=var, in_=var, func=AF.Sqrt, bias=EPS, scale=1.0), r=["var"], w=["var"])
        P.op("dve", lambda e: e.reciprocal(out=rs, in_=var), r=["var"], w=["rs"])

    def cE(tt):
        for c in range(4):
            tb = c % 2
            P.op("pool", lambda e, c=c, tb=tb: e.tensor_tensor(out=tz[tb], in0=v32[:, c, :], in1=mean_sb, op=ALU.subtract),
                 r=[("v32", c), "mean_sb"] + AR_R, w=[("tz", tb)])
            P.op("dve", lambda e, tb=tb: e.tensor_tensor(out=tz[tb], in0=tz[tb], in1=rs, op=ALU.mult),
                 r=[("tz", tb), "rs"], w=[("tz", tb)])
            P.op("act", lambda e, c=c, tb=tb: e.activation(out=zb[:, c, :], in_=tz[tb], func=AF.Silu,
                                                          bias=vT[:, 48 + c:49 + c], scale=vT[:, 44 + c:45 + c]),
                 r=[("tz", tb), "vT"], w=[("zb", c)])

    def cF(tt):
        yb = tt % 2
        tsl = slice(tt * 512, (tt + 1) * 512)
        gcb = gc[tt % 2]
        for oc in range(4):
            pb = 3 + oc % 3

            def f_pw(e, oc=oc, pb=pb):
                ins = None
                for k4 in range(4):
                    ins = e.matmul(ps[pb][:, :], lhsT=wpw[:, k4, oc * 128:(oc + 1) * 128], rhs=zb[:, k4, :],
                                   start=(k4 == 0), stop=(k4 == 3))
                return ins
            P.op("pe", f_pw, r=[("wpw", k4) for k4 in range(4)] + [("zb", c) for c in range(4)], w=["ps%d" % pb])
            P.op("dve", lambda e, oc=oc, pb=pb: e.scalar_tensor_tensor(
                out=yc[yb][:, oc, :], in0=ps[pb][:, :], scalar=vT[:, 52 + oc:53 + oc], in1=gcb[:, oc, :],
                op0=ALU.add, op1=ALU.mult), r=["ps%d" % pb, "vT", ("gc", tt % 2, oc)], w=[("yc", yb)])
        P.op("sp", lambda e: e.dma_start(out=ycT_d.rearrange("(c p) t -> p c t", p=128)[:, :, tsl], in_=yc[yb]),
             r=[("yc", yb)] + AR_R, w=[("ycT", tt)], dma=("yc", yb))

    cA(0, [0, 1, 2, 3])
    cB(0)
    cC(0)
    cD(0)
    for tt in range(NT):
        nxt = tt + 1 < NT
        if nxt:
            cA(tt + 1, [0, 1])
        cE(tt)
        if nxt:
            cA(tt + 1, [2, 3])
            cB(tt + 1)
        cF(tt)
        if nxt:
            cC(tt + 1)
            cD(tt + 1)

    if upto == 1:
        P.emit(nc, {("yc", 0): P.dmacnt[("yc", 0)], ("yc", 1): P.dmacnt[("yc", 1)]})
        return nc
    A2 = new_phase()
    wp = [A2.get([128, 8, 512], BF16) for _ in range(2)]
    KAs = [[A2.get([128, S], BF16) for _ in range(2)] for _ in range(2)]
    VAs = [A2.get([128, NS, 2, 128], BF16) for _ in range(2)]
    NQ = 5
    QAs = [[A2.get([128, 512], BF16) for _ in range(2)] for _ in range(NQ)]
    GAs = [A2.get([128, 512], BF16) for _ in range(NQ)]
    GAtmp = A2.get([128, 512])
    QK4 = [A2.get([128, 4, 256])]
    rtmp = [A2.get([128, 4, 4, 8]) for _ in range(4)]
    GSall = A2.get([128, NT, 8, 16])
    Lt = [A2.get([128, 512])]
    kms = [A2.get([128, 16]) for _ in range(2)]
    kmh = [A2.get([128, 16], BF16) for _ in range(2)]
    kml = [A2.get([128, 16], BF16) for _ in range(2)]
    m8 = A2.get([128, 8, 8])
    tmpm = A2.get([128, 8, 16])
    MB = A2.get([128, 4, 128], BF16)
    NPT = 5
    PT = [A2.get([128, 512], BF16) for _ in range(NPT)]
    Rt = [A2.get([128, 512])]
    Tt = A2.get([128, 512])
    Yb = [A2.get([128, 512], BF16) for _ in range(2)]

    for ks in range(2):
        for hh in range(2):
            P.op("act", lambda e, ks=ks, hh=hh: e.memzero(KAs[ks][hh][:, :]), r=AR_R, w=[("KAinit", ks, hh)])
            P.op("pool", lambda e, ks=ks, hh=hh: e.dma_start(out=KAs[ks][hh][64:80, :], in_=oneh_d[:, :]),
                 r=[("KAinit", ks, hh)], w=[("KAm", ks, hh)], dma=("oneh", ks, hh))
        P.op("pool", lambda e, ks=ks: e.memset(VAs[ks][:, :, :, 64:128], 1.0), r=AR_R, w=[("VAones", ks)])
    for hh in range(2):
        for qi in range(NQ):
            P.op("act", lambda e, hh=hh, qi=qi: e.memzero(QAs[qi][hh]), r=AR_R,
                 w=[("QAq", qi, hh), ("QAm", qi, hh)])
        P.op("pool", lambda e, hh=hh: e.memset(kmh[hh], 0.0), r=AR_R, w=[("kmh", hh)])
        P.op("pool", lambda e, hh=hh: e.memset(kml[hh], 0.0), r=AR_R, w=[("kml", hh)])
        P.op("pool", lambda e, hh=hh: e.memset(kms[hh], 0.0), r=AR_R, w=[("kms", hh)])
    P.op("pool", lambda e: e.memset(MB, 0.0), r=AR_R, w=["MB"])
    P.op("pool", lambda e: e.memset(GSall, -1e30), r=AR_R, w=["GSall"])
    for tt in range(NT):
        for half_, b_ in ((0, 2 * tt), (1, 2 * tt + 1)):
            P.op("pool", lambda e, tt=tt, half_=half_, b_=b_: e.memset(GSall[:, tt, half_ * 4:half_ * 4 + 4, b_:b_ + 1], 1e30),
                 r=AR_R, w=["GSall"])

    pt_ctr = [0]
    WP_R = [[("wp", wb_, kc) for kc in range(8)] for wb_ in range(2)]

    def prep(p, tt):
        wb = p % 2
        ks = p % 2
        KA = KAs[ks]
        VA = VAs[ks]
        par = (p * NT + tt) % NQ
        QA = QAs
        GA = GAs
        tsl = slice(tt * 512, (tt + 1) * 512)
        qk4 = QK4[0]

        for s in range(4):
            st_ = tt * 4 + s

            tb = 1 - s % 2

            def f_tok(e, st_=st_, tb=tb):
                ins = None
                for kc in range(8):
                    ins = e.matmul(ps[tb][:, 0:384], lhsT=hT[:, kc, st_ * 128:(st_ + 1) * 128],
                                   rhs=wp[wb][:, kc, 0:384], start=(kc == 0), stop=(kc == 7))
                return ins
            P.op("pe", f_tok, r=WP_R[wb] + HT_R(tt), w=["ps%d" % tb])
            P.op("dve", lambda e, s=s, tb=tb: e.tensor_copy(out=qk4[:, s, :], in_=ps[tb][:, 0:256]),
                 r=["ps%d" % tb] + AR_R, w=[("qk4", s)])
            P.op("dve", lambda e, st_=st_, tb=tb: e.tensor_copy(
                out=VA[:, st_, :, 0:64], in_=ps[tb][:, 256:384].rearrange("p (h d) -> p h d", d=64)),
                r=["ps%d" % tb] + AR_R, w=[("VA", ks, st_)])
            yield
        def f_ga(e):
            ins = None
            for kc in range(8):
                ins = e.matmul(ps[1][:, :], lhsT=wp[wb][:, kc, 384:512], rhs=hT[:, kc, tsl],
                               start=(kc == 0), stop=(kc == 7))
            return ins
        P.op("pe", f_ga, r=WP_R[wb] + HT_R(tt), w=["ps1"])
        P.op("act", lambda e: e.activation(out=GAtmp, in_=ps[1][:, :], func=AF.Exp, scale=-1.0),
             r=["ps1"] + AR_R, w=["GAtmp"])
        P.op("act", lambda e: e.activation(out=GAtmp, in_=GAtmp, func=AF.Ln, bias=1.0, scale=1.0),
             r=["GAtmp"], w=["GAtmp"])
        P.op("act", lambda e: e.activation(out=GAtmp, in_=GAtmp, func=AF.Exp, scale=-1.0),
             r=["GAtmp"], w=["GAtmp"])
        P.op("dve", lambda e: e.tensor_tensor(out=GA[par], in0=ps[1][:, :], in1=GAtmp, op=ALU.mult),
             r=["ps1", "GAtmp"], w=[("GA", par)])
        yield
        Q4 = qk4.rearrange("p s (g d) -> p s g d", d=64)
        c4 = cos4[:, tt * 4:(tt + 1) * 4, :].rearrange("p s (g d) -> p s g d", d=8)
        s4 = sin4[:, tt * 4:(tt + 1) * 4, :].rearrange("p s (g d) -> p s g d", d=8)
        QK_ALL = [("qk4", s) for s in range(4)]
        P.op("dve", lambda e: e.tensor_tensor(out=rtmp[0], in0=Q4[:, :, :, 0:8], in1=c4, op=ALU.mult),
             r=QK_ALL + ["cos4"], w=[("rtmp", 0)])
        P.op("dve", lambda e: e.tensor_tensor(out=rtmp[1], in0=Q4[:, :, :, 8:16], in1=s4, op=ALU.mult),
             r=QK_ALL + ["sin4"], w=[("rtmp", 1)])
        P.op("dve", lambda e: e.tensor_tensor(out=rtmp[2], in0=Q4[:, :, :, 8:16], in1=c4, op=ALU.mult),
             r=QK_ALL + ["cos4"], w=[("rtmp", 2)])
        P.op("dve", lambda e: e.tensor_tensor(out=rtmp[3], in0=Q4[:, :, :, 0:8], in1=s4, op=ALU.mult),
             r=QK_ALL + ["sin4"], w=[("rtmp", 3)])
        P.op("dve", lambda e: e.tensor_tensor(out=Q4[:, :, :, 0:8], in0=rtmp[0], in1=rtmp[1], op=ALU.subtract),
             r=[("rtmp", 0), ("rtmp", 1)], w=QK_ALL)
        P.op("dve", lambda e: e.tensor_tensor(out=Q4[:, :, :, 8:16], in0=rtmp[2], in1=rtmp[3], op=ALU.add),
             r=[("rtmp", 2), ("rtmp", 3)], w=QK_ALL)
        yield
        yield
        for s in range(4):
            h2 = s // 2
            qo = (s % 2) * 128
            tbk = s // 2
            P.op("pe", lambda e, s=s, qo=qo, tbk=tbk: e.transpose(out=ps[tbk][:, qo:qo + 128], in_=qk4[:, s, 0:128],
                                                         identity=ident[:, :]),
                 r=[("qk4", s), "ident"], w=["ps%d" % tbk])
            P.op("pe", lambda e, s=s, qo=qo, tbk=tbk: e.transpose(out=ps[tbk][:, 256 + qo:256 + qo + 128], in_=qk4[:, s, 128:256],
                                                         identity=ident[:, :]),
                 r=[("qk4", s), "ident"], w=["ps%d" % tbk])
            if s % 2 == 1:
                hsl = slice(h2 * 256, (h2 + 1) * 256)
                ksl = slice(tt * 512 + h2 * 256, tt * 512 + (h2 + 1) * 256)
                P.op("dve", lambda e, hsl=hsl, tbk=tbk: e.tensor_copy(out=QA[par][0][0:64, hsl], in_=ps[tbk][0:64, 0:256]),
                     r=["ps%d" % tbk], w=[("QAq", par, 0)])
                P.op("dve", lambda e, hsl=hsl, tbk=tbk: e.tensor_copy(out=QA[par][1][0:64, hsl], in_=ps[tbk][64:128, 0:256]),
                     r=["ps%d" % tbk], w=[("QAq", par, 1)])
                P.op("dve", lambda e, ksl=ksl, tbk=tbk: e.tensor_copy(out=KA[0][0:64, ksl], in_=ps[tbk][0:64, 256:512]),
                     r=["ps%d" % tbk, ("KAinit", ks, 0)], w=[("KA", ks, 0, tt)])
                P.op("dve", lambda e, ksl=ksl, tbk=tbk: e.tensor_copy(out=KA[1][0:64, ksl], in_=ps[tbk][64:128, 256:512]),
                     r=["ps%d" % tbk, ("KAinit", ks, 1)], w=[("KA", ks, 1, tt)])
                yield
        bsl = slice(2 * tt, 2 * tt + 2)
        for hh in range(2):
            P.op("dve", lambda e, hh=hh: e.tensor_reduce(
                out=kms[hh][0:64, bsl], in_=KA[hh][0:64, tsl].rearrange("p (n k) -> p n k", k=256),
                axis=AX.X, op=ALU.add), r=[("KA", ks, hh, tt)], w=[("kms", hh)])
            P.op("dve", lambda e, hh=hh: e.tensor_scalar(out=kmh[hh][0:64, bsl], in0=kms[hh][0:64, bsl],
                                                        scalar1=1.0 / 256.0, scalar2=None, op0=ALU.mult),
                 r=[("kms", hh)], w=[("kmh", hh)])
            P.op("dve", lambda e, hh=hh: e.scalar_tensor_tensor(out=kml[hh][0:64, bsl], in0=kms[hh][0:64, bsl],
                                                               scalar=1.0 / 256.0, in1=kmh[hh][0:64, bsl],
                                                               op0=ALU.mult, op1=ALU.subtract),
                 r=[("kms", hh), ("kmh", hh)], w=[("kml", hh)])
        yield
        b0, b1 = 2 * tt, 2 * tt + 1
        if b0 >= 4:
            def f_gate(e):
                ins = None
                for s in range(4):
                    for hh in range(2):
                        j = s * 2 + hh
                        e.matmul(ps[0][:, j * 16:(j + 1) * 16], lhsT=QA[par][hh][0:80, s * 128:(s + 1) * 128],
                                 rhs=kmh[hh][0:80, :], start=True, stop=False)
                        ins = e.matmul(ps[0][:, j * 16:(j + 1) * 16], lhsT=QA[par][hh][0:80, s * 128:(s + 1) * 128],
                                       rhs=kml[hh][0:80, :], start=False, stop=True)
                return ins
            P.op("pe", f_gate, r=[("QAq", par, 0), ("QAq", par, 1), ("kmh", 0), ("kmh", 1), ("kml", 0), ("kml", 1)],
                 w=["ps0"])
            gsb = GSall[:, tt]
            G3 = ps[0][:, 0:128].rearrange("p (j n) -> p j n", n=16)
            P.op("dve", lambda e: e.tensor_copy(out=gsb[:, 0:4, 0:b0], in_=G3[:, 0:4, 0:b0]),
                 r=["ps0", "GSall"], w=[("gsb", tt)])
            P.op("dve", lambda e: e.tensor_copy(out=gsb[:, 4:8, 0:b1], in_=G3[:, 4:8, 0:b1]),
                 r=["ps0", "GSall"], w=[("gsb", tt)])
            for j in range(8):
                P.op("dve", lambda e, j=j: e.max(out=m8[:, j, :], in_=gsb[:, j, :]), r=[("gsb", tt)], w=[("m8", j)])
            P.op("dve", lambda e: e.tensor_tensor(out=tmpm, in0=gsb, in1=m8[:, :, 3:4].to_broadcast([128, 8, 16]),
                                                  op=ALU.is_lt),
                 r=[("gsb", tt)] + [("m8", j) for j in range(8)], w=[("tmpm", j) for j in range(8)])
            MBv = MB[:, :, 64:128].rearrange("p s (h c) -> p s h c", c=32)[:, :, :, 0:16]
            P.op("dve", lambda e: e.tensor_scalar(out=MBv, in0=tmpm.rearrange("p (s h) n -> p s h n", h=2),
                                                  scalar1=NEGB, scalar2=None, op0=ALU.mult),
                 r=[("tmpm", j) for j in range(8)], w=["MB"])
        else:
            P.op("dve", lambda e: e.memset(MB[:, :, 64:128], 0.0), r=AR_R, w=["MB"])
        yield
        yield
        yield
        def f_mk(e):
            ins = None
            for s in range(4):
                ins = e.matmul(ps[1][:, s * 128:(s + 1) * 128], lhsT=MB[:, s, :], rhs=identb[:, :],
                               start=True, stop=True)
            return ins
        P.op("pe", f_mk, r=["MB", "identb"], w=["ps1"])
        P.op("dve", lambda e: e.tensor_copy(out=QA[par][0][64:80, :], in_=ps[1][64:80, :]),
             r=["ps1"], w=[("QAm", par, 0)])
        P.op("dve", lambda e: e.tensor_copy(out=QA[par][1][64:80, :], in_=ps[1][96:112, :]),
             r=["ps1"], w=[("QAm", par, 1)])
        yield

    def attn(p, tt):
        ks = p % 2
        KA = KAs[ks]
        VA = VAs[ks]
        par = (p * NT + tt) % NQ
        QA = QAs
        GA = GAs
        tsl = slice(tt * 512, (tt + 1) * 512)
        nkt = 4 * tt + 4
        SBK = [2, 3, 4, 5]
        LAG = 4
        pend = []

        def emit_pv(info):
            hh, kt, c0, pti = info
            po = 6 + hh
            P.op("pe", lambda e: e.matmul(ps[po][:, c0:512], lhsT=VA[:, kt, hh, :], rhs=PT[pti][:, c0:512],
                                          start=(kt == 0), stop=(kt == nkt - 1)),
                 r=[("VA", ks, kt), ("VAones", ks), ("PT", pti)], w=["ps%d" % po])
            if kt == nkt - 1:
                rb = 0
                rows = slice(hh * 64, hh * 64 + 64)
                P.op("act", lambda e: e.activation(out=Lt[rb][64:128, :], in_=ps[po][64:128, :], func=AF.Ln),
                     r=["ps%d" % po] + AR_R, w=[("Lt", rb)])
                P.op("act", lambda e: e.activation(out=Rt[rb][64:128, :], in_=Lt[rb][64:128, :], func=AF.Exp, scale=-1.0),
                     r=[("Lt", rb)], w=[("Rt", rb)])
                P.op("dve", lambda e: e.tensor_tensor(out=Tt[rows, :], in0=ps[po][0:64, :], in1=Rt[rb][64:128, :],
                                                      op=ALU.mult),
                     r=["ps%d" % po, ("Rt", rb)], w=[("Tt", hh)])
                P.op("pool", lambda e: e.tensor_tensor(out=Yb[tt % 2][rows, :], in0=Tt[rows, :], in1=GA[par][rows, :],
                                                       op=ALU.mult),
                     r=[("Tt", hh), ("GA", par)] + AR_R, w=[("Yb", tt % 2)])

        steps = [(hh, kt) for hh in range(2) for kt in range(nkt)]
        for i, (hh, kt) in enumerate(steps):
            j = kt - 4 * tt
            c0 = max(0, j) * 128
            sbk = SBK[i % len(SBK)]
            pti = pt_ctr[0] % NPT
            pt_ctr[0] += 1

            def f_s(e, hh=hh, kt=kt, j=j, c0=c0, sbk=sbk):
                ins = e.matmul(ps[sbk][:, c0:512], lhsT=KA[hh][0:80, kt * 128:(kt + 1) * 128],
                               rhs=QA[par][hh][0:80, c0:512], start=True, stop=(j < 0))
                if j >= 0:
                    ins = e.matmul(ps[sbk][:, c0:c0 + 128], lhsT=identb[:, :], rhs=tribb[:, :],
                                   start=False, stop=True)
                return ins
            P.op("pe", f_s, r=[("KA", ks, hh, kt // 4), ("KAm", ks, hh), ("QAq", par, hh), ("QAm", par, hh),
                               "identb", "tribb"], w=["ps%d" % sbk])
            P.op("act", lambda e, sbk=sbk, c0=c0, pti=pti: e.activation(
                out=PT[pti][:, c0:512], in_=ps[sbk][:, c0:512], func=AF.Exp, scale=0.125),
                r=["ps%d" % sbk] + AR_R, w=[("PT", pti)])
            pend.append((hh, kt, c0, pti))
            if len(pend) > LAG:
                emit_pv(pend.pop(0))
            yield
        while pend:
            emit_pv(pend.pop(0))
        P.op("sp", lambda e: e.dma_start(out=yT_d[p * 128:(p + 1) * 128, tsl], in_=Yb[tt % 2]),
             r=[("Yb", tt % 2)] + AR_R, w=[("yT", p, tt)], dma=("Yb", tt % 2))

    def interleave(ga, gb, na, nb):
        done_b = 0
        for i, _ in enumerate(ga):
            want = ((i + 1) * nb + na - 1) // na
            while done_b < want:
                try:
                    next(gb)
                except StopIteration:
                    done_b = 10 ** 9
                    break
                done_b += 1
        for _ in gb:
            pass

    NPREP = 15

    def load_wp(p):
        wb = p % 2
        for kc in range(8):
            P.op("pool", lambda e, kc=kc: e.dma_start(out=wp[wb][:, kc, :], in_=wpair_d[p, kc * 128:(kc + 1) * 128, :]),
                 r=AR_R, w=[("wp", wb, kc)], dma=("wp", wb, kc % 2))

    G = 4 * NT
    nsteps = [2 * (4 * tt + 4) for tt in range(NT)]
    SP = sum(nsteps)
    Wn = SP / NT
    Cn = [sum(nsteps[:tt]) for tt in range(NT)]
    lead = max((n + 1) * Wn - Cn[n] for n in range(NT)) + 2
    prep_pos = []
    for g in range(G):
        p, tt = g // NT, g % NT
        start = p * SP + tt * Wn - lead
        for k in range(NPREP + 1):
            prep_pos.append((start + k * Wn / (NPREP + 1), g))
    preps = {}
    pi = [0]
    cur_g = [0]

    def advance_prep(upto_pos):
        while pi[0] < len(prep_pos) and prep_pos[pi[0]][0] <= upto_pos:
            g = prep_pos[pi[0]][1]
            if g - cur_g[0] > NQ - 2:
                break
            if g not in preps:
                if g % NT == 0 and g // NT >= 2:
                    load_wp(g // NT)
                preps[g] = prep(g // NT, g % NT)
            try:
                next(preps[g])
            except StopIteration:
                pass
            pi[0] += 1

    def finish_prep(g):
        while pi[0] < len(prep_pos) and prep_pos[pi[0]][1] <= g:
            gg = prep_pos[pi[0]][1]
            if gg not in preps:
                if gg % NT == 0 and gg // NT >= 2:
                    load_wp(gg // NT)
                preps[gg] = prep(gg // NT, gg % NT)
            try:
                next(preps[gg])
            except StopIteration:
                pass
            pi[0] += 1
        if g in preps:
            for _ in preps[g]:
                pass

    pos = 0
    _pre3 = Arena()
    wo = _pre3.get([128, 8, D], BF16)
    YT_pre = [_pre3.get([128, 8, 512], BF16) for _ in range(2)]
    yt_pref = set()
    for g in range(G):
        cur_g[0] = g
        finish_prep(g)
        if g == G - 1:
            for kc in range(8):
                for half in range(2):
                    P.op("pool", lambda e, kc=kc, half=half: e.dma_start(
                        out=wo[:, kc, half * 512:(half + 1) * 512],
                        in_=wout_d[kc * 128:(kc + 1) * 128, half * 512:(half + 1) * 512]),
                        r=AR_R,
                        w=[("wo", kc, half)] + (WP_R[0] + WP_R[1] if (kc, half) == (0, 0) else []),
                        dma=("wo", (kc * 2 + half) % 4))
        if g == G - 1:
            for t0 in range(min(2, NT - 1) if S == 4096 else 0):
                yb0 = t0 % 2
                tsl0 = slice(t0 * 512, (t0 + 1) * 512)
                dead = [("KA", 0, yb0, t) for t in range(NT)] + [("KAm", 0, yb0), ("KAinit", 0, yb0)]
                P.op("sp", lambda e, yb0=yb0, tsl0=tsl0: e.dma_start(
                    out=YT_pre[yb0][:, 0:4, :], in_=ycT_d.rearrange("(c p) t -> p c t", p=128)[:, :, tsl0]),
                    r=[("ycT", t0)] + AR_R, w=[("YTa", yb0)] + dead, dma=("YTa", yb0))
                P.op("sp", lambda e, yb0=yb0, tsl0=tsl0: e.dma_start(
                    out=YT_pre[yb0][:, 4:8, :], in_=yT_d.rearrange("(c p) t -> p c t", p=128)[:, :, tsl0]),
                    r=[("yT", p_, t0) for p_ in range(4)] + AR_R, w=[("YTb", yb0)], dma=("YTb", yb0))
                yt_pref.add(t0)
        for _ in attn(g // NT, g % NT):
            pos += 1
            advance_prep(pos)
    for kc in range(8):
        eng = "dve" if kc % 2 == 0 else "pool"
        WO_ALL = [("wo", k_, h_) for k_ in range(8) for h_ in range(2)]
        P.op(eng, lambda e, kc=kc: e.tensor_tensor(out=wo[:, kc, :], in0=wo[:, kc, :], in1=gate_bc[:, :], op=ALU.mult),
             r=[("gate_bc", 0), ("gate_bc", 1)] + AR_R + (WO_ALL if kc == 0 else ["wo_ready"]),
             w=[("wo", kc, 0), ("wo", kc, 1)] + (["wo_ready"] + WO_ALL if kc == 0 else []))
    if upto == 2:
        P.emit(nc, {("Yb", 0): P.dmacnt[("Yb", 0)], ("Yb", 1): P.dmacnt[("Yb", 1)]})
        return nc
    A3 = new_phase()
    wo_ = A3.get([128, 8, D], BF16)
    YT = [A3.get([128, 8, 512], BF16) for _ in range(2)]
    NX3 = 8
    xt3 = [A3.get([128, D]) for _ in range(NX3)]
    NOO = 4
    oo = [A3.get([128, D]) for _ in range(NOO)]
    sq3 = A3.get([128, D])
    sd3 = A3.get([128, NS])
    WO_R = [("wo", kc, h_) for kc in range(8) for h_ in range(2)]
    final_waits = {}

    def load_YT(tt):
        if tt in yt_pref:
            return
        yb = tt % 2
        tsl = slice(tt * 512, (tt + 1) * 512)
        P.op("sp", lambda e: e.dma_start(out=YT[yb][:, 0:4, :], in_=ycT_d.rearrange("(c p) t -> p c t", p=128)[:, :, tsl]),
             r=[("ycT", tt)] + AR_R, w=[("YTa", yb)], dma=("YTa", yb))
        P.op("sp", lambda e: e.dma_start(out=YT[yb][:, 4:8, :], in_=yT_d.rearrange("(c p) t -> p c t", p=128)[:, :, tsl]),
             r=[("yT", p, tt) for p in range(4)] + AR_R, w=[("YTb", yb)], dma=("YTb", yb))

    NRR = 4
    rr = [A3.get([128, D]) for _ in range(NRR)]
    tails = []

    def emit_tail(st_):
        rb_ = st_ % NRR
        ob = st_ % NOO
        P.op("dve", lambda e: e.reciprocal(out=rstd2[:, st_:st_ + 1], in_=sd3[:, st_:st_ + 1]),
             r=[("sd3", st_)], w=[("rstd2", st_)])
        P.op("act", lambda e: e.activation(out=oo[ob], in_=rr[rb_], func=AF.Identity, scale=rstd2[:, st_:st_ + 1]),
             r=[("rr", rb_, 0), ("rr", rb_, 1), ("rstd2", st_)], w=[("oo", ob)])
        P.op("pool", lambda e: e.tensor_tensor(out=oo[ob], in0=oo[ob], in1=gfin_bc[:, :], op=ALU.mult),
             r=[("oo", ob), "gfin"] + AR_R, w=[("oo", ob)])
        P.op("act", lambda e: e.dma_start(out=out_d[st_ * 128:(st_ + 1) * 128, :], in_=oo[ob]),
             r=[("oo", ob)] + AR_R, dma=("oo", ob))

    load_YT(0)
    for tt in range(NT):
        yb = tt % 2
        tsl = slice(tt * 512, (tt + 1) * 512)
        if tt + 1 < NT:
            load_YT(tt + 1)
        for s in range(4):
            st_ = tt * 4 + s
            xb_ = st_ % NX3
            rb_ = st_ % NRR
            P.op("sp", lambda e, st_=st_, xb_=xb_: e.dma_start(out=xt3[xb_], in_=x_d[st_ * 128:(st_ + 1) * 128, :]),
                 r=AR_R, w=[("xt3", xb_)], dma=("xt3", xb_))
            for half in range(2):
                pbk = (st_ % 4) * 2 + half

                def f_o(e, yb=yb, s=s, half=half, pbk=pbk):
                    ins = None
                    for c in range(8):
                        ins = e.matmul(ps[pbk][:, :], lhsT=YT[yb][:, c, s * 128:(s + 1) * 128],
                                       rhs=wo[:, c, half * 512:(half + 1) * 512], start=(c == 0), stop=(c == 7))
                    return ins
                P.op("pe", f_o, r=[("YTa", yb), ("YTb", yb)] + WO_R, w=["ps%d" % pbk])
                hs = slice(half * 512, (half + 1) * 512)
                P.op("dve", lambda e, hs=hs, rb_=rb_, pbk=pbk, xb_=xb_: e.tensor_tensor(
                    out=rr[rb_][:, hs], in0=ps[pbk][:, :], in1=xt3[xb_][:, hs], op=ALU.add),
                    r=["ps%d" % pbk, ("xt3", xb_)] + AR_R, w=[("rr", rb_, half)])
            P.op("act", lambda e, rb_=rb_, st_=st_: e.activation(out=sq3, in_=rr[rb_], func=AF.Square,
                                                                accum_out=ssq2[:, st_:st_ + 1]),
                 r=[("rr", rb_, 0), ("rr", rb_, 1), "ssq2"], w=["sq3", ("ssq2", st_)])
            P.op("act", lambda e, st_=st_: e.activation(out=sd3[:, st_:st_ + 1], in_=ssq2[:, st_:st_ + 1], func=AF.Sqrt,
                                                       bias=EPS, scale=1.0 / D), r=[("ssq2", st_)], w=[("sd3", st_)])
            tails.append(st_)
            if len(tails) > 2:
                emit_tail(tails.pop(0))
    while tails:
        emit_tail(tails.pop(0))
    for ob in range(NOO):
        final_waits[("oo", ob)] = P.dmacnt[("oo", ob)]
    if debug:
        final_waits["dbg_hT"] = 1
    P.emit(nc, final_waits)
    return nc


def host_inputs(S, x, c, positions, w_ada, b_ada, g_norm, w_in, w_dw, b_dw, g_ln_conv, b_ln_conv, w_pw, b_pw,
                w_out, g_final):
    B = x.shape[0]
    NS = S // 128
    f = np.float32
    w_in0 = np.asarray(w_in[0], f)
    w_c = np.ascontiguousarray(w_in0[:, 0:1536])
    w_pair = np.stack([np.concatenate([w_in0[:, 1536 + 128 * p:1536 + 128 * (p + 1)],
                                       w_in0[:, 2048 + 128 * p:2048 + 128 * (p + 1)],
                                       w_in0[:, 2560 + 128 * p:2560 + 128 * (p + 1)],
                                       w_in0[:, 3072 + 128 * p:3072 + 128 * (p + 1)]], axis=1) for p in range(4)])
    ident = np.eye(128, dtype=f)
    half = 8
    inv = (500000.0 ** (-(np.arange(half, dtype=np.float32) * 2.0) / 16.0)).astype(f)
    invf = np.ascontiguousarray(np.broadcast_to(np.tile(inv, 4)[None, :], (128, 32))).astype(f)
    pk = np.arange(128)
    trib = np.where(pk[:, None] <= pk[None, :], 0.0, NEGB).astype(f)
    onehot = (np.arange(16)[:, None] == (np.arange(S)[None, :] // 256)).astype(f)
    maps = []
    for b in range(B):
        vecs = np.concatenate([np.asarray(b_ada[0], f).reshape(24, 128), np.asarray(g_norm[0], f).reshape(8, 128),
                               np.asarray(c[b], f).reshape(8, 128), np.asarray(b_dw[0], f).reshape(4, 128),
                               np.asarray(g_ln_conv[0], f).reshape(4, 128), np.asarray(b_ln_conv[0], f).reshape(4, 128),
                               np.asarray(b_pw[0], f).reshape(4, 128)], axis=0)
        pos = np.ascontiguousarray(np.asarray(positions[b], np.int32).reshape(NS, 128).T)
        maps.append({
            "x": np.ascontiguousarray(np.asarray(x[b], f)), "vecs": np.ascontiguousarray(vecs),
            "b_ada": np.ascontiguousarray(np.asarray(b_ada[0], f)), "pos": pos,
            "w_ada": np.ascontiguousarray(np.asarray(w_ada[0], f)), "w_c": w_c, "w_pair": np.ascontiguousarray(w_pair),
            "w_dw": np.ascontiguousarray(np.asarray(w_dw[0], f)), "w_pw": np.ascontiguousarray(np.asarray(w_pw[0], f)),
            "w_out": np.ascontiguousarray(np.asarray(w_out[0], f)), "g_final": np.ascontiguousarray(np.asarray(g_final, f)),
            "ident": ident, "invf": invf, "trib": trib, "onehot": onehot,
        })
    return maps


_NC_CACHE = {}


def kernel(x, c, positions, w_ada, b_ada, g_norm, w_in, w_dw, b_dw, g_ln_conv, b_ln_conv, w_pw, b_pw, w_out, g_final):
    x = np.asarray(x)
    B, S, _ = x.shape
    maps = host_inputs(S, x, np.asarray(c), np.asarray(positions), np.asarray(w_ada), np.asarray(b_ada),
                       np.asarray(g_norm), np.asarray(w_in), np.asarray(w_dw), np.asarray(b_dw), np.asarray(g_ln_conv),
                       np.asarray(b_ln_conv), np.asarray(w_pw), np.asarray(b_pw), np.asarray(w_out), np.asarray(g_final))
    if S not in _NC_CACHE:
        _NC_CACHE[S] = build(S)
    nc = _NC_CACHE[S]
    res = run_bass_kernel_spmd(nc, maps, core_ids=list(range(B)))
    return np.stack([np.asarray(r["out"], np.float32) for r in res.results], axis=0)
```
